# Optimizing a Trainium2 kernel written in Bass

```python
import math
import jax, jax.numpy as jnp
from jax import lax
import numpy as np

D_MODEL = 1024
BATCH = 8
SEQ = 2048
DEPTH = 2
DEC_BATCH = 128
DEC_SEQ = 1
PAST_LEN = 16384
PAGE_SIZE = 128

N_EVEN = (DEPTH + 1) // 2
N_ODD = DEPTH // 2
MH = 4
MLSTM_DIM = D_MODEL
MDK = MLSTM_DIM // MH
MDV = MLSTM_DIM // MH
GATE_CAP = 15.0
SSM_DIM = D_MODEL
SP = 64
SH = SSM_DIM // SP
SG = 2
SN = 128
CONV_W = 4
CONV_DIM = SSM_DIM + 2 * SG * SN
IN_DIM = 4 * MLSTM_DIM + 2 * MH + SSM_DIM + CONV_DIM + SH
MIX_DIM = MLSTM_DIM + SSM_DIM
POOL_WINDOWS = (2, 4, 8, 16)
POOL_GROUPS = 4
POOL_C = D_MODEL // POOL_GROUPS
POOL_BUF = max(POOL_WINDOWS) - 1
FF = -(-8 * D_MODEL // (3 * 256)) * 256
CHUNK = 64
EPS = 1e-6

kernel_name = 'mlstm_ssd_pool_hybrid_step'

F32 = jnp.float32


def _rmsnorm(x, g):
    xf = x.astype(F32)
    y = xf * lax.rsqrt(jnp.mean(xf * xf, axis=-1, keepdims=True) + EPS)
    return (y * g.astype(F32)).astype(x.dtype)


def _softcap(x):
    return GATE_CAP * jnp.tanh(x / GATE_CAP)


def _chunk_len(t):
    return CHUNK if t % CHUNK == 0 else t


def _mlstm(q, k, v, logi, logf, c0, n0, m0):
    B, T, H, DK = q.shape
    DV = v.shape[-1]
    L = _chunk_len(T)
    NC = T // L

    def blk(a):
        a = a.reshape((B, NC, L, H) + a.shape[3:])
        return a.transpose((1, 0, 3, 2) + tuple(range(4, a.ndim)))

    mask = jnp.tril(jnp.ones((L, L), bool))

    def step(carry, inp):
        c, n, m = carry
        qc, kc, vc, li, lf = inp
        b = jnp.cumsum(lf, axis=-1)
        dlog = jnp.where(mask, b[..., :, None] - b[..., None, :] + li[..., None, :], -jnp.inf)
        inter = b + m[..., None]
        mt = jnp.maximum(inter, jnp.max(dlog, axis=-1))
        s = jnp.einsum('bhtd,bhsd->bhts', qc, kc) * jnp.exp(dlog - mt[..., None])
        iw = jnp.exp(inter - mt)
        num = jnp.einsum('bhts,bhsv->bhtv', s, vc) + iw[..., None] * jnp.einsum('bhtd,bhdv->bhtv', qc, c)
        den = jnp.sum(s, axis=-1) + iw * jnp.einsum('bhtd,bhd->bht', qc, n)
        h = num / jnp.maximum(jnp.abs(den), jnp.exp(-mt))[..., None]
        bl = b[..., -1]
        g = bl[..., None] - b + li
        m_new = jnp.maximum(bl + m, jnp.max(g, axis=-1))
        w = jnp.exp(g - m_new[..., None])
        dec = jnp.exp(bl + m - m_new)
        c_new = dec[..., None, None] * c + jnp.einsum('bhs,bhsd,bhsv->bhdv', w, kc, vc)
        n_new = dec[..., None] * n + jnp.einsum('bhs,bhsd->bhd', w, kc)
        return (c_new, n_new, m_new), h

    xs = (blk(q.astype(F32)), blk(k.astype(F32)), blk(v.astype(F32)), blk(logi), blk(logf))
    (c, n, m), h = lax.scan(step, (c0.astype(F32), n0.astype(F32), m0.astype(F32)), xs)
    h = h.transpose(1, 0, 3, 2, 4).reshape(B, T, H, DV)
    return h, c, n, m


def _ssd(x, dt, a, bm, cm, h0):
    B, T, H, P = x.shape
    G, N = bm.shape[2], bm.shape[3]
    E = H // G
    L = _chunk_len(T)
    NC = T // L
    xb = x.astype(F32).reshape(B, NC, L, G, E, P).transpose(1, 0, 3, 4, 2, 5)
    dtb = dt.reshape(B, NC, L, G, E).transpose(1, 0, 3, 4, 2)
    ab = a.reshape(B, NC, L, G, E).transpose(1, 0, 3, 4, 2)
    bb = bm.astype(F32).reshape(B, NC, L, G, N).transpose(1, 0, 3, 2, 4)
    cb = cm.astype(F32).reshape(B, NC, L, G, N).transpose(1, 0, 3, 2, 4)
    mask = jnp.tril(jnp.ones((L, L), bool))

    def step(h, inp):
        xc, dtc, ac, bc, cc = inp
        cum = jnp.cumsum(ac, axis=-1)
        decay = jnp.exp(jnp.where(mask, cum[..., :, None] - cum[..., None, :], -jnp.inf))
        w = jnp.einsum('bgtn,bgsn->bgts', cc, bc)[:, :, None] * decay * dtc[..., None, :]
        y = jnp.einsum('bgets,bgesp->bgetp', w, xc) + jnp.exp(cum)[..., None] * jnp.einsum('bgtn,bgepn->bgetp', cc, h)
        wend = jnp.exp(cum[..., -1:] - cum) * dtc
        h_new = jnp.exp(cum[..., -1])[..., None, None] * h + jnp.einsum('bges,bgesp,bgsn->bgepn', wend, xc, bc)
        return h_new, y

    h, y = lax.scan(step, h0.astype(F32).reshape(B, G, E, P, N), (xb, dtb, ab, bb, cb))
    y = y.transpose(1, 0, 4, 2, 3, 5).reshape(B, T, H, P)
    return y, h.reshape(B, H, P, N)


def _causal_conv(xbc, buf, w, b):
    T = xbc.shape[1]
    ext = jnp.concatenate([buf.astype(xbc.dtype), xbc], axis=1)
    y = sum(ext[:, j:j + T] * w[j] for j in range(CONV_W)) + b
    return jax.nn.silu(y), ext[:, -(CONV_W - 1):]


def _mixer_ab(u, c0, n0, m0, h0, cbuf, w_in, b_ig, b_fg, g_mh, conv_w, conv_b, dt_bias, a_log, d_skip, g_ssm, w_out):
    B, T, _ = u.shape
    proj = jnp.einsum('btd,de->bte', u, w_in)
    sizes = (MLSTM_DIM, MLSTM_DIM, MLSTM_DIM, MLSTM_DIM, MH, MH, SSM_DIM, CONV_DIM, SH)
    q, k, v, o, ig, fg, z, xbc, dtr = jnp.split(proj, np.cumsum(sizes)[:-1].tolist(), axis=-1)
    logi = _softcap(ig.astype(F32) + b_ig.astype(F32))
    logf = jax.nn.log_sigmoid(_softcap(fg.astype(F32) + b_fg.astype(F32)))
    hq = q.reshape(B, T, MH, MDK) * (MDK ** -0.5)
    h, c, n, m = _mlstm(hq, k.reshape(B, T, MH, MDK), v.reshape(B, T, MH, MDV), logi, logf, c0, n0, m0)
    h = _rmsnorm(h.astype(u.dtype), g_mh.reshape(MH, MDV)).reshape(B, T, MLSTM_DIM) * jax.nn.sigmoid(o)
    xbc, cbuf_new = _causal_conv(xbc, cbuf, conv_w, conv_b)
    xs, bm, cm = jnp.split(xbc, [SSM_DIM, SSM_DIM + SG * SN], axis=-1)
    dt = jax.nn.softplus(dtr.astype(F32) + dt_bias.astype(F32))
    A = -jnp.exp(a_log.astype(F32))
    xh = xs.reshape(B, T, SH, SP)
    y, hs = _ssd(xh, dt, dt * A, bm.reshape(B, T, SG, SN), cm.reshape(B, T, SG, SN), h0)
    y = (y + d_skip.astype(F32)[:, None] * xh.astype(F32)).astype(u.dtype).reshape(B, T, SSM_DIM) * jax.nn.silu(z)
    y = _rmsnorm(y.reshape(B, T, SG, SSM_DIM // SG), g_ssm.reshape(SG, SSM_DIM // SG)).reshape(B, T, SSM_DIM)
    out = jnp.einsum('bte,ed->btd', jnp.concatenate([h, y], axis=-1), w_out)
    return out, c, n, m, hs, cbuf_new


def _mixer_pool(u, buf, start, w_pool, scale):
    B, T, D = u.shape
    ext = jnp.concatenate([buf.astype(u.dtype), u], axis=1)
    csum = lax.cumsum(ext.astype(F32), axis=1)
    csum = jnp.concatenate([jnp.zeros((B, 1, D), F32), csum], axis=1)
    hi = csum[:, POOL_BUF + 1:]
    pos = start + jnp.arange(T)
    parts = []
    for g, w in enumerate(POOL_WINDOWS):
        sl = slice(g * POOL_C, (g + 1) * POOL_C)
        lo = csum[:, POOL_BUF + 1 - w:POOL_BUF + 1 - w + T, sl]
        cnt = jnp.minimum(pos + 1, w).astype(F32)
        parts.append((hi[..., sl] - lo) / cnt[None, :, None])
    pooled = jnp.concatenate(parts, axis=-1)
    d = (pooled - u.astype(F32)).astype(u.dtype).reshape(B, T, POOL_GROUPS, POOL_C)
    out = jnp.einsum('btgc,gce->btge', d, w_pool).reshape(B, T, D) * scale
    return out, ext[:, -POOL_BUF:]


def _ffn(u, wg, wu, wd):
    return jnp.einsum('btf,fd->btd', jax.nn.silu(u @ wg) * (u @ wu), wd)


def _trunk(x, st_c, st_n, st_m, st_ssm, st_conv, st_pool, start, p):
    cs, ns, ms, hss, cvs, pls = [], [], [], [], [], []
    dt = x.dtype
    for l in range(DEPTH):
        u = _rmsnorm(x, p['g_mix_pre'][l])
        if l % 2 == 0:
            e = l // 2
            mix, c, n, m, hs, cb = _mixer_ab(u, st_c[e], st_n[e], st_m[e], st_ssm[e], st_conv[e],
                                             p['w_in_ab'][e], p['b_igate'][e], p['b_fgate'][e], p['g_mlstm'][e],
                                             p['conv_w'][e], p['conv_b'][e], p['dt_bias'][e], p['a_log'][e],
                                             p['d_skip'][e], p['g_ssm'][e], p['w_out_ab'][e])
            cs.append(c.astype(dt)); ns.append(n.astype(dt)); ms.append(m.astype(dt))
            hss.append(hs.astype(dt)); cvs.append(cb.astype(dt))
        else:
            o = l // 2
            mix, pb = _mixer_pool(u, st_pool[o], start, p['w_pool'][o], p['pool_scale'][o])
            pls.append(pb.astype(dt))
        x = x + _rmsnorm(mix, p['g_mix_post'][l])
        u = _rmsnorm(x, p['g_ffn_pre'][l])
        x = x + _rmsnorm(_ffn(u, p['w_gate'][l], p['w_up'][l], p['w_down'][l]), p['g_ffn_post'][l])
    return x, jnp.stack(cs), jnp.stack(ns), jnp.stack(ms), jnp.stack(hss), jnp.stack(cvs), jnp.stack(pls)


def setup_inputs(seed: int = 0) -> dict:
    key = jax.random.key(seed)
    ks = iter(jax.random.split(key, 32))

    def nrm(shape, s=1.0):
        return s * jax.random.normal(next(ks), shape, F32)

    x_prompt = nrm((BATCH, SEQ, D_MODEL))
    x_sample = nrm((DEC_BATCH, DEC_SEQ, D_MODEL))
    state_mlstm_c = nrm((N_EVEN, DEC_BATCH, MH, MDK, MDV), 0.1)
    state_mlstm_n = nrm((N_EVEN, DEC_BATCH, MH, MDK), 0.1)
    state_mlstm_m = nrm((N_EVEN, DEC_BATCH, MH))
    state_ssm = nrm((N_EVEN, DEC_BATCH, SH, SP, SN), 0.1)
    state_conv = nrm((N_EVEN, DEC_BATCH, CONV_W - 1, CONV_DIM))
    state_pool = nrm((N_ODD, DEC_BATCH, POOL_BUF, D_MODEL))
    g_mix_pre = 1.0 + nrm((DEPTH, D_MODEL), 0.05)
    g_mix_post = 1.0 + nrm((DEPTH, D_MODEL), 0.05)
    g_ffn_pre = 1.0 + nrm((DEPTH, D_MODEL), 0.05)
    g_ffn_post = 1.0 + nrm((DEPTH, D_MODEL), 0.05)
    w_in_ab = nrm((N_EVEN, D_MODEL, IN_DIM), D_MODEL ** -0.5)
    b_igate = nrm((N_EVEN, MH), 0.1)
    b_fgate = jnp.linspace(3.0, 6.0, MH, dtype=F32)[None, :] + nrm((N_EVEN, MH), 0.1)
    g_mlstm = 1.0 + nrm((N_EVEN, MLSTM_DIM), 0.05)
    conv_w = nrm((N_EVEN, CONV_W, CONV_DIM), CONV_W ** -0.5)
    conv_b = nrm((N_EVEN, CONV_DIM), 0.01)
    dt0 = jnp.exp(jax.random.uniform(next(ks), (N_EVEN, SH), F32, math.log(1e-3), math.log(1e-1)))
    dt_bias = dt0 + jnp.log(-jnp.expm1(-dt0))
    a_log = jnp.log(jax.random.uniform(next(ks), (N_EVEN, SH), F32, 1.0, 16.0))
    d_skip = 1.0 + nrm((N_EVEN, SH), 0.1)
    g_ssm = 1.0 + nrm((N_EVEN, SSM_DIM), 0.05)
    w_out_ab = nrm((N_EVEN, MIX_DIM, D_MODEL), MIX_DIM ** -0.5)
    w_pool = nrm((N_ODD, POOL_GROUPS, POOL_C, POOL_C), POOL_C ** -0.5)
    pool_scale = 1.0 + nrm((N_ODD, D_MODEL), 0.1)
    w_gate = nrm((DEPTH, D_MODEL, FF), D_MODEL ** -0.5)
    w_up = nrm((DEPTH, D_MODEL, FF), D_MODEL ** -0.5)
    w_down = nrm((DEPTH, FF, D_MODEL), FF ** -0.5)
    return {'x_prompt': x_prompt, 'x_sample': x_sample,
            'state_mlstm_c': state_mlstm_c, 'state_mlstm_n': state_mlstm_n, 'state_mlstm_m': state_mlstm_m,
            'state_ssm': state_ssm, 'state_conv': state_conv, 'state_pool': state_pool,
            'g_mix_pre': g_mix_pre, 'g_mix_post': g_mix_post, 'g_ffn_pre': g_ffn_pre, 'g_ffn_post': g_ffn_post,
            'w_in_ab': w_in_ab, 'b_igate': b_igate, 'b_fgate': b_fgate, 'g_mlstm': g_mlstm,
            'conv_w': conv_w, 'conv_b': conv_b, 'dt_bias': dt_bias, 'a_log': a_log, 'd_skip': d_skip,
            'g_ssm': g_ssm, 'w_out_ab': w_out_ab, 'w_pool': w_pool, 'pool_scale': pool_scale,
            'w_gate': w_gate, 'w_up': w_up, 'w_down': w_down}


def reference(x_prompt, x_sample, state_mlstm_c, state_mlstm_n, state_mlstm_m, state_ssm, state_conv, state_pool,
              g_mix_pre, g_mix_post, g_ffn_pre, g_ffn_post, w_in_ab, b_igate, b_fgate, g_mlstm,
              conv_w, conv_b, dt_bias, a_log, d_skip, g_ssm, w_out_ab, w_pool, pool_scale,
              w_gate, w_up, w_down):
    p = dict(g_mix_pre=g_mix_pre, g_mix_post=g_mix_post, g_ffn_pre=g_ffn_pre, g_ffn_post=g_ffn_post,
             w_in_ab=w_in_ab, b_igate=b_igate, b_fgate=b_fgate, g_mlstm=g_mlstm, conv_w=conv_w, conv_b=conv_b,
             dt_bias=dt_bias, a_log=a_log, d_skip=d_skip, g_ssm=g_ssm, w_out_ab=w_out_ab,
             w_pool=w_pool, pool_scale=pool_scale, w_gate=w_gate, w_up=w_up, w_down=w_down)
    dt = x_prompt.dtype
    z_c = jnp.zeros((N_EVEN, BATCH, MH, MDK, MDV), dt)
    z_n = jnp.zeros((N_EVEN, BATCH, MH, MDK), dt)
    z_m = jnp.zeros((N_EVEN, BATCH, MH), dt)
    z_ssm = jnp.zeros((N_EVEN, BATCH, SH, SP, SN), dt)
    z_conv = jnp.zeros((N_EVEN, BATCH, CONV_W - 1, CONV_DIM), dt)
    z_pool = jnp.zeros((N_ODD, BATCH, POOL_BUF, D_MODEL), dt)
    y_prompt, pc, pn, pm, pssm, pconv, ppool = _trunk(x_prompt, z_c, z_n, z_m, z_ssm, z_conv, z_pool, 0, p)
    y_sample, sc, sn, sm, sssm, sconv, spool = _trunk(x_sample, state_mlstm_c, state_mlstm_n, state_mlstm_m,
                                                     state_ssm, state_conv, state_pool, PAST_LEN, p)
    return (y_prompt, y_sample, pc, sc, pn, sn, pm, sm, pssm, sssm, pconv, sconv, ppool, spool)
```

```python
import numpy as np
import concourse.bass as bass
import concourse.mybir as mybir

F32 = mybir.dt.float32
BF16 = mybir.dt.bfloat16
ALU = mybir.AluOpType
AF = mybir.ActivationFunctionType
AX = mybir.AxisListType

_DTSIZE = {}


def dtsize(dt):
    s = str(dt)
    if '64' in s:
        return 8
    if '32' in s:
        return 4
    if '16' in s:
        return 2
    return 1


class Op:
    __slots__ = ('eng', 'fn', 'deps', 'idx', 'sig', 'dma', 'semslot', 'semval', 'waits', 'eidx', 'tag')


class Prog:
    ENGS = ('pe', 'act', 'dve', 'pool', 'sp')
    NDMA = 12

    def __init__(self, nc):
        self.nc = nc
        self.ops = []
        self.w = {}
        self.r = {}
        self.dma_count = {e: 0 for e in self.ENGS}
        self.dma_last = {}
        self.bank_last = {}
        self.tag = ''

    @staticmethod
    def box(ap):
        t = ap.tensor
        name = t.name
        esz = dtsize(ap.dtype)
        dims = list(ap.ap)
        sp = str(ap.space)
        if 'DRAM' in sp.upper():
            lo = ap.offset
            hi = lo
            for st, cnt in dims:
                hi += abs(st) * (cnt - 1)
            return name, (0, 1, lo * esz, (hi + 1) * esz)
        pstep, pcnt = dims[0]
        p0 = ap.start_partition
        if callable(p0):
            p0 = p0()
        off = ap.offset - p0 * pstep
        lo = off
        hi = off
        for st, cnt in dims[1:]:
            hi += abs(st) * (cnt - 1)
        return name, (p0, p0 + pcnt, lo * esz, (hi + 1) * esz)

    @staticmethod
    def _ov(a, b):
        return a[0] < b[1] and b[0] < a[1] and a[2] < b[3] and b[2] < a[3]

    @staticmethod
    def _cov(a, b):
        return a[0] <= b[0] and a[1] >= b[1] and a[2] <= b[2] and a[3] >= b[3]

    def add(self, eng, fn, outs=(), ins=(), dma=False):
        op = Op()
        op.eng = eng
        op.fn = fn
        op.idx = len(self.ops)
        op.sig = False
        op.dma = dma
        op.deps = set()
        op.waits = None
        op.tag = self.tag
        for ap in ins:
            name, bx = self.box(ap)
            for (b2, o2) in self.w.get(name, ()):
                if self._ov(bx, b2):
                    op.deps.add(o2)
        outboxes = []
        for ap in outs:
            name, bx = self.box(ap)
            outboxes.append((name, bx))
            for (b2, o2) in self.w.get(name, ()):
                if self._ov(bx, b2):
                    op.deps.add(o2)
            for (b2, o2) in self.r.get(name, ()):
                if self._ov(bx, b2):
                    op.deps.add(o2)
        for ap in ins:
            name, bx = self.box(ap)
            self.r.setdefault(name, []).append((bx, op))
        for name, bx in outboxes:
            self.w[name] = [(b2, o2) for (b2, o2) in self.w.get(name, []) if not self._cov(bx, b2)]
            self.w[name].append((bx, op))
            self.r[name] = [(b2, o2) for (b2, o2) in self.r.get(name, []) if not self._cov(bx, b2)]
        for ap in list(ins) + list(outs):
            if 'PSUM' in str(ap.space).upper():
                bl = self.bank_last.setdefault(ap.tensor.name, {})
                for e2, o2 in bl.items():
                    if e2 != eng:
                        op.deps.add(o2)
                bl[eng] = op
        op.deps.discard(op)
        if dma:
            n = self.dma_count[eng]
            self.dma_count[eng] = n + 1
            op.semslot = n % self.NDMA
            op.semval = 16 * (n // self.NDMA + 1)
            prev = self.dma_last.get((eng, op.semslot))
            if prev is not None:
                op.deps.add(prev)
            self.dma_last[(eng, op.semslot)] = op
        self.ops.append(op)
        return op

    def emit(self):
        nc = self.nc
        for op in self.ops:
            nd = set()
            for d in op.deps:
                if d.eng == 'pe' and op.eng == 'pe' and not d.dma and not op.dma:
                    continue
                nd.add(d)
            best = {}
            keep = set()
            for d in nd:
                if d.dma:
                    keep.add(d)
                else:
                    if d.eng not in best or d.idx > best[d.eng].idx:
                        best[d.eng] = d
            keep.update(best.values())
            op.deps = keep
            for d in keep:
                d.sig = True
        cnt = {e: 0 for e in self.ENGS}
        for op in self.ops:
            if op.dma:
                continue
            if op.sig:
                cnt[op.eng] += 1
                op.semval = cnt[op.eng]
        SEG = 1500
        esem = {e: [nc.alloc_semaphore('s_%s_%d' % (e, k)) for k in range(cnt[e] // SEG + 1)] for e in self.ENGS}
        dsem = {}
        for e in self.ENGS:
            if self.dma_count[e]:
                for i in range(min(self.NDMA, self.dma_count[e])):
                    dsem[(e, i)] = nc.alloc_semaphore('d_%s_%d' % (e, i))
        seen = {e: {} for e in self.ENGS}
        nwaits = 0
        for op in self.ops:
            need = {}
            for d in op.deps:
                key = ('d', d.eng, d.semslot) if d.dma else ('e', d.eng)
                if d.semval > need.get(key, 0):
                    need[key] = d.semval
            ws = []
            for key, v in need.items():
                if seen[op.eng].get(key, 0) >= v:
                    continue
                seen[op.eng][key] = v
                if key[0] == 'd':
                    ws.append((dsem[(key[1], key[2])], v))
                else:
                    ws.append((esem[key[1]][(v - 1) // SEG], (v - 1) % SEG + 1))
            op.waits = ws
            nwaits += len(ws)
        per = {e: [o for o in self.ops if o.eng == e] for e in self.ENGS}
        self.stats = {e: len(per[e]) for e in self.ENGS}
        self.stats['waits'] = nwaits
        self.stats['sig'] = dict(cnt)

        def run(eng, name):
            for op in per[name]:
                for sem, v in op.waits:
                    eng.wait_ge(sem, v)
                ins = op.fn(eng)
                if op.dma:
                    ins.then_inc(dsem[(name, op.semslot)], 16)
                elif op.sig:
                    ins.then_inc(esem[name][(op.semval - 1) // SEG], 1)
            for i in range(min(self.NDMA, self.dma_count[name])):
                last = self.dma_last[(name, i)]
                eng.wait_ge(dsem[(name, i)], last.semval)

        with nc.Block() as block:
            @block.tensor
            def _(e):
                run(e, 'pe')

            @block.scalar
            def _(e):
                run(e, 'act')

            @block.vector
            def _(e):
                run(e, 'dve')

            @block.gpsimd
            def _(e):
                run(e, 'pool')

            @block.sync
            def _(e):
                run(e, 'sp')

from concourse.bass_utils import run_bass_kernel_spmd

D = 1024
KC = 8
T = 2048
NBLK = 4
TPB = 4
SB = 16
FF = 2816
FC = 22
IN_DIM = 6680
EPS = 1e-6
OQ, OK_, OV, OO, OIG, OFG, OZ, OXBC, ODT = 0, 1024, 2048, 3072, 4096, 4100, 4104, 5128, 6664


KSTAGE = 99


class _Stop(Exception):
    pass


def chk(n):
    if n >= KSTAGE:
        raise _Stop()


class Bld:
    def __init__(s, nc):
        s.nc = nc
        s.P = Prog(nc)
        s.rr = 0

    def mm(s, out, lhsT, rhs, start=True, stop=True):
        s.P.add('pe', lambda e: e.matmul(out, lhsT, rhs, start=start, stop=stop), outs=[out], ins=[lhsT, rhs])

    def tr(s, out, in_, ident):
        s.P.add('pe', lambda e: e.transpose(out, in_, ident), outs=[out], ins=[in_, ident])

    def act(s, out, in_, func, scale=1.0, bias=None, accum=None):
        ins = [in_]
        outs = [out]
        kw = {}
        if bias is not None:
            kw['bias'] = bias
            if not isinstance(bias, (int, float)):
                ins.append(bias)
        if not isinstance(scale, (int, float)):
            ins.append(scale)
        if accum is not None:
            kw['accum_out'] = accum
            outs.append(accum)
        s.P.add('act', lambda e: e.activation(out, in_, func, scale=scale, **kw), outs=outs, ins=ins)

    def ts(s, out, in0, s1, s2=None, op0=ALU.mult, op1=None, eng='dve', accum=None):
        ins = [in0]
        outs = [out]
        for x in (s1, s2):
            if x is not None and not isinstance(x, (int, float)):
                ins.append(x)
        kw = {}
        if op1 is not None:
            kw['op1'] = op1
        if accum is not None:
            kw['accum_out'] = accum
            outs.append(accum)
        s.P.add(eng, lambda e: e.tensor_scalar(out, in0, s1, s2, op0=op0, **kw), outs=outs, ins=ins)

    def stt(s, out, in0, sc, in1, op0, op1):
        ins = [in0, in1]
        if not isinstance(sc, (int, float)):
            ins.append(sc)
        s.P.add('dve', lambda e: e.scalar_tensor_tensor(out, in0, sc, in1, op0=op0, op1=op1), outs=[out], ins=ins)

    def tt(s, out, in0, in1, op, eng='dve'):
        s.P.add(eng, lambda e: e.tensor_tensor(out, in0, in1, op=op), outs=[out], ins=[in0, in1])

    def cp(s, out, in_, eng='dve'):
        if eng == 'act':
            s.P.add('act', lambda e: e.copy(out, in_), outs=[out], ins=[in_])
        else:
            s.P.add(eng, lambda e: e.tensor_copy(out, in_), outs=[out], ins=[in_])

    def cpa(s, out, in_):
        s.rr += 1
        s.cp(out, in_, eng='act' if s.rr % 2 else 'dve')

    def memset(s, ap, v, eng='dve'):
        s.P.add(eng, lambda e: e.memset(ap, v), outs=[ap])

    def recip(s, out, in_):
        s.P.add('dve', lambda e: e.reciprocal(out, in_), outs=[out], ins=[in_])

    def scan(s, out, d0, d1, init, op0, op1):
        ins = [d0, d1]
        if not isinstance(init, (int, float)):
            ins.append(init)
        s.P.add('dve', lambda e: e.tensor_tensor_scan(out, d0, d1, init, op0=op0, op1=op1), outs=[out], ins=ins)

    def asel(s, ap, pattern, cm, base, cmp, fill=0.0):
        s.P.add('pool', lambda e: e.affine_select(ap, ap, pattern=pattern, compare_op=cmp, fill=fill, base=base, channel_multiplier=cm), outs=[ap], ins=[ap])

    def dma(s, out, in_, q='sp'):
        s.P.add(q, lambda e: e.dma_start(out=out, in_=in_), outs=[out], ins=[in_], dma=True)

    def dma_nc(s, out, in_, q='sp'):
        s.P.add(q, lambda e: e.dma_start(out=out, in_=in_, allow_slow_non_contiguous=True), outs=[out], ins=[in_], dma=True)

    def sb(s, name, shape, dt=F32):
        return s.nc.alloc_sbuf_tensor(name, list(shape), dt)


def build():
    nc = bass.Bass("TRN2", target_bir_lowering=False)
    b = Bld(nc)

    def din(name, shape):
        return nc.dram_tensor(name, list(shape), F32, kind="ExternalInput").ap()

    def dout(name, shape):
        return nc.dram_tensor(name, list(shape), F32, kind="ExternalOutput").ap()

    xp = din("x_prompt", [T, D])
    xs = din("x_sample", [SB, D])
    st_c = din("state_mlstm_c", [SB, 4, 256, 256])
    st_n = din("state_mlstm_n", [SB, 1024])
    st_m = din("state_mlstm_m", [SB, 4])
    st_ssm = din("state_ssm", [SB, 16, 64, 128])
    st_conv = din("state_conv", [SB, 3, 1536])
    st_pool = din("state_pool", [SB, 15, D])
    g_mix_pre = din("g_mix_pre", [2, D])
    g_mix_post = din("g_mix_post", [2, D])
    g_ffn_pre = din("g_ffn_pre", [2, D])
    g_ffn_post = din("g_ffn_post", [2, D])
    w_in = din("w_in_ab", [D, IN_DIM])
    b_ig = din("b_igate", [4, 1])
    b_fg = din("b_fgate", [4, 1])
    g_mlstm = din("g_mlstm", [1, D])
    conv_w = din("conv_w", [4, 1536])
    conv_b = din("conv_b", [1, 1536])
    dt_bias = din("dt_bias", [16, 1])
    a_log = din("a_log", [16, 1])
    d_skip = din("d_skip", [1, 16])
    g_ssm = din("g_ssm", [1, D])
    w_out = din("w_out_ab", [2048, D])
    w_pool = din("w_pool", [4, 256, 256])
    pool_scale = din("pool_scale", [1, D])
    w_gate = din("w_gate", [2, D, FF])
    w_up = din("w_up", [2, D, FF])
    w_down = din("w_down", [2, FF, D])

    y_p = dout("y_prompt", [T, D])
    y_s = dout("y_sample", [SB, D])
    c_p = dout("c_prompt", [4, 256, 256])
    c_s = dout("c_sample", [SB, 4, 256, 256])
    n_p = dout("n_prompt", [8, 128])
    n_s = dout("n_sample", [SB, 1024])
    m_p = dout("m_prompt", [4, 1])
    m_s = dout("m_sample", [SB, 4])
    ssm_p = dout("ssm_prompt", [16 * 64, 128])
    ssm_s = dout("ssm_sample", [SB, 16 * 64, 128])
    conv_p = dout("conv_prompt", [3, 1536])
    conv_s = dout("conv_sample", [SB, 3, 1536])
    pool_p = dout("pool_prompt", [15, D])
    pool_s = dout("pool_sample", [SB, 15, D])

    uT = b.sb("uT", [128, KC, 528], BF16)
    xres = b.sb("xres", [128, 5, D])
    mixT = b.sb("mixT", [128, 16, 528], BF16)
    Fb = b.sb("Fb", [128, 16384], BF16)
    NSLOT = 10
    wst = [b.sb("wst%d" % i, [128, KC, 256], BF16) for i in range(NSLOT)]
    kq4 = b.sb("kq4", [128, 4, 256], BF16)
    GT2 = b.sb("GT2", [128, 2, 128])
    decb2 = b.sb("decb2", [128, 2, 16])
    xw2 = b.sb("xw2", [128, 1024], BF16)
    btok2 = b.sb("btok2", [128, 2, 128], BF16)
    yap = [b.sb("yap0", [128, D]), b.sb("yap1", [128, D])]
    eAm = b.sb("eAm", [128, 4, 128])
    negm4b = b.sb("negm4b", [128, 4, 128], BF16)
    gbc = [b.sb("gbc%d" % i, [128, D]) for i in range(2)]
    xin = b.sb("xin", [128, D])
    ub = b.sb("ub", [128, D], BF16)
    tmpa = b.sb("tmpa", [128, D])
    tmpb = b.sb("tmpb", [128, D])
    Cst = b.sb("Cst", [128, 4, 2, 256])
    Cbf = b.sb("Cbf", [128, 4, 2, 256], BF16)
    nst = b.sb("nst", [128, 8])
    nbf = b.sb("nbf", [128, 8], BF16)
    hT = b.sb("hT", [128, 16, 64])
    hTbf = b.sb("hTbf", [128, 16, 64], BF16)
    ident32 = b.sb("ident32", [128, 128])
    identb = b.sb("identb", [128, 128], BF16)
    mask32 = b.sb("mask32", [128, 128])
    sel127 = b.sb("sel127", [128, 128])
    ones32 = b.sb("ones32", [128, 512])
    negm4 = b.sb("negm4", [128, 4, 128])
    negones = b.sb("negones", [16, 128])
    onesb = b.sb("onesb", [128, 1], BF16)
    rows = [b.sb("rows%d" % i, [128, 512]) for i in range(6)]
    sm = b.sb("sm", [128, 96])
    GT = b.sb("GT", [128, 128])
    kp = b.sb("kp", [128, 256], BF16)
    kp2 = b.sb("kp2", [128, 256], BF16)
    spb = b.sb("spb", [128, 4, 128], BF16)
    hf = b.sb("hf", [128, D], BF16)
    decb = b.sb("decb", [128, 16])
    carry = b.sb("carry", [128, 8])
    convtail = b.sb("convtail", [128, 12, 3])
    pooltail = b.sb("pooltail", [128, KC, 15], BF16)
    invcnt = b.sb("invcnt", [128, 4, 16])
    gpar = b.sb("gpar", [128, 64])
    gload = b.sb("gload", [64, 128])
    smallp = b.sb("smallp", [16, 8])
    dskb = b.sb("dskb", [128, 16])
    wpl = b.sb("wpl", [128, 4, 2, 256], BF16)
    u1f = b.sb("u1f", [128, D])

    pb = [nc.alloc_psum_tensor("pb%d" % i, [128, 512], F32) for i in range(8)]

    def pbf(i):
        return pb[i][:, :].bitcast(BF16)

    qT = Fb[:, 0:4096].rearrange("p (c n) -> p c n", c=8)
    kT = Fb[:, 4096:8192].rearrange("p (c n) -> p c n", c=8)
    vtok = Fb[:, 8192:12288].rearrange("p (i n) -> p i n", i=4)
    gsig = Fb[:, 12288:16384].rearrange("p (i n) -> p i n", i=4)
    zs = Fb[:, 0:4096].rearrange("p (i n) -> p i n", i=4)
    xbcT = Fb[:, 4096:10240].rearrange("p (c n) -> p c n", c=12)
    WT = Fb[:, 10240:12288].rearrange("p (h n) -> p h n", h=16)
    xtok = Fb[:, 12288:13312]
    xdt = Fb[:, 13312:14336]
    xw = Fb[:, 14336:15360]
    btok = Fb[:, 15360:15616].rearrange("p (g n) -> p g n", g=2)
    E4 = Fb[:, 15616:16128]
    hid = Fb[:, 0:11616].rearrange("p (c n) -> p c n", c=FC)
    sproj = Fb[:, 0:13360].bitcast(F32)

    b.memset(ident32[:, :], 1.0, 'pool')
    b.asel(ident32[:, :], [[-1, 128]], 1, 0, ALU.is_equal)
    b.cp(identb[:, :], ident32[:, :])
    b.memset(mask32[:, :], 1.0, 'pool')
    b.asel(mask32[:, :], [[1, 128]], -1, 0, ALU.is_ge)
    b.memset(sel127[:, :], 1.0, 'pool')
    b.asel(sel127[:, :], [[0, 128]], 1, -127, ALU.is_equal)
    b.memset(ones32[:, :], 1.0)
    b.memset(negm4[:, :, :], 0.0, 'pool')
    b.asel(negm4[:, :, :], [[0, 4], [1, 128]], -1, 0, ALU.is_ge, fill=-30000.0)
    b.memset(negones[:, :], -1.0)
    b.cp(negm4b[:, :, :], negm4[:, :, :])
    b.memset(onesb[:, :], 1.0)
    b.memset(Cst[:, :, :, :], 0.0)
    b.memset(Cbf[:, :, :, :], 0.0)
    b.memset(nst[:, :], 0.0)
    b.memset(nbf[:, :], 0.0)
    b.memset(hT[:, :, :], 0.0)
    b.memset(hTbf[:, :, :], 0.0)
    b.memset(carry[:, :], 0.0)
    b.memset(convtail[:, :, :], 0.0)
    b.memset(pooltail[:, :, :], 0.0)
    b.memset(GT[:, :], 0.0)
    for r in rows:
        b.memset(r[:, :], 0.0)
    for wi, w in enumerate((2, 4, 8, 16)):
        b.memset(invcnt[:, wi, :], 1.0 / w)
        for t_ in range(w - 1):
            b.memset(invcnt[:, wi, t_:t_ + 1], 1.0 / (t_ + 1))
    b.memset(gload[:, :], 0.0)
    b.dma(gload[0:48, :], conv_w.rearrange("j (c p) -> (j c) p", p=128))
    b.dma(gload[48:60, :], conv_b.rearrange("o (c p) -> (o c) p", p=128))
    b.tr(pb[7][:, 0:64], gload[0:64, :], ident32[0:64, 0:64])
    b.cp(gpar[:, :], pb[7][:, 0:64])
    b.dma_nc(smallp[0:4, 0:1], b_ig)
    b.dma_nc(smallp[0:4, 1:2], b_fg)
    b.dma_nc(smallp[0:16, 2:3], dt_bias)
    b.dma_nc(smallp[0:16, 3:4], a_log)
    b.ts(smallp[0:4, 0:2], smallp[0:4, 0:2], 1.0 / 15.0)
    b.act(smallp[0:16, 4:5], smallp[0:16, 3:4], AF.Exp)
    b.dma(dskb[:, :], d_skip.broadcast_to([128, 16]))
    for g in range(4):
        b.dma(wpl[:, g, :, :], w_pool[g].rearrange("(kc p) e -> p kc e", p=128), q='pool')
    b.dma(gbc[0][:, :], pool_scale[0:1, :].broadcast_to([128, D]))
    for g in range(4):
        b.tt(wpl[:, g, :, :], wpl[:, g, :, :], gbc[0][:, g * 256:(g + 1) * 256].unsqueeze(1).broadcast_to([128, 2, 256]), ALU.mult)

    gslot = [0]

    def load_g(src_row):
        t_ = gbc[gslot[0] % 2]
        gslot[0] += 1
        b.dma(t_[:, :], src_row.broadcast_to([128, D]))
        return t_

    wslot = [0]

    NPC = 104
    wscr = nc.dram_tensor("wscr", [NPC, 128, 2048], BF16, kind="Internal").ap()
    wcache = {}

    def fetch_piece(key, slot_view, src, scr_view_fn):
        if key in wcache:
            b.dma(slot_view, scr_view_fn(wcache[key]), q='sp')
        else:
            idx = len(wcache)
            assert idx < NPC
            wcache[key] = idx
            b.dma(slot_view, src, q='pool')
            b.dma(scr_view_fn(idx), slot_view, q='sp')

    def load_w(src):
        t_ = wst[wslot[0] % NSLOT]
        wslot[0] += 1
        n = src.shape[2]
        key = (src.tensor.name, src.offset, n)
        fetch_piece(key, t_[:, :, 0:n], src, lambda idx: wscr[idx].rearrange("p (k c) -> p k c", k=KC)[:, :, 0:n])
        return t_

    def rstd_from_ss(dst, ss, n, rows_):
        b.act(dst, ss, AF.Ln, scale=1.0 / n, bias=EPS)
        b.act(dst, dst, AF.Exp, scale=-0.5)

    def norm_to_T(src, R, g_t, dstT, col0, keep32=None):
        b.act(tmpa[0:R, :], src, AF.Square, accum=sm[0:R, 0:1])
        rstd_from_ss(sm[0:R, 1:2], sm[0:R, 0:1], D, R)
        if keep32 is not None:
            b.stt(keep32[0:R, :], src, sm[0:R, 1:2], g_t[0:R, :], ALU.mult, ALU.mult)
            b.cp(ub[0:R, :], keep32[0:R, :])
        else:
            b.stt(ub[0:R, :], src, sm[0:R, 1:2], g_t[0:R, :], ALU.mult, ALU.mult)
        pt = pbf(2)
        for c in range(KC):
            b.tr(pt[:, c * 128:c * 128 + R], ub[0:R, c * 128:(c + 1) * 128], identb[0:R, 0:R])
        if R == 128:
            b.cpa(dstT[:, :, col0:col0 + R], pt[:, 0:1024].rearrange("p (c n) -> p c n", c=KC))
        else:
            b.cpa(dstT[:, :, col0:col0 + R], pt[:, 0:1024].rearrange("p (c n) -> p c n", c=KC)[:, :, 0:R])

    def tok_to_T(src_bf, R, dstT, c0, nch, col0, bank=2):
        pt = pbf(bank)
        for c in range(nch):
            b.tr(pt[:, c * 128:c * 128 + R], src_bf[0:R, c * 128:(c + 1) * 128], identb[0:R, 0:R])
        b.cpa(dstT[:, c0:c0 + nch, col0:col0 + R], pt[:, 0:nch * 128].rearrange("p (c n) -> p c n", c=nch)[:, :, 0:R])

    pj = [0]

    def pjbank():
        pj[0] += 1
        return pb[pj[0] % 4]

    def post_norm_residual(ps2, R, g_t, xsrc, xdst):
        b.act(tmpa[0:R, 0:512], ps2[0][0:R, :], AF.Square, accum=sm[0:R, 2:3])
        b.act(tmpa[0:R, 512:1024], ps2[1][0:R, :], AF.Square, accum=sm[0:R, 3:4])
        b.tt(sm[0:R, 4:5], sm[0:R, 2:3], sm[0:R, 3:4], ALU.add)
        rstd_from_ss(sm[0:R, 5:6], sm[0:R, 4:5], D, R)
        for j in range(2):
            b.stt(tmpb[0:R, j * 512:(j + 1) * 512], ps2[j][0:R, :], sm[0:R, 5:6], g_t[0:R, j * 512:(j + 1) * 512], ALU.mult, ALU.mult)
        b.tt(xdst, tmpb[0:R, :], xsrc, ALU.add)


    smn = b.sb("smn", [128, 48])

    def post_staged(grp, banks, g_t, xadd):
        for n, (i, R) in enumerate(grp):
            b.act(tmpa[0:R, 0:512], banks[n][0][0:R, :], AF.Square, accum=smn[0:R, 2 * n:2 * n + 1])
            b.act(tmpa[0:R, 512:1024], banks[n][1][0:R, :], AF.Square, accum=smn[0:R, 2 * n + 1:2 * n + 2])
        ng = len(grp)
        b.P.add('dve', lambda e: e.tensor_reduce(smn[:, 16:16 + ng], smn[:, 0:2 * ng].rearrange("p (n t) -> p n t", t=2), axis=AX.X, op=ALU.add),
                outs=[smn[:, 16:16 + ng]], ins=[smn[:, 0:2 * ng]])
        rstd_from_ss(smn[:, 24:24 + ng], smn[:, 16:16 + ng], D, 128)
        for n, (i, R) in enumerate(grp):
            src = xadd(i, R)
            if src is not None:
                for j in range(2):
                    b.stt(xres[0:R, i, j * 512:(j + 1) * 512], banks[n][j][0:R, :], smn[0:R, 24 + n:25 + n], g_t[0:R, j * 512:(j + 1) * 512], ALU.mult, ALU.mult)
                xo = xres[0:R, i, :]
                b.P.add('pool', lambda e, xo=xo, src=src: e.dma_start(out=xo, in_=src, accum_op=ALU.add), outs=[xo], ins=[xo, src], dma=True)
            else:
                tmp = tmpb if n % 2 == 0 else tmpa
                for j in range(2):
                    b.stt(tmp[0:R, j * 512:(j + 1) * 512], banks[n][j][0:R, :], smn[0:R, 24 + n:25 + n], g_t[0:R, j * 512:(j + 1) * 512], ALU.mult, ALU.mult)
                b.tt(xres[0:R, i, :], tmp[0:R, :], xres[0:R, i, :], ALU.add, eng='pool')

    def prenorm_staged(tl, g_t, dstT, col0_fn, ubs, keep32_fn=None):
        nt = len(tl)
        for n, (i, R) in enumerate(tl):
            b.act(tmpa[0:R, :], xres[0:R, i, :], AF.Square, accum=smn[0:R, 32 + n:33 + n])
        rstd_from_ss(smn[:, 40:40 + nt], smn[:, 32:32 + nt], D, 128)
        for n, (i, R) in enumerate(tl):
            k32 = keep32_fn(i, R) if keep32_fn else None
            if k32 is not None:
                b.stt(k32[0:R, :], xres[0:R, i, :], smn[0:R, 40 + n:41 + n], g_t[0:R, :], ALU.mult, ALU.mult)
                b.cp(ubs[n][0:R, :], k32[0:R, :], eng='act')
            else:
                b.stt(ubs[n][0:R, :], xres[0:R, i, :], smn[0:R, 40 + n:41 + n], g_t[0:R, :], ALU.mult, ALU.mult)
        for n, (i, R) in enumerate(tl):
            pt = pbf(2 + n % 2)
            for c in range(KC):
                b.tr(pt[:, c * 128:c * 128 + R], ubs[n][0:R, c * 128:(c + 1) * 128], identb[0:R, 0:R])
            b.cpa(dstT[:, :, col0_fn(i):col0_fn(i) + R], pt[:, 0:1024].rearrange("p (c n) -> p c n", c=KC)[:, :, 0:R])

    curblk = [0]

    def wview(w2d, c0, n):
        return w2d[:, c0:c0 + n].rearrange("(kc p) n -> p kc n", p=128)

    def load_w2(w2d, k2):
        t_ = wst[wslot[0] % NSLOT]
        wslot[0] += 1
        v = t_[:, :, :].rearrange("p a n -> p (a n)").rearrange("p (kc n) -> p kc n", kc=2)
        src = w2d[k2 * 128:(k2 + 2) * 128, :].rearrange("(kc p) n -> p kc n", p=128)
        key = (src.tensor.name, src.offset, 'w2')
        fetch_piece(key, v, src, lambda idx: wscr[idx].rearrange("p (kc n) -> p kc n", kc=2))
        return v

    def stream_proj(actT, nk, w2d, tiles, post):
        groups = [tiles[0:4]] + ([tiles[4:]] if len(tiles) > 4 else [])
        for grp in groups:
            for k2 in range(0, nk, 2):
                wv = load_w2(w2d, k2)
                for kk in range(2):
                    k = k2 + kk
                    for gi, (i, R) in enumerate(grp):
                        for j in range(2):
                            b.mm(pb[2 * gi + j][0:R, :], actT[:, k, i * 128:i * 128 + R], wv[:, kk, j * 512:(j + 1) * 512], start=(k == 0), stop=(k == nk - 1))
            post(grp, [(pb[2 * gi], pb[2 * gi + 1]) for gi in range(len(grp))])

    def ffn(layer, tiles, last):
        b.P.tag = 'b%d.ffn%d.gateup' % (curblk[0], layer)
        g_t = load_g(g_ffn_pre[layer:layer + 1, :])
        ubs_f = [Fb[:, 11616 + n * 1024:11616 + (n + 1) * 1024] for n in range(4)] + [ub]
        prenorm_staged(tiles, g_t, uT, lambda i: i * 128, ubs_f)
        for j in range(11):
            wg = load_w(wview(w_gate[layer], j * 256, 256))
            wu = load_w(wview(w_up[layer], j * 256, 256))
            for m in range(2):
                nset = 3 if last else 4
                st_ = (2 * j + m) % nset
                pg, pu = ((pb[0], pb[1]), (pb[2], pb[3]), (pb[4], pb[5]), (pb[6], pb[7]))[st_]
                tmp = (tmpa[:, 0:512], tmpa[:, 512:1024], tmpb[:, 0:512], tmpb[:, 512:1024])[st_]
                for k in range(KC):
                    b.mm(pg[:, :], wg[:, k, m * 128:(m + 1) * 128], uT[:, k, 0:512], start=(k == 0), stop=(k == KC - 1))
                for k in range(KC):
                    b.mm(pu[:, :], wu[:, k, m * 128:(m + 1) * 128], uT[:, k, 0:512], start=(k == 0), stop=(k == KC - 1))
                b.act(tmp, pg[:, :], AF.Silu)
                b.tt(hid[:, 2 * j + m, 0:512], tmp, pu[:, :], ALU.mult)
                if last:
                    ps_ = pb[6] if (2 * j + m) % 2 == 0 else pb[7]
                    tmps = ub[:, 0:16] if (2 * j + m) % 2 == 0 else ub[:, 16:32]
                    for k in range(KC):
                        b.mm(ps_[:, 0:16], wg[:, k, m * 128:(m + 1) * 128], uT[:, k, 512:528], start=(k == 0), stop=(k == KC - 1))
                    for k in range(KC):
                        b.mm(ps_[:, 16:32], wu[:, k, m * 128:(m + 1) * 128], uT[:, k, 512:528], start=(k == 0), stop=(k == KC - 1))
                    b.act(tmps, ps_[:, 0:16], AF.Silu)
                    b.tt(hid[:, 2 * j + m, 512:528], tmps, ps_[:, 16:32], ALU.mult)
        b.P.tag = 'b%d.ffn%d.down' % (curblk[0], layer)
        g_p = load_g(g_ffn_post[layer:layer + 1, :])
        stream_proj(hid, FC, w_down[layer], tiles, lambda grp, banks: post_staged(grp, banks, g_p, lambda i, R: None))

    pext = Fb[:, 0:4352].rearrange("p (c n) -> p c n", c=8)
    dT = Fb[:, 4352:8576].rearrange("p (c n) -> p c n", c=8)
    hist = Fb[:, 8576:16256].bitcast(F32)

    def pool_mixer(blk, tiles, last):
        b.P.tag = 'b%d.pool' % curblk[0]
        g_t = load_g(g_mix_pre[1:2, :])
        b.cp(pext[:, :, 0:15], pooltail[:, :, :])
        ubs_p = [Fb[:, 8576 + n * 1024:8576 + (n + 1) * 1024] for n in range(4)] + [ub]

        def k32f(i, R):
            if R != 128:
                return xin
            if last and i == TPB - 1:
                return u1f
            return None
        prenorm_staged(tiles, g_t, pext, lambda i: 15 + i * 128, ubs_p, keep32_fn=k32f)
        b.cp(pooltail[:, :, :], pext[:, :, 512:527])
        if last:
            b.dma(pool_p[:, :], u1f[113:128, :], q='act')
            b.dma(pool_s[:, 0:14, :], st_pool[:, 1:15, :], q='act')
            b.dma(pool_s[:, 14, :], xin[0:SB, :], q='act')
        for c in range(8):
            wi = c // 2
            w = 2 << wi
            src, dst = tmpa, tmpb
            b.cp(src[:, 0:527], pext[:, c, 0:527], eng='act')
            for k in range(wi + 1):
                sh = 1 << k
                b.tt(dst[:, sh:527], src[:, sh:527], src[:, 0:527 - sh], ALU.add)
                src, dst = dst, src
            b.stt(dT[:, c, 0:512], src[:, 15:527], 1.0 / w, pext[:, c, 15:527], ALU.mult, ALU.subtract)
            if blk == 0:
                b.tt(dst[:, 0:16], src[:, 15:31], invcnt[:, wi, :], ALU.mult)
                b.tt(dT[:, c, 0:16], dst[:, 0:16], pext[:, c, 15:31], ALU.subtract)
        if last:
            dsm = tmpa
            for wi in range(4):
                w = 2 << wi
                gc = slice(wi * 256, (wi + 1) * 256)
                hv = hist[0:SB, 0:(w - 1) * 256].rearrange("p (j n) -> p j n", j=w - 1)
                b.dma(hv, st_pool[:, 15 - (w - 1):15, gc])
                b.P.add('dve', lambda e, hv=hv, gc=gc: e.tensor_reduce(dsm[0:SB, gc], hv.rearrange("p j n -> p n j"), axis=AX.X, op=ALU.add),
                        outs=[dsm[0:SB, gc]], ins=[hv])
                b.tt(dsm[0:SB, gc], dsm[0:SB, gc], xin[0:SB, gc], ALU.add)
                b.stt(dsm[0:SB, gc], dsm[0:SB, gc], 1.0 / w, xin[0:SB, gc], ALU.mult, ALU.subtract)
            b.cp(ub[0:SB, :], dsm[0:SB, :])
            tok_to_T(ub, SB, dT, 0, 8, 512, bank=2)
        gp_t = load_g(g_mix_post[1:2, :])
        banks = []
        for n, (i, R) in enumerate(tiles[0:4]):
            pp = (pb[2 * n], pb[2 * n + 1])
            banks.append(pp)
            for g in range(4):
                for kc in range(2):
                    b.mm(pp[g // 2][0:R, (g % 2) * 256:(g % 2 + 1) * 256], dT[:, g * 2 + kc, i * 128:i * 128 + R], wpl[:, g, kc, :], start=(kc == 0), stop=(kc == 1))
        post_staged(tiles[0:4], banks, gp_t, lambda i, R: None)
        if len(tiles) > 4:
            (i, R) = tiles[4]
            pp = (pb[0], pb[1])
            for g in range(4):
                for kc in range(2):
                    b.mm(pp[g // 2][0:R, (g % 2) * 256:(g % 2 + 1) * 256], dT[:, g * 2 + kc, i * 128:i * 128 + R], wpl[:, g, kc, :], start=(kc == 0), stop=(kc == 1))
            post_staged(tiles[4:], [pp], gp_t, lambda i, R: None)

    def sample_mixers():
        S = SB
        xf = xres[0:S, :, :].rearrange("p i n -> p (i n)")
        sp_ = sproj[0:S, :]
        onehot3 = rows[4][:, 0:256].rearrange("p (a c) -> p a c", a=16)
        qTs = rows[4][:, 256:384].rearrange("p (a c) -> p a c", a=8)
        qTm = rows[4][:, 384:512].rearrange("p (a c) -> p a c", a=8)
        decS = rows[5][:, 0:64]
        EAc = rows[5][:, 64:192]
        CTs = rows[5][:, 192:224].rearrange("p (a c) -> p a c", a=2)
        s2 = rows[3]
        cbuf = [rows[0], rows[1], rows[2]]
        for (pc0, pcn) in ((OQ, 1024), (OK_, 1024), (OV, 1024), (OO, 1024), (OIG, 8), (OZ, 1024), (OXBC, 1536), (ODT, 16)):
            for gg in range(0, pcn, 256):
                g0 = pc0 + gg
                n = min(256, pcn - gg)
                wt = load_w(w_in[:, g0:g0 + n].rearrange("(kc p) n -> p kc n", p=128))
                ps = pjbank()
                for k in range(KC):
                    b.mm(ps[0:S, 0:n], uT[:, k, 512:528], wt[:, k, 0:n], start=(k == 0), stop=(k == KC - 1))
                b.cpa(sp_[:, g0:g0 + n], ps[0:S, 0:n])
        b.memset(onehot3, 0.0)
        for j in range(16):
            b.memset(onehot3[:, j, j:j + 1], 1.0)
        q_ = sp_[:, OQ:OQ + 1024]
        k_ = sp_[:, OK_:OK_ + 1024]
        v_ = sp_[:, OV:OV + 1024]
        o_ = sp_[:, OO:OO + 1024]
        b.P.tag = 's.gates'
        b.dma(s2[0:S, 0:4], b_ig.rearrange("h o -> o h").broadcast_to([S, 4]))
        b.dma(s2[0:S, 4:8], b_fg.rearrange("h o -> o h").broadcast_to([S, 4]))
        b.dma(s2[0:S, 8:12], st_m[:, :])
        b.tt(s2[0:S, 12:20], sp_[:, OIG:OIG + 8], s2[0:S, 0:8], ALU.add)
        b.act(s2[0:S, 12:20], s2[0:S, 12:20], AF.Tanh, scale=1.0 / 15.0)
        li = s2[0:S, 20:24]
        b.ts(li, s2[0:S, 12:16], 15.0)
        b.act(s2[0:S, 24:28], s2[0:S, 16:20], AF.Exp, scale=-15.0)
        b.act(s2[0:S, 24:28], s2[0:S, 24:28], AF.Ln, bias=1.0)
        lfm = s2[0:S, 28:32]
        b.tt(lfm, s2[0:S, 8:12], s2[0:S, 24:28], ALU.subtract)
        mt = s2[0:S, 32:36]
        b.tt(mt, lfm, li, ALU.max)
        b.dma(m_s[:, :], mt, q='act')
        a1 = s2[0:S, 36:40]
        a2 = s2[0:S, 40:44]
        emt = s2[0:S, 44:48]
        b.tt(a1, li, mt, ALU.subtract)
        b.act(a1, a1, AF.Exp)
        b.tt(a2, lfm, mt, ALU.subtract)
        b.act(a2, a2, AF.Exp)
        b.act(emt, mt, AF.Exp, scale=-1.0)
        b.ts(q_, q_, 0.0625)
        nsv = xf[:, 0:1024]
        t1 = xf[:, 1024:2048]
        t2 = xf[:, 2048:3072]
        hh = xf[:, 3072:4096]
        b.dma(nsv, st_n[:, :])
        b.tt(t1, q_, k_, ALU.mult)
        qk = s2[0:S, 48:52]
        b.P.add('dve', lambda e: e.tensor_reduce(qk, t1.rearrange("p (h d) -> p h d", h=4), axis=AX.X, op=ALU.add), outs=[qk], ins=[t1])
        b.tt(t1, q_, nsv, ALU.mult)
        qn = s2[0:S, 52:56]
        b.P.add('dve', lambda e: e.tensor_reduce(qn, t1.rearrange("p (h d) -> p h d", h=4), axis=AX.X, op=ALU.add), outs=[qn], ins=[t1])
        sv = s2[0:S, 56:60]
        b.tt(sv, qk, a1, ALU.mult)
        den = s2[0:S, 60:64]
        b.tt(den, qn, a2, ALU.mult)
        b.tt(den, den, sv, ALU.add)
        b.tt(t1.rearrange("p (h d) -> p h d", h=4), k_.rearrange("p (h d) -> p h d", h=4), a1.unsqueeze(2).broadcast_to([S, 4, 256]), ALU.mult)
        b.tt(t2.rearrange("p (h d) -> p h d", h=4), nsv.rearrange("p (h d) -> p h d", h=4), a2.unsqueeze(2).broadcast_to([S, 4, 256]), ALU.mult)
        b.tt(t2, t2, t1, ALU.add)
        b.dma(n_s[:, :], t2, q='act')
        for c in range(8):
            b.tr(pb[7][:, c * 16:(c + 1) * 16], q_[:, c * 128:(c + 1) * 128], ident32[0:S, 0:S])
        b.cp(qTs, pb[7][:, 0:128].rearrange("p (a c) -> p a c", a=8))
        Rd = s2[0:S, 64:128].rearrange("p (a c) -> p a c", a=16)
        b.tt(Rd, a2.unsqueeze(1).broadcast_to([S, 16, 4]), ident32[0:S, 0:16].unsqueeze(2).broadcast_to([S, 16, 4]), ALU.mult)
        b.mm(pb[7][:, 128:192], ones32[0:S, 0:128], s2[0:S, 64:128])
        b.cp(decS, pb[7][:, 128:192])
        b.P.tag = 's.cloop'
        qTmAll = Cst[:, :, :, :].rearrange("p h c v -> p (h c v)")[:, 0:1024].bitcast(BF16).rearrange("p (a bq j) -> p a bq j", a=8, bq=16)
        cbb = [hf, Fb[:, 13360:14384]]
        for a_ in range(8):
            b.tt(qTmAll[:, a_, :, :], qTs[:, a_:a_ + 1, :].broadcast_to([128, 16, 16]), onehot3, ALU.mult)
        kwb = [s2[0:S, 128:256].bitcast(BF16), s2[0:S, 256:384].bitcast(BF16)]
        vbf = ub[0:S, :]
        b.cp(vbf, v_)
        cb4 = [tmpa, tmpb, xin, u1f]
        its = [(bb, hp) for hp in range(2) for bb in range(S)]
        DEPTH = 3

        def c_load(n_):
            bb, hp = its[n_]
            cbv = cb4[n_ % 4][:, :].rearrange("p (hh c v) -> p hh c v", hh=2, c=2)
            b.dma(cbv, st_c[bb, 2 * hp:2 * hp + 2].rearrange("hh (c p) v -> p hh c v", p=128))

        def c_compute(n_):
            bb, hp = its[n_]
            cb = cb4[n_ % 4]
            cbv = cb[:, :].rearrange("p (hh c v) -> p hh c v", hh=2, c=2)
            pso = (pb[6], pb[7]) if n_ % 2 == 0 else (pb[2], pb[3])
            cq = cbb[n_ % 2]
            b.cp(cq[:, :], cb[:, :], eng='act')
            cqv = cq[:, :].rearrange("p (hh c v) -> p hh c v", hh=2, c=2)
            for hh in range(2):
                h = 2 * hp + hh
                qcp = pb[4 + hh][0:S, hp * 256:(hp + 1) * 256]
                for c in range(2):
                    b.mm(qcp, qTmAll[:, 2 * h + c, bb, :], cqv[:, hh, c, :], start=(bb == 0 and c == 0), stop=(bb == S - 1 and c == 1))
                kw_ = kwb[hh]
                b.ts(kw_, t1[:, h * 256:(h + 1) * 256], ident32[0:S, bb:bb + 1], None, op0=ALU.mult)
                for c in range(2):
                    b.mm(pso[hh][:, c * 256:(c + 1) * 256], kw_[:, c * 128:(c + 1) * 128], vbf[:, h * 256:(h + 1) * 256])
                b.stt(cb[:, hh * 512:(hh + 1) * 512], cb[:, hh * 512:(hh + 1) * 512], decS[:, bb * 4 + h:bb * 4 + h + 1], pso[hh][:, :], ALU.mult, ALU.add)
            b.dma(c_s[bb, 2 * hp:2 * hp + 2].rearrange("hh (c p) v -> p hh c v", p=128), cbv, q='pool')

        for n_ in range(len(its) + DEPTH):
            if n_ < len(its):
                c_load(n_)
            if n_ >= DEPTH:
                c_compute(n_ - DEPTH)
        b.P.tag = 's.hfin'
        b.act(den, den, AF.Abs)
        b.tt(den, den, emt, ALU.max)
        rr_ = s2[0:S, 384:388]
        b.recip(rr_, den)
        b.tt(t2.rearrange("p (h d) -> p h d", h=4), v_.rearrange("p (h d) -> p h d", h=4), sv.unsqueeze(2).broadcast_to([S, 4, 256]), ALU.mult)
        for h in range(4):
            qcp = pb[4 + h % 2][0:S, (h // 2) * 256:(h // 2 + 1) * 256]
            hs_ = slice(h * 256, (h + 1) * 256)
            b.stt(hh[:, hs_], qcp, a2[:, h:h + 1], t2[:, hs_], ALU.mult, ALU.add)
            b.ts(hh[:, hs_], hh[:, hs_], rr_[:, h:h + 1], None, op0=ALU.mult)
            b.act(t2[:, hs_], hh[:, hs_], AF.Square, accum=s2[0:S, 388 + h:389 + h])
        rstd_from_ss(s2[0:S, 392:396], s2[0:S, 388:392], 256, S)
        b.act(t2, o_, AF.Sigmoid)
        gm2 = load_g(g_mlstm[0:1, :])
        b.tt(t2, t2, gm2[0:S, :], ALU.mult)
        for h in range(4):
            hs_ = slice(h * 256, (h + 1) * 256)
            b.stt(ub[0:S, hs_], hh[:, hs_], s2[0:S, 392 + h:393 + h], t2[:, hs_], ALU.mult, ALU.mult)
        tok_to_T(ub, S, mixT, 0, 8, 512, bank=2)

        b.P.tag = 'sample_ssd'
        z_ = sp_[:, OZ:OZ + 1024]
        xb = sp_[:, OXBC:OXBC + 1536]
        dtr = sp_[:, ODT:ODT + 16]
        b.dma(conv_s[:, 0:2, :], st_conv[:, 1:3, :], q='act')
        b.dma(conv_s[:, 2, :], xb, q='act')
        scv = xf[:, 0:4608].rearrange("p (j n) -> p j n", j=3)
        b.dma(scv, st_conv[:, :, :])
        cwb = uT[0:S, :, :].rearrange("p c n -> p (c n)").bitcast(F32)[:, 0:1536]
        b.dma(cwb, conv_w[3:4, :].broadcast_to([S, 1536]))
        b.tt(xb, xb, cwb, ALU.mult)
        for j in range(3):
            b.dma(cwb, conv_w[j:j + 1, :].broadcast_to([S, 1536]))
            b.tt(scv[:, j, :], scv[:, j, :], cwb, ALU.mult)
            b.tt(xb, xb, scv[:, j, :], ALU.add)
        b.dma(cwb, conv_b[0:1, :].broadcast_to([S, 1536]))
        b.tt(xb, xb, cwb, ALU.add)
        b.act(xb, xb, AF.Silu)
        xs_ = xb[:, 0:1024]
        Bm = xb[:, 1024:1280]
        Cm = xb[:, 1280:1536]
        b.dma(s2[0:S, 400:416], dt_bias.rearrange("h o -> o h").broadcast_to([S, 16]))
        b.dma(s2[0:S, 416:432], a_log.rearrange("h o -> o h").broadcast_to([S, 16]))
        dts = s2[0:S, 432:448]
        b.tt(dts, dtr, s2[0:S, 400:416], ALU.add)
        b.act(dts, dts, AF.Exp)
        b.act(dts, dts, AF.Ln, bias=1.0)
        eas = s2[0:S, 448:464]
        b.act(eas, s2[0:S, 416:432], AF.Exp)
        b.tt(eas, eas, dts, ALU.mult)
        b.act(eas, eas, AF.Exp, scale=-1.0)
        b.tt(t1[:, 0:256], Bm, Cm, ALU.mult)
        cbs = s2[0:S, 464:466]
        b.P.add('dve', lambda e: e.tensor_reduce(cbs, t1[:, 0:256].rearrange("p (g n) -> p g n", g=2), axis=AX.X, op=ALU.add), outs=[cbs], ins=[t1[:, 0:256]])
        cf = s2[0:S, 466:482]
        b.tt(cf.rearrange("p (g e) -> p g e", g=2), dts.rearrange("p (g e) -> p g e", g=2), cbs.unsqueeze(2).broadcast_to([S, 2, 8]), ALU.mult)
        b.tt(cf, cf, dskb[0:S, :], ALU.add)
        yac = xf[:, 4096:5120]
        b.tt(yac.rearrange("p (h q) -> p h q", h=16), xs_.rearrange("p (h q) -> p h q", h=16), cf.unsqueeze(2).broadcast_to([S, 16, 64]), ALU.mult)
        b.tt(t2.rearrange("p (h q) -> p h q", h=16), xs_.rearrange("p (h q) -> p h q", h=16), dts.unsqueeze(2).broadcast_to([S, 16, 64]), ALU.mult)
        for g in range(2):
            b.tr(pb[7][:, 192 + g * 16:192 + (g + 1) * 16], Cm[:, g * 128:(g + 1) * 128], ident32[0:S, 0:S])
        b.cp(CTs, pb[7][:, 192:224].rearrange("p (a c) -> p a c", a=2))
        for h2 in range(2):
            Re = s2[0:S, 128:256].rearrange("p (a c) -> p a c", a=16)
            b.tt(Re, eas.rearrange("p (j t) -> p j t", t=2)[:, :, h2].unsqueeze(1).broadcast_to([S, 16, 8]),
                 ident32[0:S, 0:16].unsqueeze(2).broadcast_to([S, 16, 8]), ALU.mult)
            b.mm(pb[7][h2 * 64:(h2 + 1) * 64, 256:384], ones32[0:S, 0:64], s2[0:S, 128:256])
        b.cp(EAc, pb[7][:, 256:384])
        yacc = hh
        b.memset(yacc, 0.0)
        b.P.tag = 's.ssdloop'
        h0b = [tmpa, xin]
        hTb = [tmpb[:, :].bitcast(BF16), u1f[:, :].bitcast(BF16)]
        xdmb = [t1.bitcast(BF16)[:, 0:1024], xf[:, 0:1024].bitcast(BF16)[:, 0:1024]]
        Bmb = kp[0:S, :]
        b.cp(Bmb, Bm)
        CTb = kp2[:, 0:32].rearrange("p (a c) -> p a c", a=2)
        b.cp(CTb, CTs)

        def s_load(bb):
            h0 = h0b[bb % 2][:, :].rearrange("p (j n) -> p j n", j=8)
            b.dma(h0, st_ssm[bb].rearrange("(j t) p n -> (t p) j n", t=2))

        def s_compute(bb):
            h0f = h0b[bb % 2]
            h0 = h0f[:, :].rearrange("p (j n) -> p j n", j=8)
            hb_ = hTb[bb % 2][:, 0:1024]
            hTs = hTb[bb % 2][:, 1024:2048]
            xdm = xdmb[bb % 2]
            b.cp(hb_, h0f[:, :], eng='act')
            ptq = pbf(0)
            for j in range(8):
                b.tr(ptq[:, j * 128:(j + 1) * 128], hb_[:, j * 128:(j + 1) * 128], identb[:, :])
            b.cp(hTs, ptq[:, 0:1024])
            for g in range(2):
                b.mm(pb[2 + g][0:S, :], CTb[:, g, :], hTs[:, g * 512:(g + 1) * 512])
                b.stt(yacc[:, g * 512:(g + 1) * 512], pb[2 + g][0:S, :], ident32[0:S, bb:bb + 1], yacc[:, g * 512:(g + 1) * 512], ALU.mult, ALU.add)
            b.ts(xdm, t2, ident32[0:S, bb:bb + 1], None, op0=ALU.mult)
            for j in range(8):
                g = j // 4
                b.mm(pb[4 + (bb % 2) * 2 + j // 4][:, (j % 4) * 128:(j % 4 + 1) * 128], xdm[:, j * 128:(j + 1) * 128], Bmb[:, g * 128:(g + 1) * 128])
            b.tt(h0, h0, EAc[:, bb * 8:(bb + 1) * 8].unsqueeze(2).broadcast_to([128, 8, 128]), ALU.mult, eng='pool')
            for jj in range(2):
                b.tt(h0f[:, jj * 512:(jj + 1) * 512], h0f[:, jj * 512:(jj + 1) * 512], pb[4 + (bb % 2) * 2 + jj][:, :], ALU.add)
            b.dma(ssm_s[bb].rearrange("(j t p) n -> (t p) j n", t=2, p=64), h0, q='pool')

        for n_ in range(S + 1):
            if n_ < S:
                s_load(n_)
            if n_ >= 1:
                s_compute(n_ - 1)
        b.P.tag = 's.ssdpost'
        b.tt(yacc.rearrange("p (h q) -> p h q", h=16), yacc.rearrange("p (h q) -> p h q", h=16), eas.unsqueeze(2).broadcast_to([S, 16, 64]), ALU.mult)
        b.tt(yac, yac, yacc, ALU.add)
        b.act(t2, z_, AF.Silu)
        b.tt(yac, yac, t2, ALU.mult)
        for g in range(2):
            gs = slice(g * 512, (g + 1) * 512)
            b.act(t2[:, gs], yac[:, gs], AF.Square, accum=s2[0:S, 484 + g:485 + g])
        rstd_from_ss(s2[0:S, 486:488], s2[0:S, 484:486], 512, S)
        gs2 = load_g(g_ssm[0:1, :])
        for g in range(2):
            gs = slice(g * 512, (g + 1) * 512)
            b.stt(ub[0:S, gs], yac[:, gs], s2[0:S, 486 + g:487 + g], gs2[0:S, gs], ALU.mult, ALU.mult)
        tok_to_T(ub, S, mixT, 8, 8, 512, bank=2)

    def main_loop():
        for blk in range(NBLK):
            last = blk == NBLK - 1
            tiles = [(i, 128) for i in range(TPB)] + ([(TPB, SB)] if last else [])
            NTOK = 512 + (SB if last else 0)
            t0 = blk * 512

            curblk[0] = blk
            b.P.tag = 'b%d.prenorm' % blk

            g_t = load_g(g_mix_pre[0:1, :])
            for (i, R) in tiles:
                src = xp[t0 + i * 128:t0 + (i + 1) * 128, :] if R == 128 else xs[:, :]
                b.dma(xres[0:R, i, :], src)
            ubs_0 = [Fb[:, n * 1024:(n + 1) * 1024] for n in range(4)] + [ub]
            prenorm_staged(tiles, g_t, uT, lambda i: i * 128, ubs_0)

            def formB(col0, ncols, evac):
                for g0 in range(0, ncols, 256):
                    n = min(256, ncols - g0)
                    wt = load_w(w_in[:, col0 + g0:col0 + g0 + n].rearrange("(kc p) n -> p kc n", p=128))
                    for m0 in range(0, n, 128):
                        mm_ = min(128, n - m0)
                        ps = pjbank()
                        for k in range(KC):
                            b.mm(ps[0:mm_, :], wt[:, k, m0:m0 + mm_], uT[:, k, 0:512], start=(k == 0), stop=(k == KC - 1))
                        evac(ps, (g0 + m0) // 128, mm_)

            def formA(col0, ncols, evac, tl):
                for g0 in range(0, ncols, 256):
                    n = min(256, ncols - g0)
                    wt = load_w(w_in[:, col0 + g0:col0 + g0 + n].rearrange("(kc p) n -> p kc n", p=128))
                    for (i, R) in tl:
                        ps = pjbank()
                        for k in range(KC):
                            b.mm(ps[0:R, 0:n], uT[:, k, i * 128:i * 128 + R], wt[:, k, 0:n], start=(k == 0), stop=(k == KC - 1))
                        evac(ps, i, R, g0, n)

            ptiles = tiles[:TPB]

            b.P.tag = 'b%d.inproj_m' % blk
            formB(OQ, 1024, lambda ps, c, m_: b.act(qT[:, c, :], ps[:, :], AF.Copy, scale=0.0625))
            formB(OK_, 1024, lambda ps, c, m_: b.cpa(kT[:, c, :], ps[:, :]))
            formA(OV, 1024, lambda ps, i, R, g0, n: b.cpa(vtok[:, i, g0:g0 + n], ps[:, 0:n]), ptiles)
            gm = load_g(g_mlstm[0:1, :])

            def ev_o(ps, i, R, g0, n):
                b.act(tmpa[:, 0:n], ps[:, 0:n], AF.Sigmoid)
                b.tt(gsig[:, i, g0:g0 + n], tmpa[:, 0:n], gm[:, g0:g0 + n], ALU.mult)
            formA(OO, 1024, ev_o, ptiles)
            wt = load_w(w_in[:, OIG:OIG + 8].rearrange("(kc p) n -> p kc n", p=128))
            for gi in range(2):
                ps = pjbank()
                for k in range(KC):
                    b.mm(ps[0:4, :], wt[:, k, gi * 4:gi * 4 + 4], uT[:, k, 0:512], start=(k == 0), stop=(k == KC - 1))
                b.cp(rows[gi][0:4, :], ps[0:4, :])

            chk(1 + 10 * blk)
            b.P.tag = 'b%d.gates' % blk
            ti, tf, lnv, Bn, Aa, Gf = rows[0], rows[1], rows[2], rows[3], rows[4], rows[5]
            b.act(ti[0:4, :], rows[0][0:4, :], AF.Tanh, scale=1.0 / 15.0, bias=smallp[0:4, 0:1])
            b.act(tf[0:4, :], rows[1][0:4, :], AF.Tanh, scale=1.0 / 15.0, bias=smallp[0:4, 1:2])
            b.act(lnv[0:4, :], tf[0:4, :], AF.Exp, scale=-15.0)
            b.act(lnv[0:4, :], lnv[0:4, :], AF.Ln, bias=1.0)
            b.scan(Bn[0:4, :], ones32[0:4, :], lnv[0:4, :], carry[0:4, 0:1], ALU.mult, ALU.add)
            b.stt(Aa[0:4, :], ti[0:4, :], 15.0, Bn[0:4, :], ALU.mult, ALU.add)
            Gx = rows[1]
            b.scan(Gx[0:4, :], ones32[0:4, :], Aa[0:4, :], carry[0:4, 1:2], ALU.mult, ALU.max)
            GP = rows[5]
            for (i, R) in ptiles:
                cs = slice(i * 128, (i + 1) * 128)
                gc = carry[0:4, 1:2] if i == 0 else Gx[0:4, i * 128 - 1:i * 128]
                b.ts(GP[0:4, cs], Aa[0:4, cs], gc, None, op0=ALU.subtract)
                b.ts(GP[32:36, cs], Gx[0:4, cs], -1.0, gc, op0=ALU.mult, op1=ALU.add)
                b.ts(GP[96:100, cs], Aa[0:4, cs], Gx[0:4, (i + 1) * 128 - 1:(i + 1) * 128], None, op0=ALU.subtract)
            b.tt(GP[64:68, :], Bn[0:4, :], Gx[0:4, :], ALU.subtract)
            for r0 in (0, 32, 64, 96):
                b.act(GP[r0:r0 + 4, :], GP[r0:r0 + 4, :], AF.Exp)
            if last:
                b.tt(sm[0:4, 8:9], Gx[0:4, 511:512], Bn[0:4, 511:512], ALU.subtract)
                b.dma_nc(m_p, sm[0:4, 8:9], q='act')
            b.cp(carry[0:4, 0:1], Bn[0:4, 511:512])
            b.cp(carry[0:4, 1:2], Gx[0:4, 511:512])

            b.P.tag = 'b%d.mlstm' % blk
            cub = [pb[2], pb[3], pb[6], pb[0]]
            for (i, R) in ptiles:
                cs = slice(i * 128, (i + 1) * 128)
                b.tr(pb[7][:, 0:128], GP[:, cs], ident32[:, :])
                b.cp(GT[:, :], pb[7][:, 0:128])
                b.mm(pb[7][:, 128:132], sel127[:, :], GT[:, 32:36])
                b.cp(decb[:, 0:4], pb[7][:, 128:132], eng='act')
                b.tt(eAm[:, :, :], mask32[:, :].unsqueeze(1).broadcast_to([128, 4, 128]), GT[:, 0:4].unsqueeze(2).broadcast_to([128, 4, 128]), ALU.mult)
                ptk = pbf(0)
                for h in range(4):
                    for c in range(2):
                        b.tr(ptk[:, h * 256 + c * 128:h * 256 + (c + 1) * 128], kT[:, h * 2 + c, cs], identb[:, :])
                for h in range(4):
                    for c in range(2):
                        b.mm(pb[1][:, h * 128:(h + 1) * 128], kT[:, h * 2 + c, cs], qT[:, h * 2 + c, cs], start=(c == 0), stop=(c == 1))
                b.tt(kq4[:, :, :], ptk[:, 0:1024].rearrange("p (h d) -> p h d", h=4), GT[:, 96:100].unsqueeze(2).broadcast_to([128, 4, 256]), ALU.mult)
                b.tt(spb[:, :, :], pb[1][:, :].rearrange("p (h t) -> p h t", h=4), eAm[:, :, :], ALU.mult)
                for h in range(4):
                    brp = pb[4 + h // 2][:, (h % 2) * 256:(h % 2 + 1) * 256]
                    b.mm(brp, spb[:, h, :], vtok[:, i, h * 256:(h + 1) * 256], start=True, stop=False)
                    b.mm(brp, qT[:, h * 2, cs], Cbf[:, h, 0, :], start=False, stop=False)
                    b.mm(brp, qT[:, h * 2 + 1, cs], Cbf[:, h, 1, :], start=False, stop=True)
                for h in range(4):
                    dn = pb[7][:, 136 + h:137 + h]
                    b.mm(dn, spb[:, h, :], onesb[:, :], start=True, stop=False)
                    b.mm(dn, qT[:, h * 2, cs], nbf[:, h * 2:h * 2 + 1], start=False, stop=False)
                    b.mm(dn, qT[:, h * 2 + 1, cs], nbf[:, h * 2 + 1:h * 2 + 2], start=False, stop=True)
                for h in range(4):
                    for c in range(2):
                        b.mm(cub[h][:, c * 256:(c + 1) * 256], kq4[:, h, c * 128:(c + 1) * 128], vtok[:, i, h * 256:(h + 1) * 256])
                for h in range(4):
                    for c in range(2):
                        b.mm(pb[7][:, 144 + h * 2 + c:145 + h * 2 + c], kq4[:, h, c * 128:(c + 1) * 128], onesb[:, :])
                for h in range(4):
                    brp = pb[4 + h // 2][:, (h % 2) * 256:(h % 2 + 1) * 256]
                    b.act(tmpa[:, h * 256:(h + 1) * 256], brp, AF.Square, accum=sm[:, 16 + h:17 + h])
                b.cp(sm[:, 12:16], pb[7][:, 136:140])
                for h in range(4):
                    b.stt(Cst[:, h, :, :].rearrange("p c v -> p (c v)"), Cst[:, h, :, :].rearrange("p c v -> p (c v)"), decb[:, h:h + 1], cub[h][:, :], ALU.mult, ALU.add)
                    b.cp(Cbf[:, h, :, :], Cst[:, h, :, :], eng='act')
                b.tt(nst[:, :].rearrange("p (h c) -> p h c", h=4), nst[:, :].rearrange("p (h c) -> p h c", h=4), decb[:, 0:4].unsqueeze(2).broadcast_to([128, 4, 2]), ALU.mult)
                b.tt(nst[:, :], nst[:, :], pb[7][:, 144:152], ALU.add)
                b.cp(nbf[:, :], nst[:, :])
                den = sm[:, 12:16]
                iw = GT[:, 32:36]
                emt = GT[:, 64:68]
                b.act(sm[:, 20:24], den, AF.Abs)
                b.tt(sm[:, 20:24], sm[:, 20:24], iw, ALU.mult)
                b.tt(sm[:, 20:24], sm[:, 20:24], emt, ALU.max)
                b.recip(sm[:, 24:28], sm[:, 20:24])
                b.tt(sm[:, 24:28], sm[:, 24:28], iw, ALU.mult)
                b.tt(sm[:, 28:32], sm[:, 16:20], sm[:, 24:28], ALU.mult)
                b.tt(sm[:, 28:32], sm[:, 28:32], sm[:, 24:28], ALU.mult)
                rstd_from_ss(sm[:, 32:36], sm[:, 28:32], 256, 128)
                b.tt(sm[:, 36:40], sm[:, 32:36], sm[:, 24:28], ALU.mult)
                for h in range(4):
                    brp = pb[4 + h // 2][:, (h % 2) * 256:(h % 2 + 1) * 256]
                    b.stt(hf[:, h * 256:(h + 1) * 256], brp, sm[:, 36 + h:37 + h], gsig[:, i, h * 256:(h + 1) * 256], ALU.mult, ALU.mult)
                tok_to_T(hf, 128, mixT, 0, 8, i * 128, bank=1)

            if last:
                for h in range(4):
                    b.dma(c_p[h].rearrange("(c p) v -> p c v", p=128), Cst[:, h, :, :], q='act')
                b.tr(pb[7][0:8, 160:288], nst[:, :], ident32[:, :])
                b.cp(tmpa[0:8, 0:128], pb[7][0:8, 160:288])
                b.dma(n_p, tmpa[0:8, 0:128], q='act')

            chk(2 + 10 * blk)
            b.P.tag = 'b%d.inproj_s' % blk
            formA(OZ, 1024, lambda ps, i, R, g0, n: b.act(zs[:, i, g0:g0 + n], ps[:, 0:n], AF.Silu), ptiles)
            def ev_xbc(ps, c, m_):
                xraw = tmpb if c % 2 == 0 else xin
                acc = tmpa[:, 0:512] if c % 2 == 0 else tmpa[:, 512:1024]
                b.cp(xraw[:, 0:3], convtail[:, c, :])
                b.cp(xraw[:, 3:515], ps[:, :], eng='act')
                b.cp(convtail[:, c, :], xraw[:, 512:515])
                b.ts(acc, xraw[:, 0:512], gpar[:, c:c + 1], gpar[:, 48 + c:49 + c], op0=ALU.mult, op1=ALU.add)
                for j in range(1, 4):
                    b.stt(acc, xraw[:, j:j + 512], gpar[:, j * 12 + c:j * 12 + c + 1], acc, ALU.mult, ALU.add)
                b.act(xbcT[:, c, :], acc, AF.Silu)
            formB(OXBC, 1536, ev_xbc)
            if last:
                for g3 in range(3):
                    for c4 in range(4):
                        c = g3 * 4 + c4
                        b.tr(pb[6][0:3, c4 * 128:(c4 + 1) * 128], convtail[:, c, :], ident32[:, :])
                    dstt = tmpa[0:3, g3 * 512:(g3 + 1) * 512] if g3 < 2 else tmpb[0:3, 0:512]
                    b.cp(dstt, pb[6][0:3, :])
                b.dma(conv_p[:, 0:1024], tmpa[0:3, 0:1024], q='act')
                b.dma(conv_p[:, 1024:1536], tmpb[0:3, 0:512], q='act')
            wt = load_w(w_in[:, ODT:ODT + 16].rearrange("(kc p) n -> p kc n", p=128))
            ps = pjbank()
            for k in range(KC):
                b.mm(ps[0:16, :], wt[:, k, 0:16], uT[:, k, 0:512], start=(k == 0), stop=(k == KC - 1))
            dtv, av, cumn = rows[0], rows[1], rows[2]
            SPk = rows[3]
            b.act(dtv[0:16, :], ps[0:16, :], AF.Exp, bias=smallp[0:16, 2:3])
            b.act(dtv[0:16, :], dtv[0:16, :], AF.Ln, bias=1.0)
            b.ts(av[0:16, :], dtv[0:16, :], smallp[0:16, 4:5], None, op0=ALU.mult)

            chk(3 + 10 * blk)
            b.P.tag = 'b%d.ssd' % blk
            gs_t = load_g(g_ssm[0:1, :])
            xwq = [xw, xw2[:, :]]
            btq = [btok, btok2[:, :, :]]

            def ssd_front(i):
                q_ = i % 2
                cs = slice(i * 128, (i + 1) * 128)
                GTq = GT2[:, q_, :]
                ops = []
                A = ops.append
                ptx = pbf(2)
                for c in range(8):
                    A(lambda c=c: b.tr(ptx[:, c * 128:(c + 1) * 128], xbcT[:, c, cs], identb[:, :]))
                ptb = pbf(3)
                for g in range(2):
                    A(lambda g=g: b.tr(ptb[:, 512 + g * 128:512 + (g + 1) * 128], xbcT[:, 8 + g, cs], identb[:, :]))
                for g in range(2):
                    A(lambda g=g: b.mm(pb[3][:, g * 128:(g + 1) * 128], xbcT[:, 8 + g, cs], xbcT[:, 10 + g, cs]))
                A(lambda: b.scan(cumn[0:16, cs], ones32[0:16, 0:128], av[0:16, cs], 0.0, ALU.mult, ALU.add))
                A(lambda: b.cp(SPk[0:16, cs], cumn[0:16, cs]))
                A(lambda: b.cp(SPk[32:48, cs], dtv[0:16, cs]))
                A(lambda: b.act(SPk[64:80, cs], cumn[0:16, cs], AF.Exp, scale=-1.0))
                wtmp = rows[4]
                A(lambda: b.ts(wtmp[0:16, cs], cumn[0:16, cs], cumn[0:16, (i + 1) * 128 - 1:(i + 1) * 128], None, op0=ALU.subtract))
                A(lambda: b.act(wtmp[0:16, cs], wtmp[0:16, cs], AF.Exp))
                A(lambda: b.tt(SPk[96:112, cs], wtmp[0:16, cs], dtv[0:16, cs], ALU.mult))
                A(lambda: b.cp(xtok, ptx[:, 0:1024], eng='act'))
                A(lambda: b.cp(tmpb[:, 0:256], pb[3][:, 0:256], eng='act'))
                A(lambda: b.cp(btq[q_], ptb[:, 512:768].rearrange("p (g n) -> p g n", g=2)))
                A(lambda: b.tr(pb[7][:, 0:128], SPk[:, cs], ident32[:, :]))
                A(lambda: b.cp(GTq, pb[7][:, 0:128]))
                A(lambda: b.mm(pb[7][:, 128:144], sel127[:, :], GTq[:, 64:80]))
                A(lambda: b.cp(decb2[:, q_, :], pb[7][:, 128:144], eng='act'))
                x3 = xtok.rearrange("p (h q) -> p h q", h=16)
                A(lambda: b.tt(xdt.rearrange("p (h q) -> p h q", h=16), x3, GTq[:, 32:48].unsqueeze(2).broadcast_to([128, 16, 64]), ALU.mult, eng='pool'))
                A(lambda: b.tt(xwq[q_].rearrange("p (h q) -> p h q", h=16), x3, GTq[:, 96:112].unsqueeze(2).broadcast_to([128, 16, 64]), ALU.mult, eng='pool'))
                A(lambda: b.tt(yap[q_][:, :].rearrange("p (h q) -> p h q", h=16), x3, dskb[:, :].unsqueeze(2).broadcast_to([128, 16, 64]), ALU.mult, eng='pool'))
                for j in range(4):
                    R4 = tmpb[0:16, 512:1024].rearrange("p (h n) -> p h n", h=4)
                    A(lambda j=j, R4=R4: b.tt(R4, cumn[0:16, cs].unsqueeze(1).broadcast_to([16, 4, 128]),
                                              ident32[0:16, 4 * j:4 * j + 4].unsqueeze(2).broadcast_to([16, 4, 128]), ALU.mult))
                    A(lambda: b.mm(pb[6][:, :], negones[:, :], tmpb[0:16, 512:1024], start=True, stop=False))
                    A(lambda: b.mm(pb[6][:, :], identb[:, :], negm4b[:, :, :], start=False, stop=True))
                    for hh in range(4):
                        A(lambda j=j, hh=hh: b.act(tmpa[:, hh * 128:(hh + 1) * 128], pb[6][:, hh * 128:(hh + 1) * 128], AF.Exp, bias=GTq[:, 4 * j + hh:4 * j + hh + 1]))
                    A(lambda j=j: b.tt(WT[:, 4 * j:4 * j + 4, :], tmpa[:, 0:512].rearrange("p (h n) -> p h n", h=4),
                                       tmpb[:, (j // 2) * 128:(j // 2 + 1) * 128].unsqueeze(1).broadcast_to([128, 4, 128]), ALU.mult))
                for h in range(16):
                    A(lambda h=h: b.mm(pb[4 + h // 8][:, (h % 8) * 64:(h % 8 + 1) * 64], WT[:, h, :], xdt[:, h * 64:(h + 1) * 64]))
                for g in range(2):
                    A(lambda g=g: b.tt(yap[q_][:, g * 512:(g + 1) * 512], yap[q_][:, g * 512:(g + 1) * 512], pb[4 + g][:, :], ALU.add))
                return ops

            def ssd_back(i):
                q_ = i % 2
                cs = slice(i * 128, (i + 1) * 128)
                GTq = GT2[:, q_, :]
                ya = yap[q_]
                ops = []
                A = ops.append
                for g in range(2):
                    A(lambda g=g: b.mm(pb[g][:, :], xbcT[:, 10 + g, cs], hTbf[:, 8 * g:8 * g + 8, :]))
                for g in range(2):
                    A(lambda g=g: b.tt(xin[:, g * 512:(g + 1) * 512].rearrange("p (h q) -> p h q", h=8), pb[g][:, :].rearrange("p (h q) -> p h q", h=8),
                                       GTq[:, 64 + 8 * g:72 + 8 * g].unsqueeze(2).broadcast_to([128, 8, 64]), ALU.mult))
                A(lambda: b.tt(ya[:, :], ya[:, :], xin[:, :], ALU.add))
                for g in range(2):
                    A(lambda g=g: b.mm(pb[g][:, :], btq[q_][:, g, :], xwq[q_][:, g * 512:(g + 1) * 512]))
                A(lambda: b.tt(ya[:, :], ya[:, :], zs[:, i, :], ALU.mult))
                for g in range(2):
                    hv = hT[:, 8 * g:8 * g + 8, :]
                    A(lambda g=g, hv=hv: b.tt(hv, hv, decb2[:, q_, 8 * g:8 * g + 8].unsqueeze(2).broadcast_to([128, 8, 64]), ALU.mult, eng='pool'))
                for g in range(2):
                    A(lambda g=g: b.act(xin[:, g * 512:(g + 1) * 512], ya[:, g * 512:(g + 1) * 512], AF.Square, accum=sm[:, 40 + g:41 + g]))
                for g in range(2):
                    hv = hT[:, 8 * g:8 * g + 8, :]
                    A(lambda g=g, hv=hv: b.tt(hv, hv, pb[g][:, :].rearrange("p (h q) -> p h q", h=8), ALU.add))
                A(lambda: b.cp(hTbf[:, :, :], hT[:, :, :], eng='act'))
                A(lambda: rstd_from_ss(sm[:, 42:44], sm[:, 40:42], 512, 128))
                for g in range(2):
                    A(lambda g=g: b.stt(hf[:, g * 512:(g + 1) * 512], ya[:, g * 512:(g + 1) * 512], sm[:, 42 + g:43 + g], gs_t[:, g * 512:(g + 1) * 512], ALU.mult, ALU.mult))
                A(lambda: tok_to_T(hf, 128, mixT, 8, 8, i * 128, bank=7))
                return ops

            def interleave(fl, bl):
                nf, nb = len(fl), len(bl)
                out = []
                fi = bi = 0
                while fi < nf or bi < nb:
                    if bi >= nb or (fi < nf and fi * nb <= bi * nf):
                        out.append(fl[fi]); fi += 1
                    else:
                        out.append(bl[bi]); bi += 1
                return out

            for op in ssd_front(0):
                op()
            for (i, R) in ptiles:
                fl = ssd_front(i + 1) if i + 1 < TPB else []
                for op in interleave(fl, ssd_back(i)):
                    op()

            if last:
                for j in range(8):
                    b.tr(pb[j % 2][:, 0:128], hT[:, 2 * j:2 * j + 2, :].rearrange("p a q -> p (a q)"), ident32[:, :])
                    b.cp(tmpb[:, (j % 4) * 128:(j % 4 + 1) * 128], pb[j % 2][:, 0:128])
                    b.dma(ssm_p[j * 128:(j + 1) * 128, :], tmpb[:, (j % 4) * 128:(j % 4 + 1) * 128], q='act')

            chk(4 + 10 * blk)
            b.P.tag = 'b%d.sample' % blk
            if last:
                if True:
                    sample_mixers()

            chk(5 + 10 * blk)
            b.P.tag = 'b%d.outproj' % blk
            g_t = load_g(g_mix_post[0:1, :])
            stream_proj(mixT, 16, w_out, tiles, lambda grp, banks: post_staged(
                grp, banks, g_t, lambda i, R: (xp[t0 + i * 128:t0 + (i + 1) * 128, :] if R == 128 else xs[:, :])))

            chk(6 + 10 * blk)
            for layer in range(2):
                if layer == 1:
                    pool_mixer(blk, tiles, last)
                    chk(8 + 10 * blk)
                ffn(layer, tiles, last)
                chk(7 + 2 * layer + 10 * blk)

            b.P.tag = 'b%d.out' % blk
            for (i, R) in tiles:
                if R == 128:
                    b.dma(y_p[t0 + i * 128:t0 + (i + 1) * 128, :], xres[:, i, :], q='act')
                else:
                    b.dma(y_s[:, :], xres[0:R, i, :], q='act')


    try:
        main_loop()
    except _Stop:
        pass
    b.P.emit()
    return nc, b.P


_CACHE = {}


def kernel(x_prompt, x_sample, state_mlstm_c, state_mlstm_n, state_mlstm_m, state_ssm, state_conv, state_pool,
           g_mix_pre, g_mix_post, g_ffn_pre, g_ffn_post, w_in_ab, b_igate, b_fgate, g_mlstm,
           conv_w, conv_b, dt_bias, a_log, d_skip, g_ssm, w_out_ab, w_pool, pool_scale,
           w_gate, w_up, w_down):
    f = lambda a: np.ascontiguousarray(np.asarray(a, dtype=np.float32))
    if 'nc' not in _CACHE:
        _CACHE['nc'] = build()[0]
    nc = _CACHE['nc']
    NCORE = 8
    shared = {
        "g_mix_pre": f(g_mix_pre), "g_mix_post": f(g_mix_post), "g_ffn_pre": f(g_ffn_pre), "g_ffn_post": f(g_ffn_post),
        "w_in_ab": f(w_in_ab[0]), "b_igate": f(b_igate[0]).reshape(4, 1), "b_fgate": f(b_fgate[0]).reshape(4, 1),
        "g_mlstm": f(g_mlstm[0]).reshape(1, D), "conv_w": f(conv_w[0]), "conv_b": f(conv_b[0]).reshape(1, 1536),
        "dt_bias": f(dt_bias[0]).reshape(16, 1), "a_log": f(a_log[0]).reshape(16, 1), "d_skip": f(d_skip[0]).reshape(1, 16),
        "g_ssm": f(g_ssm[0]).reshape(1, D), "w_out_ab": f(w_out_ab[0]), "w_pool": f(w_pool[0]),
        "pool_scale": f(pool_scale[0]).reshape(1, D), "w_gate": f(w_gate), "w_up": f(w_up), "w_down": f(w_down),
    }
    in_maps = []
    for c in range(NCORE):
        r = slice(c * SB, (c + 1) * SB)
        m = dict(shared)
        m["x_prompt"] = f(x_prompt[c])
        m["x_sample"] = f(x_sample[r, 0, :])
        m["state_mlstm_c"] = f(state_mlstm_c[0, r])
        m["state_mlstm_n"] = f(state_mlstm_n[0, r]).reshape(SB, 1024)
        m["state_mlstm_m"] = f(state_mlstm_m[0, r])
        m["state_ssm"] = f(state_ssm[0, r])
        m["state_conv"] = f(state_conv[0, r])
        m["state_pool"] = f(state_pool[0, r])
        in_maps.append(m)
    res = run_bass_kernel_spmd(nc, in_maps, core_ids=list(range(NCORE)))
    R = res.results
    cat = lambda k: np.concatenate([np.asarray(r_[k]) for r_ in R], axis=0)
    stk = lambda k: np.stack([np.asarray(r_[k]) for r_ in R], axis=0)
    y_prompt = stk("y_prompt")
    y_sample = cat("y_sample").reshape(128, 1, D)
    c_prompt = stk("c_prompt")[None]
    c_sample = cat("c_sample")[None]
    n_prompt = stk("n_prompt").reshape(1, 8, 4, 256)
    n_sample = cat("n_sample").reshape(1, 128, 4, 256)
    m_prompt = stk("m_prompt").reshape(1, 8, 4)
    m_sample = cat("m_sample").reshape(1, 128, 4)
    ssm_prompt = stk("ssm_prompt").reshape(1, 8, 16, 64, 128)
    ssm_sample = cat("ssm_sample").reshape(1, 128, 16, 64, 128)
    conv_prompt = stk("conv_prompt")[None]
    conv_sample = cat("conv_sample")[None]
    pool_prompt = stk("pool_prompt")[None]
    pool_sample = cat("pool_sample")[None]
    outs = (y_prompt, y_sample, c_prompt, c_sample, n_prompt, n_sample, m_prompt, m_sample,
            ssm_prompt, ssm_sample, conv_prompt, conv_sample, pool_prompt, pool_sample)
    return tuple(np.ascontiguousarray(o, dtype=np.float32) for o in outs)
```

```python
import numpy as np
import concourse.bass as bass
import concourse.mybir as mybir

F32 = mybir.dt.float32
BF16 = mybir.dt.bfloat16
ALU = mybir.AluOpType
AF = mybir.ActivationFunctionType
AX = mybir.AxisListType

_DTSIZE = {}


def dtsize(dt):
    s = str(dt)
    if '64' in s:
        return 8
    if '32' in s:
        return 4
    if '16' in s:
        return 2
    return 1


class Op:
    __slots__ = ('eng', 'fn', 'deps', 'idx', 'sig', 'dma', 'semslot', 'semval', 'waits', 'eidx', 'tag')


class Prog:
    ENGS = ('pe', 'act', 'dve', 'pool', 'sp')
    NDMA = 12

    def __init__(self, nc):
        self.nc = nc
        self.ops = []
        self.w = {}
        self.r = {}
        self.dma_count = {e: 0 for e in self.ENGS}
        self.dma_last = {}
        self.bank_last = {}
        self.tag = ''

    @staticmethod
    def box(ap):
        t = ap.tensor
        name = t.name
        esz = dtsize(ap.dtype)
        dims = list(ap.ap)
        sp = str(ap.space)
        if 'DRAM' in sp.upper():
            lo = ap.offset
            hi = lo
            for st, cnt in dims:
                hi += abs(st) * (cnt - 1)
            return name, (0, 1, lo * esz, (hi + 1) * esz)
        pstep, pcnt = dims[0]
        p0 = ap.start_partition
        if callable(p0):
            p0 = p0()
        off = ap.offset - p0 * pstep
        lo = off
        hi = off
        for st, cnt in dims[1:]:
            hi += abs(st) * (cnt - 1)
        return name, (p0, p0 + pcnt, lo * esz, (hi + 1) * esz)

    @staticmethod
    def _ov(a, b):
        return a[0] < b[1] and b[0] < a[1] and a[2] < b[3] and b[2] < a[3]

    @staticmethod
    def _cov(a, b):
        return a[0] <= b[0] and a[1] >= b[1] and a[2] <= b[2] and a[3] >= b[3]

    def add(self, eng, fn, outs=(), ins=(), dma=False):
        op = Op()
        op.eng = eng
        op.fn = fn
        op.idx = len(self.ops)
        op.sig = False
        op.dma = dma
        op.deps = set()
        op.waits = None
        op.tag = self.tag
        for ap in ins:
            name, bx = self.box(ap)
            for (b2, o2) in self.w.get(name, ()):
                if self._ov(bx, b2):
                    op.deps.add(o2)
        outboxes = []
        for ap in outs:
            name, bx = self.box(ap)
            outboxes.append((name, bx))
            for (b2, o2) in self.w.get(name, ()):
                if self._ov(bx, b2):
                    op.deps.add(o2)
            for (b2, o2) in self.r.get(name, ()):
                if self._ov(bx, b2):
                    op.deps.add(o2)
        for ap in ins:
            name, bx = self.box(ap)
            self.r.setdefault(name, []).append((bx, op))
        for name, bx in outboxes:
            self.w[name] = [(b2, o2) for (b2, o2) in self.w.get(name, []) if not self._cov(bx, b2)]
            self.w[name].append((bx, op))
            self.r[name] = [(b2, o2) for (b2, o2) in self.r.get(name, []) if not self._cov(bx, b2)]
        for ap in list(ins) + list(outs):
            if 'PSUM' in str(ap.space).upper():
                bl = self.bank_last.setdefault(ap.tensor.name, {})
                for e2, o2 in bl.items():
                    if e2 != eng:
                        op.deps.add(o2)
                bl[eng] = op
        op.deps.discard(op)
        if dma:
            n = self.dma_count[eng]
            self.dma_count[eng] = n + 1
            op.semslot = n % self.NDMA
            op.semval = 16 * (n // self.NDMA + 1)
            prev = self.dma_last.get((eng, op.semslot))
            if prev is not None:
                op.deps.add(prev)
            self.dma_last[(eng, op.semslot)] = op
        self.ops.append(op)
        return op

    def emit(self):
        nc = self.nc
        for op in self.ops:
            nd = set()
            for d in op.deps:
                if d.eng == 'pe' and op.eng == 'pe' and not d.dma and not op.dma:
                    continue
                nd.add(d)
            best = {}
            keep = set()
            for d in nd:
                if d.dma:
                    keep.add(d)
                else:
                    if d.eng not in best or d.idx > best[d.eng].idx:
                        best[d.eng] = d
            keep.update(best.values())
            op.deps = keep
            for d in keep:
                d.sig = True
        cnt = {e: 0 for e in self.ENGS}
        for op in self.ops:
            if op.dma:
                continue
            if op.sig:
                cnt[op.eng] += 1
                op.semval = cnt[op.eng]
        SEG = 1500
        esem = {e: [nc.alloc_semaphore('s_%s_%d' % (e, k)) for k in range(cnt[e] // SEG + 1)] for e in self.ENGS}
        dsem = {}
        for e in self.ENGS:
            if self.dma_count[e]:
                for i in range(min(self.NDMA, self.dma_count[e])):
                    dsem[(e, i)] = nc.alloc_semaphore('d_%s_%d' % (e, i))
        seen = {e: {} for e in self.ENGS}
        nwaits = 0
        for op in self.ops:
            need = {}
            for d in op.deps:
                key = ('d', d.eng, d.semslot) if d.dma else ('e', d.eng)
                if d.semval > need.get(key, 0):
                    need[key] = d.semval
            ws = []
            for key, v in need.items():
                if seen[op.eng].get(key, 0) >= v:
                    continue
                seen[op.eng][key] = v
                if key[0] == 'd':
                    ws.append((dsem[(key[1], key[2])], v))
                else:
                    ws.append((esem[key[1]][(v - 1) // SEG], (v - 1) % SEG + 1))
            op.waits = ws
            nwaits += len(ws)
        per = {e: [o for o in self.ops if o.eng == e] for e in self.ENGS}
        self.stats = {e: len(per[e]) for e in self.ENGS}
        self.stats['waits'] = nwaits
        self.stats['sig'] = dict(cnt)

        def run(eng, name):
            for op in per[name]:
                for sem, v in op.waits:
                    eng.wait_ge(sem, v)
                ins = op.fn(eng)
                if op.dma:
                    ins.then_inc(dsem[(name, op.semslot)], 16)
                elif op.sig:
                    ins.then_inc(esem[name][(op.semval - 1) // SEG], 1)
            for i in range(min(self.NDMA, self.dma_count[name])):
                last = self.dma_last[(name, i)]
                eng.wait_ge(dsem[(name, i)], last.semval)

        with nc.Block() as block:
            @block.tensor
            def _(e):
                run(e, 'pe')

            @block.scalar
            def _(e):
                run(e, 'act')

            @block.vector
            def _(e):
                run(e, 'dve')

            @block.gpsimd
            def _(e):
                run(e, 'pool')

            @block.sync
            def _(e):
                run(e, 'sp')

from concourse.bass_utils import run_bass_kernel_spmd

D = 1024
KC = 8
T = 2048
NBLK = 4
TPB = 4
SB = 16
FF = 2816
FC = 22
IN_DIM = 6680
EPS = 1e-6
OQ, OK_, OV, OO, OIG, OFG, OZ, OXBC, ODT = 0, 1024, 2048, 3072, 4096, 4100, 4104, 5128, 6664


KSTAGE = 99


class _Stop(Exception):
    pass


def chk(n):
    if n >= KSTAGE:
        raise _Stop()


class Bld:
    def __init__(s, nc):
        s.nc = nc
        s.P = Prog(nc)
        s.rr = 0

    def mm(s, out, lhsT, rhs, start=True, stop=True):
        s.P.add('pe', lambda e: e.matmul(out, lhsT, rhs, start=start, stop=stop), outs=[out], ins=[lhsT, rhs])

    def tr(s, out, in_, ident):
        s.P.add('pe', lambda e: e.transpose(out, in_, ident), outs=[out], ins=[in_, ident])

    def act(s, out, in_, func, scale=1.0, bias=None, accum=None):
        ins = [in_]
        outs = [out]
        kw = {}
        if bias is not None:
            kw['bias'] = bias
            if not isinstance(bias, (int, float)):
                ins.append(bias)
        if not isinstance(scale, (int, float)):
            ins.append(scale)
        if accum is not None:
            kw['accum_out'] = accum
            outs.append(accum)
        s.P.add('act', lambda e: e.activation(out, in_, func, scale=scale, **kw), outs=outs, ins=ins)

    def ts(s, out, in0, s1, s2=None, op0=ALU.mult, op1=None, eng='dve', accum=None):
        ins = [in0]
        outs = [out]
        for x in (s1, s2):
            if x is not None and not isinstance(x, (int, float)):
                ins.append(x)
        kw = {}
        if op1 is not None:
            kw['op1'] = op1
        if accum is not None:
            kw['accum_out'] = accum
            outs.append(accum)
        s.P.add(eng, lambda e: e.tensor_scalar(out, in0, s1, s2, op0=op0, **kw), outs=outs, ins=ins)

    def stt(s, out, in0, sc, in1, op0, op1):
        ins = [in0, in1]
        if not isinstance(sc, (int, float)):
            ins.append(sc)
        s.P.add('dve', lambda e: e.scalar_tensor_tensor(out, in0, sc, in1, op0=op0, op1=op1), outs=[out], ins=ins)

    def tt(s, out, in0, in1, op, eng='dve'):
        s.P.add(eng, lambda e: e.tensor_tensor(out, in0, in1, op=op), outs=[out], ins=[in0, in1])

    def cp(s, out, in_, eng='dve'):
        if eng == 'act':
            s.P.add('act', lambda e: e.copy(out, in_), outs=[out], ins=[in_])
        else:
            s.P.add(eng, lambda e: e.tensor_copy(out, in_), outs=[out], ins=[in_])

    def cpa(s, out, in_):
        s.rr += 1
        s.cp(out, in_, eng='act' if s.rr % 2 else 'dve')

    def memset(s, ap, v, eng='dve'):
        s.P.add(eng, lambda e: e.memset(ap, v), outs=[ap])

    def recip(s, out, in_):
        s.P.add('dve', lambda e: e.reciprocal(out, in_), outs=[out], ins=[in_])

    def scan(s, out, d0, d1, init, op0, op1):
        ins = [d0, d1]
        if not isinstance(init, (int, float)):
            ins.append(init)
        s.P.add('dve', lambda e: e.tensor_tensor_scan(out, d0, d1, init, op0=op0, op1=op1), outs=[out], ins=ins)

    def asel(s, ap, pattern, cm, base, cmp, fill=0.0):
        s.P.add('pool', lambda e: e.affine_select(ap, ap, pattern=pattern, compare_op=cmp, fill=fill, base=base, channel_multiplier=cm), outs=[ap], ins=[ap])

    def dma(s, out, in_, q='sp'):
        s.P.add(q, lambda e: e.dma_start(out=out, in_=in_), outs=[out], ins=[in_], dma=True)

    def dma_nc(s, out, in_, q='sp'):
        s.P.add(q, lambda e: e.dma_start(out=out, in_=in_, allow_slow_non_contiguous=True), outs=[out], ins=[in_], dma=True)

    def sb(s, name, shape, dt=F32):
        return s.nc.alloc_sbuf_tensor(name, list(shape), dt)


def build():
    nc = bass.Bass("TRN2", target_bir_lowering=False)
    b = Bld(nc)

    def din(name, shape):
        return nc.dram_tensor(name, list(shape), F32, kind="ExternalInput").ap()

    def dout(name, shape):
        return nc.dram_tensor(name, list(shape), F32, kind="ExternalOutput").ap()

    xp = din("x_prompt", [T, D])
    xs = din("x_sample", [SB, D])
    st_c = din("state_mlstm_c", [SB, 4, 256, 256])
    st_n = din("state_mlstm_n", [SB, 1024])
    st_m = din("state_mlstm_m", [SB, 4])
    st_ssm = din("state_ssm", [SB, 16, 64, 128])
    st_conv = din("state_conv", [SB, 3, 1536])
    st_pool = din("state_pool", [SB, 15, D])
    g_mix_pre = din("g_mix_pre", [2, D])
    g_mix_post = din("g_mix_post", [2, D])
    g_ffn_pre = din("g_ffn_pre", [2, D])
    g_ffn_post = din("g_ffn_post", [2, D])
    w_in = din("w_in_ab", [D, IN_DIM])
    b_ig = din("b_igate", [4, 1])
    b_fg = din("b_fgate", [4, 1])
    g_mlstm = din("g_mlstm", [1, D])
    conv_w = din("conv_w", [4, 1536])
    conv_b = din("conv_b", [1, 1536])
    dt_bias = din("dt_bias", [16, 1])
    a_log = din("a_log", [16, 1])
    d_skip = din("d_skip", [1, 16])
    g_ssm = din("g_ssm", [1, D])
    w_out = din("w_out_ab", [2048, D])
    w_pool = din("w_pool", [4, 256, 256])
    pool_scale = din("pool_scale", [1, D])
    w_gate = din("w_gate", [2, D, FF])
    w_up = din("w_up", [2, D, FF])
    w_down = din("w_down", [2, FF, D])

    y_p = dout("y_prompt", [T, D])
    y_s = dout("y_sample", [SB, D])
    c_p = dout("c_prompt", [4, 256, 256])
    c_s = dout("c_sample", [SB, 4, 256, 256])
    n_p = dout("n_prompt", [8, 128])
    n_s = dout("n_sample", [SB, 1024])
    m_p = dout("m_prompt", [4, 1])
    m_s = dout("m_sample", [SB, 4])
    ssm_p = dout("ssm_prompt", [16 * 64, 128])
    ssm_s = dout("ssm_sample", [SB, 16 * 64, 128])
    conv_p = dout("conv_prompt", [3, 1536])
    conv_s = dout("conv_sample", [SB, 3, 1536])
    pool_p = dout("pool_prompt", [15, D])
    pool_s = dout("pool_sample", [SB, 15, D])

    uT = b.sb("uT", [128, KC, 528], BF16)
    xres = b.sb("xres", [128, 5, D])
    mixT = b.sb("mixT", [128, 16, 528], BF16)
    Fb = b.sb("Fb", [128, 16384], BF16)
    NSLOT = 10
    wst = [b.sb("wst%d" % i, [128, KC, 256], BF16) for i in range(NSLOT)]
    kq4 = b.sb("kq4", [128, 4, 256], BF16)
    GT2 = b.sb("GT2", [128, 2, 128])
    decb2 = b.sb("decb2", [128, 2, 16])
    xw2 = b.sb("xw2", [128, 1024], BF16)
    btok2 = b.sb("btok2", [128, 2, 128], BF16)
    yap = [b.sb("yap0", [128, D]), b.sb("yap1", [128, D])]
    eAm = b.sb("eAm", [128, 4, 128])
    negm4b = b.sb("negm4b", [128, 4, 128], BF16)
    gbc = [b.sb("gbc%d" % i, [128, D]) for i in range(2)]
    xin = b.sb("xin", [128, D])
    ub = b.sb("ub", [128, D], BF16)
    tmpa = b.sb("tmpa", [128, D])
    tmpb = b.sb("tmpb", [128, D])
    Cst = b.sb("Cst", [128, 4, 2, 256])
    Cbf = b.sb("Cbf", [128, 4, 2, 256], BF16)
    nst = b.sb("nst", [128, 8])
    nbf = b.sb("nbf", [128, 8], BF16)
    hT = b.sb("hT", [128, 16, 64])
    hTbf = b.sb("hTbf", [128, 16, 64], BF16)
    ident32 = b.sb("ident32", [128, 128])
    identb = b.sb("identb", [128, 128], BF16)
    mask32 = b.sb("mask32", [128, 128])
    sel127 = b.sb("sel127", [128, 128])
    ones32 = b.sb("ones32", [128, 512])
    negm4 = b.sb("negm4", [128, 4, 128])
    negones = b.sb("negones", [16, 128])
    onesb = b.sb("onesb", [128, 1], BF16)
    rows = [b.sb("rows%d" % i, [128, 512]) for i in range(6)]
    sm = b.sb("sm", [128, 96])
    GT = b.sb("GT", [128, 128])
    kp = b.sb("kp", [128, 256], BF16)
    kp2 = b.sb("kp2", [128, 256], BF16)
    spb = b.sb("spb", [128, 4, 128], BF16)
    hf = b.sb("hf", [128, D], BF16)
    decb = b.sb("decb", [128, 16])
    carry = b.sb("carry", [128, 8])
    convtail = b.sb("convtail", [128, 12, 3])
    pooltail = b.sb("pooltail", [128, KC, 15], BF16)
    invcnt = b.sb("invcnt", [128, 4, 16])
    gpar = b.sb("gpar", [128, 64])
    gload = b.sb("gload", [64, 128])
    smallp = b.sb("smallp", [16, 8])
    dskb = b.sb("dskb", [128, 16])
    wpl = b.sb("wpl", [128, 4, 2, 256], BF16)
    u1f = b.sb("u1f", [128, D])

    pb = [nc.alloc_psum_tensor("pb%d" % i, [128, 512], F32) for i in range(8)]

    def pbf(i):
        return pb[i][:, :].bitcast(BF16)

    qT = Fb[:, 0:4096].rearrange("p (c n) -> p c n", c=8)
    kT = Fb[:, 4096:8192].rearrange("p (c n) -> p c n", c=8)
    vtok = Fb[:, 8192:12288].rearrange("p (i n) -> p i n", i=4)
    gsig = Fb[:, 12288:16384].rearrange("p (i n) -> p i n", i=4)
    zs = Fb[:, 0:4096].rearrange("p (i n) -> p i n", i=4)
    xbcT = Fb[:, 4096:10240].rearrange("p (c n) -> p c n", c=12)
    WT = Fb[:, 10240:12288].rearrange("p (h n) -> p h n", h=16)
    xtok = Fb[:, 12288:13312]
    xdt = Fb[:, 13312:14336]
    xw = Fb[:, 14336:15360]
    btok = Fb[:, 15360:15616].rearrange("p (g n) -> p g n", g=2)
    E4 = Fb[:, 15616:16128]
    hid = Fb[:, 0:11616].rearrange("p (c n) -> p c n", c=FC)
    sproj = Fb[:, 0:13360].bitcast(F32)

    b.memset(ident32[:, :], 1.0, 'pool')
    b.asel(ident32[:, :], [[-1, 128]], 1, 0, ALU.is_equal)
    b.cp(identb[:, :], ident32[:, :])
    b.memset(mask32[:, :], 1.0, 'pool')
    b.asel(mask32[:, :], [[1, 128]], -1, 0, ALU.is_ge)
    b.memset(sel127[:, :], 1.0, 'pool')
    b.asel(sel127[:, :], [[0, 128]], 1, -127, ALU.is_equal)
    b.memset(ones32[:, :], 1.0)
    b.memset(negm4[:, :, :], 0.0, 'pool')
    b.asel(negm4[:, :, :], [[0, 4], [1, 128]], -1, 0, ALU.is_ge, fill=-30000.0)
    b.memset(negones[:, :], -1.0)
    b.cp(negm4b[:, :, :], negm4[:, :, :])
    b.memset(onesb[:, :], 1.0)
    b.memset(Cst[:, :, :, :], 0.0)
    b.memset(Cbf[:, :, :, :], 0.0)
    b.memset(nst[:, :], 0.0)
    b.memset(nbf[:, :], 0.0)
    b.memset(hT[:, :, :], 0.0)
    b.memset(hTbf[:, :, :], 0.0)
    b.memset(carry[:, :], 0.0)
    b.memset(convtail[:, :, :], 0.0)
    b.memset(pooltail[:, :, :], 0.0)
    b.memset(GT[:, :], 0.0)
    for r in rows:
        b.memset(r[:, :], 0.0)
    for wi, w in enumerate((2, 4, 8, 16)):
        b.memset(invcnt[:, wi, :], 1.0 / w)
        for t_ in range(w - 1):
            b.memset(invcnt[:, wi, t_:t_ + 1], 1.0 / (t_ + 1))
    b.memset(gload[:, :], 0.0)
    b.dma(gload[0:48, :], conv_w.rearrange("j (c p) -> (j c) p", p=128))
    b.dma(gload[48:60, :], conv_b.rearrange("o (c p) -> (o c) p", p=128))
    b.tr(pb[7][:, 0:64], gload[0:64, :], ident32[0:64, 0:64])
    b.cp(gpar[:, :], pb[7][:, 0:64])
    b.dma_nc(smallp[0:4, 0:1], b_ig)
    b.dma_nc(smallp[0:4, 1:2], b_fg)
    b.dma_nc(smallp[0:16, 2:3], dt_bias)
    b.dma_nc(smallp[0:16, 3:4], a_log)
    b.ts(smallp[0:4, 0:2], smallp[0:4, 0:2], 1.0 / 15.0)
    b.act(smallp[0:16, 4:5], smallp[0:16, 3:4], AF.Exp)
    b.dma(dskb[:, :], d_skip.broadcast_to([128, 16]))
    for g in range(4):
        b.dma(wpl[:, g, :, :], w_pool[g].rearrange("(kc p) e -> p kc e", p=128), q='pool')
    b.dma(gbc[0][:, :], pool_scale[0:1, :].broadcast_to([128, D]))
    for g in range(4):
        b.tt(wpl[:, g, :, :], wpl[:, g, :, :], gbc[0][:, g * 256:(g + 1) * 256].unsqueeze(1).broadcast_to([128, 2, 256]), ALU.mult)

    gslot = [0]

    def load_g(src_row):
        t_ = gbc[gslot[0] % 2]
        gslot[0] += 1
        b.dma(t_[:, :], src_row.broadcast_to([128, D]))
        return t_

    wslot = [0]

    NPC = 104
    wscr = nc.dram_tensor("wscr", [NPC, 128, 2048], BF16, kind="Internal").ap()
    wcache = {}

    def fetch_piece(key, slot_view, src, scr_view_fn):
        if key in wcache:
            b.dma(slot_view, scr_view_fn(wcache[key]), q='sp')
        else:
            idx = len(wcache)
            assert idx < NPC
            wcache[key] = idx
            b.dma(slot_view, src, q='pool')
            b.dma(scr_view_fn(idx), slot_view, q='sp')

    def load_w(src):
        t_ = wst[wslot[0] % NSLOT]
        wslot[0] += 1
        n = src.shape[2]
        key = (src.tensor.name, src.offset, n)
        fetch_piece(key, t_[:, :, 0:n], src, lambda idx: wscr[idx].rearrange("p (k c) -> p k c", k=KC)[:, :, 0:n])
        return t_

    def rstd_from_ss(dst, ss, n, rows_):
        b.act(dst, ss, AF.Ln, scale=1.0 / n, bias=EPS)
        b.act(dst, dst, AF.Exp, scale=-0.5)

    def norm_to_T(src, R, g_t, dstT, col0, keep32=None):
        b.act(tmpa[0:R, :], src, AF.Square, accum=sm[0:R, 0:1])
        rstd_from_ss(sm[0:R, 1:2], sm[0:R, 0:1], D, R)
        if keep32 is not None:
            b.stt(keep32[0:R, :], src, sm[0:R, 1:2], g_t[0:R, :], ALU.mult, ALU.mult)
            b.cp(ub[0:R, :], keep32[0:R, :])
        else:
            b.stt(ub[0:R, :], src, sm[0:R, 1:2], g_t[0:R, :], ALU.mult, ALU.mult)
        pt = pbf(2)
        for c in range(KC):
            b.tr(pt[:, c * 128:c * 128 + R], ub[0:R, c * 128:(c + 1) * 128], identb[0:R, 0:R])
        if R == 128:
            b.cpa(dstT[:, :, col0:col0 + R], pt[:, 0:1024].rearrange("p (c n) -> p c n", c=KC))
        else:
            b.cpa(dstT[:, :, col0:col0 + R], pt[:, 0:1024].rearrange("p (c n) -> p c n", c=KC)[:, :, 0:R])

    def tok_to_T(src_bf, R, dstT, c0, nch, col0, bank=2):
        pt = pbf(bank)
        for c in range(nch):
            b.tr(pt[:, c * 128:c * 128 + R], src_bf[0:R, c * 128:(c + 1) * 128], identb[0:R, 0:R])
        b.cpa(dstT[:, c0:c0 + nch, col0:col0 + R], pt[:, 0:nch * 128].rearrange("p (c n) -> p c n", c=nch)[:, :, 0:R])

    pj = [0]

    def pjbank():
        pj[0] += 1
        return pb[pj[0] % 4]

    def post_norm_residual(ps2, R, g_t, xsrc, xdst):
        b.act(tmpa[0:R, 0:512], ps2[0][0:R, :], AF.Square, accum=sm[0:R, 2:3])
        b.act(tmpa[0:R, 512:1024], ps2[1][0:R, :], AF.Square, accum=sm[0:R, 3:4])
        b.tt(sm[0:R, 4:5], sm[0:R, 2:3], sm[0:R, 3:4], ALU.add)
        rstd_from_ss(sm[0:R, 5:6], sm[0:R, 4:5], D, R)
        for j in range(2):
            b.stt(tmpb[0:R, j * 512:(j + 1) * 512], ps2[j][0:R, :], sm[0:R, 5:6], g_t[0:R, j * 512:(j + 1) * 512], ALU.mult, ALU.mult)
        b.tt(xdst, tmpb[0:R, :], xsrc, ALU.add)


    smn = b.sb("smn", [128, 48])

    def post_staged(grp, banks, g_t, xadd):
        for n, (i, R) in enumerate(grp):
            b.act(tmpa[0:R, 0:512], banks[n][0][0:R, :], AF.Square, accum=smn[0:R, 2 * n:2 * n + 1])
            b.act(tmpa[0:R, 512:1024], banks[n][1][0:R, :], AF.Square, accum=smn[0:R, 2 * n + 1:2 * n + 2])
        ng = len(grp)
        b.P.add('dve', lambda e: e.tensor_reduce(smn[:, 16:16 + ng], smn[:, 0:2 * ng].rearrange("p (n t) -> p n t", t=2), axis=AX.X, op=ALU.add),
                outs=[smn[:, 16:16 + ng]], ins=[smn[:, 0:2 * ng]])
        rstd_from_ss(smn[:, 24:24 + ng], smn[:, 16:16 + ng], D, 128)
        for n, (i, R) in enumerate(grp):
            src = xadd(i, R)
            if src is not None:
                for j in range(2):
                    b.stt(xres[0:R, i, j * 512:(j + 1) * 512], banks[n][j][0:R, :], smn[0:R, 24 + n:25 + n], g_t[0:R, j * 512:(j + 1) * 512], ALU.mult, ALU.mult)
                xo = xres[0:R, i, :]
                b.P.add('pool', lambda e, xo=xo, src=src: e.dma_start(out=xo, in_=src, accum_op=ALU.add), outs=[xo], ins=[xo, src], dma=True)
            else:
                tmp = tmpb if n % 2 == 0 else tmpa
                for j in range(2):
                    b.stt(tmp[0:R, j * 512:(j + 1) * 512], banks[n][j][0:R, :], smn[0:R, 24 + n:25 + n], g_t[0:R, j * 512:(j + 1) * 512], ALU.mult, ALU.mult)
                b.tt(xres[0:R, i, :], tmp[0:R, :], xres[0:R, i, :], ALU.add)

    def prenorm_staged(tl, g_t, dstT, col0_fn, ubs, keep32_fn=None):
        nt = len(tl)
        for n, (i, R) in enumerate(tl):
            b.act(tmpa[0:R, :], xres[0:R, i, :], AF.Square, accum=smn[0:R, 32 + n:33 + n])
        rstd_from_ss(smn[:, 40:40 + nt], smn[:, 32:32 + nt], D, 128)
        for n, (i, R) in enumerate(tl):
            k32 = keep32_fn(i, R) if keep32_fn else None
            if k32 is not None:
                b.stt(k32[0:R, :], xres[0:R, i, :], smn[0:R, 40 + n:41 + n], g_t[0:R, :], ALU.mult, ALU.mult)
                b.cp(ubs[n][0:R, :], k32[0:R, :], eng='act')
            else:
                b.stt(ubs[n][0:R, :], xres[0:R, i, :], smn[0:R, 40 + n:41 + n], g_t[0:R, :], ALU.mult, ALU.mult)
        for n, (i, R) in enumerate(tl):
            pt = pbf(2 + n % 2)
            for c in range(KC):
                b.tr(pt[:, c * 128:c * 128 + R], ubs[n][0:R, c * 128:(c + 1) * 128], identb[0:R, 0:R])
            b.cpa(dstT[:, :, col0_fn(i):col0_fn(i) + R], pt[:, 0:1024].rearrange("p (c n) -> p c n", c=KC)[:, :, 0:R])

    curblk = [0]

    def wview(w2d, c0, n):
        return w2d[:, c0:c0 + n].rearrange("(kc p) n -> p kc n", p=128)

    def load_w2(w2d, k2):
        t_ = wst[wslot[0] % NSLOT]
        wslot[0] += 1
        v = t_[:, :, :].rearrange("p a n -> p (a n)").rearrange("p (kc n) -> p kc n", kc=2)
        src = w2d[k2 * 128:(k2 + 2) * 128, :].rearrange("(kc p) n -> p kc n", p=128)
        key = (src.tensor.name, src.offset, 'w2')
        fetch_piece(key, v, src, lambda idx: wscr[idx].rearrange("p (kc n) -> p kc n", kc=2))
        return v

    def stream_proj(actT, nk, w2d, tiles, post):
        groups = [tiles[0:4]] + ([tiles[4:]] if len(tiles) > 4 else [])
        for grp in groups:
            for k2 in range(0, nk, 2):
                wv = load_w2(w2d, k2)
                for kk in range(2):
                    k = k2 + kk
                    for gi, (i, R) in enumerate(grp):
                        for j in range(2):
                            b.mm(pb[2 * gi + j][0:R, :], actT[:, k, i * 128:i * 128 + R], wv[:, kk, j * 512:(j + 1) * 512], start=(k == 0), stop=(k == nk - 1))
            post(grp, [(pb[2 * gi], pb[2 * gi + 1]) for gi in range(len(grp))])

    def ffn(layer, tiles, last):
        b.P.tag = 'b%d.ffn%d.gateup' % (curblk[0], layer)
        g_t = load_g(g_ffn_pre[layer:layer + 1, :])
        ubs_f = [Fb[:, 11616 + n * 1024:11616 + (n + 1) * 1024] for n in range(4)] + [ub]
        prenorm_staged(tiles, g_t, uT, lambda i: i * 128, ubs_f)
        for j in range(11):
            wg = load_w(wview(w_gate[layer], j * 256, 256))
            wu = load_w(wview(w_up[layer], j * 256, 256))
            for m in range(2):
                nset = 3 if last else 4
                st_ = (2 * j + m) % nset
                pg, pu = ((pb[0], pb[1]), (pb[2], pb[3]), (pb[4], pb[5]), (pb[6], pb[7]))[st_]
                tmp = (tmpa[:, 0:512], tmpa[:, 512:1024], tmpb[:, 0:512], tmpb[:, 512:1024])[st_]
                for k in range(KC):
                    b.mm(pg[:, :], wg[:, k, m * 128:(m + 1) * 128], uT[:, k, 0:512], start=(k == 0), stop=(k == KC - 1))
                for k in range(KC):
                    b.mm(pu[:, :], wu[:, k, m * 128:(m + 1) * 128], uT[:, k, 0:512], start=(k == 0), stop=(k == KC - 1))
                b.act(tmp, pg[:, :], AF.Silu)
                b.tt(hid[:, 2 * j + m, 0:512], tmp, pu[:, :], ALU.mult)
                if last:
                    ps_ = pb[6] if (2 * j + m) % 2 == 0 else pb[7]
                    tmps = ub[:, 0:16] if (2 * j + m) % 2 == 0 else ub[:, 16:32]
                    for k in range(KC):
                        b.mm(ps_[:, 0:16], wg[:, k, m * 128:(m + 1) * 128], uT[:, k, 512:528], start=(k == 0), stop=(k == KC - 1))
                    for k in range(KC):
                        b.mm(ps_[:, 16:32], wu[:, k, m * 128:(m + 1) * 128], uT[:, k, 512:528], start=(k == 0), stop=(k == KC - 1))
                    b.act(tmps, ps_[:, 0:16], AF.Silu)
                    b.tt(hid[:, 2 * j + m, 512:528], tmps, ps_[:, 16:32], ALU.mult)
        b.P.tag = 'b%d.ffn%d.down' % (curblk[0], layer)
        g_p = load_g(g_ffn_post[layer:layer + 1, :])
        stream_proj(hid, FC, w_down[layer], tiles, lambda grp, banks: post_staged(grp, banks, g_p, lambda i, R: None))

    pext = Fb[:, 0:4352].rearrange("p (c n) -> p c n", c=8)
    dT = Fb[:, 4352:8576].rearrange("p (c n) -> p c n", c=8)
    hist = Fb[:, 8576:16256].bitcast(F32)

    def pool_mixer(blk, tiles, last):
        b.P.tag = 'b%d.pool' % curblk[0]
        g_t = load_g(g_mix_pre[1:2, :])
        b.cp(pext[:, :, 0:15], pooltail[:, :, :])
        ubs_p = [Fb[:, 8576 + n * 1024:8576 + (n + 1) * 1024] for n in range(4)] + [ub]

        def k32f(i, R):
            if R != 128:
                return xin
            if last and i == TPB - 1:
                return u1f
            return None
        prenorm_staged(tiles, g_t, pext, lambda i: 15 + i * 128, ubs_p, keep32_fn=k32f)
        b.cp(pooltail[:, :, :], pext[:, :, 512:527])
        if last:
            b.dma(pool_p[:, :], u1f[113:128, :], q='act')
            b.dma(pool_s[:, 0:14, :], st_pool[:, 1:15, :], q='act')
            b.dma(pool_s[:, 14, :], xin[0:SB, :], q='act')
        for c in range(8):
            wi = c // 2
            w = 2 << wi
            src, dst = tmpa, tmpb
            b.cp(src[:, 0:527], pext[:, c, 0:527], eng='act')
            for k in range(wi + 1):
                sh = 1 << k
                b.tt(dst[:, sh:527], src[:, sh:527], src[:, 0:527 - sh], ALU.add)
                src, dst = dst, src
            b.stt(dT[:, c, 0:512], src[:, 15:527], 1.0 / w, pext[:, c, 15:527], ALU.mult, ALU.subtract)
            if blk == 0:
                b.tt(dst[:, 0:16], src[:, 15:31], invcnt[:, wi, :], ALU.mult)
                b.tt(dT[:, c, 0:16], dst[:, 0:16], pext[:, c, 15:31], ALU.subtract)
        if last:
            dsm = tmpa
            for wi in range(4):
                w = 2 << wi
                gc = slice(wi * 256, (wi + 1) * 256)
                hv = hist[0:SB, 0:(w - 1) * 256].rearrange("p (j n) -> p j n", j=w - 1)
                b.dma(hv, st_pool[:, 15 - (w - 1):15, gc])
                b.P.add('dve', lambda e, hv=hv, gc=gc: e.tensor_reduce(dsm[0:SB, gc], hv.rearrange("p j n -> p n j"), axis=AX.X, op=ALU.add),
                        outs=[dsm[0:SB, gc]], ins=[hv])
                b.tt(dsm[0:SB, gc], dsm[0:SB, gc], xin[0:SB, gc], ALU.add)
                b.stt(dsm[0:SB, gc], dsm[0:SB, gc], 1.0 / w, xin[0:SB, gc], ALU.mult, ALU.subtract)
            b.cp(ub[0:SB, :], dsm[0:SB, :])
            tok_to_T(ub, SB, dT, 0, 8, 512, bank=2)
        gp_t = load_g(g_mix_post[1:2, :])
        banks = []
        for n, (i, R) in enumerate(tiles[0:4]):
            pp = (pb[2 * n], pb[2 * n + 1])
            banks.append(pp)
            for g in range(4):
                for kc in range(2):
                    b.mm(pp[g // 2][0:R, (g % 2) * 256:(g % 2 + 1) * 256], dT[:, g * 2 + kc, i * 128:i * 128 + R], wpl[:, g, kc, :], start=(kc == 0), stop=(kc == 1))
        post_staged(tiles[0:4], banks, gp_t, lambda i, R: None)
        if len(tiles) > 4:
            (i, R) = tiles[4]
            pp = (pb[0], pb[1])
            for g in range(4):
                for kc in range(2):
                    b.mm(pp[g // 2][0:R, (g % 2) * 256:(g % 2 + 1) * 256], dT[:, g * 2 + kc, i * 128:i * 128 + R], wpl[:, g, kc, :], start=(kc == 0), stop=(kc == 1))
            post_staged(tiles[4:], [pp], gp_t, lambda i, R: None)

    def sample_mixers():
        S = SB
        xf = xres[0:S, :, :].rearrange("p i n -> p (i n)")
        sp_ = sproj[0:S, :]
        onehot3 = rows[4][:, 0:256].rearrange("p (a c) -> p a c", a=16)
        qTs = rows[4][:, 256:384].rearrange("p (a c) -> p a c", a=8)
        qTm = rows[4][:, 384:512].rearrange("p (a c) -> p a c", a=8)
        decS = rows[5][:, 0:64]
        EAc = rows[5][:, 64:192]
        CTs = rows[5][:, 192:224].rearrange("p (a c) -> p a c", a=2)
        s2 = rows[3]
        cbuf = [rows[0], rows[1], rows[2]]
        for (pc0, pcn) in ((OQ, 1024), (OK_, 1024), (OV, 1024), (OO, 1024), (OIG, 8), (OZ, 1024), (OXBC, 1536), (ODT, 16)):
            for gg in range(0, pcn, 256):
                g0 = pc0 + gg
                n = min(256, pcn - gg)
                wt = load_w(w_in[:, g0:g0 + n].rearrange("(kc p) n -> p kc n", p=128))
                ps = pjbank()
                for k in range(KC):
                    b.mm(ps[0:S, 0:n], uT[:, k, 512:528], wt[:, k, 0:n], start=(k == 0), stop=(k == KC - 1))
                b.cpa(sp_[:, g0:g0 + n], ps[0:S, 0:n])
        b.memset(onehot3, 0.0)
        for j in range(16):
            b.memset(onehot3[:, j, j:j + 1], 1.0)
        q_ = sp_[:, OQ:OQ + 1024]
        k_ = sp_[:, OK_:OK_ + 1024]
        v_ = sp_[:, OV:OV + 1024]
        o_ = sp_[:, OO:OO + 1024]
        b.P.tag = 's.gates'
        b.dma(s2[0:S, 0:4], b_ig.rearrange("h o -> o h").broadcast_to([S, 4]))
        b.dma(s2[0:S, 4:8], b_fg.rearrange("h o -> o h").broadcast_to([S, 4]))
        b.dma(s2[0:S, 8:12], st_m[:, :])
        b.tt(s2[0:S, 12:20], sp_[:, OIG:OIG + 8], s2[0:S, 0:8], ALU.add)
        b.act(s2[0:S, 12:20], s2[0:S, 12:20], AF.Tanh, scale=1.0 / 15.0)
        li = s2[0:S, 20:24]
        b.ts(li, s2[0:S, 12:16], 15.0)
        b.act(s2[0:S, 24:28], s2[0:S, 16:20], AF.Exp, scale=-15.0)
        b.act(s2[0:S, 24:28], s2[0:S, 24:28], AF.Ln, bias=1.0)
        lfm = s2[0:S, 28:32]
        b.tt(lfm, s2[0:S, 8:12], s2[0:S, 24:28], ALU.subtract)
        mt = s2[0:S, 32:36]
        b.tt(mt, lfm, li, ALU.max)
        b.dma(m_s[:, :], mt, q='act')
        a1 = s2[0:S, 36:40]
        a2 = s2[0:S, 40:44]
        emt = s2[0:S, 44:48]
        b.tt(a1, li, mt, ALU.subtract)
        b.act(a1, a1, AF.Exp)
        b.tt(a2, lfm, mt, ALU.subtract)
        b.act(a2, a2, AF.Exp)
        b.act(emt, mt, AF.Exp, scale=-1.0)
        b.ts(q_, q_, 0.0625)
        nsv = xf[:, 0:1024]
        t1 = xf[:, 1024:2048]
        t2 = xf[:, 2048:3072]
        hh = xf[:, 3072:4096]
        b.dma(nsv, st_n[:, :])
        b.tt(t1, q_, k_, ALU.mult)
        qk = s2[0:S, 48:52]
        b.P.add('dve', lambda e: e.tensor_reduce(qk, t1.rearrange("p (h d) -> p h d", h=4), axis=AX.X, op=ALU.add), outs=[qk], ins=[t1])
        b.tt(t1, q_, nsv, ALU.mult)
        qn = s2[0:S, 52:56]
        b.P.add('dve', lambda e: e.tensor_reduce(qn, t1.rearrange("p (h d) -> p h d", h=4), axis=AX.X, op=ALU.add), outs=[qn], ins=[t1])
        sv = s2[0:S, 56:60]
        b.tt(sv, qk, a1, ALU.mult)
        den = s2[0:S, 60:64]
        b.tt(den, qn, a2, ALU.mult)
        b.tt(den, den, sv, ALU.add)
        b.tt(t1.rearrange("p (h d) -> p h d", h=4), k_.rearrange("p (h d) -> p h d", h=4), a1.unsqueeze(2).broadcast_to([S, 4, 256]), ALU.mult)
        b.tt(t2.rearrange("p (h d) -> p h d", h=4), nsv.rearrange("p (h d) -> p h d", h=4), a2.unsqueeze(2).broadcast_to([S, 4, 256]), ALU.mult)
        b.tt(t2, t2, t1, ALU.add)
        b.dma(n_s[:, :], t2, q='act')
        for c in range(8):
            b.tr(pb[7][:, c * 16:(c + 1) * 16], q_[:, c * 128:(c + 1) * 128], ident32[0:S, 0:S])
        b.cp(qTs, pb[7][:, 0:128].rearrange("p (a c) -> p a c", a=8))
        Rd = s2[0:S, 64:128].rearrange("p (a c) -> p a c", a=16)
        b.tt(Rd, a2.unsqueeze(1).broadcast_to([S, 16, 4]), ident32[0:S, 0:16].unsqueeze(2).broadcast_to([S, 16, 4]), ALU.mult)
        b.mm(pb[7][:, 128:192], ones32[0:S, 0:128], s2[0:S, 64:128])
        b.cp(decS, pb[7][:, 128:192])
        b.P.tag = 's.cloop'
        qTmAll = Cst[:, :, :, :].rearrange("p h c v -> p (h c v)")[:, 0:1024].bitcast(BF16).rearrange("p (a bq j) -> p a bq j", a=8, bq=16)
        cbb = [hf, Fb[:, 13360:14384]]
        for a_ in range(8):
            b.tt(qTmAll[:, a_, :, :], qTs[:, a_:a_ + 1, :].broadcast_to([128, 16, 16]), onehot3, ALU.mult)
        kwb = [s2[0:S, 128:256].bitcast(BF16), s2[0:S, 256:384].bitcast(BF16)]
        vbf = ub[0:S, :]
        b.cp(vbf, v_)
        cb4 = [tmpa, tmpb, xin, u1f]
        its = [(bb, hp) for hp in range(2) for bb in range(S)]
        DEPTH = 3

        def c_load(n_):
            bb, hp = its[n_]
            cbv = cb4[n_ % 4][:, :].rearrange("p (hh c v) -> p hh c v", hh=2, c=2)
            b.dma(cbv, st_c[bb, 2 * hp:2 * hp + 2].rearrange("hh (c p) v -> p hh c v", p=128))

        def c_compute(n_):
            bb, hp = its[n_]
            cb = cb4[n_ % 4]
            cbv = cb[:, :].rearrange("p (hh c v) -> p hh c v", hh=2, c=2)
            pso = (pb[6], pb[7]) if n_ % 2 == 0 else (pb[2], pb[3])
            cq = cbb[n_ % 2]
            b.cp(cq[:, :], cb[:, :], eng='act')
            cqv = cq[:, :].rearrange("p (hh c v) -> p hh c v", hh=2, c=2)
            for hh in range(2):
                h = 2 * hp + hh
                qcp = pb[4 + hh][0:S, hp * 256:(hp + 1) * 256]
                for c in range(2):
                    b.mm(qcp, qTmAll[:, 2 * h + c, bb, :], cqv[:, hh, c, :], start=(bb == 0 and c == 0), stop=(bb == S - 1 and c == 1))
                kw_ = kwb[hh]
                b.ts(kw_, t1[:, h * 256:(h + 1) * 256], ident32[0:S, bb:bb + 1], None, op0=ALU.mult)
                for c in range(2):
                    b.mm(pso[hh][:, c * 256:(c + 1) * 256], kw_[:, c * 128:(c + 1) * 128], vbf[:, h * 256:(h + 1) * 256])
                b.stt(cb[:, hh * 512:(hh + 1) * 512], cb[:, hh * 512:(hh + 1) * 512], decS[:, bb * 4 + h:bb * 4 + h + 1], pso[hh][:, :], ALU.mult, ALU.add)
            b.dma(c_s[bb, 2 * hp:2 * hp + 2].rearrange("hh (c p) v -> p hh c v", p=128), cbv, q='pool')

        for n_ in range(len(its) + DEPTH):
            if n_ < len(its):
                c_load(n_)
            if n_ >= DEPTH:
                c_compute(n_ - DEPTH)
        b.P.tag = 's.hfin'
        b.act(den, den, AF.Abs)
        b.tt(den, den, emt, ALU.max)
        rr_ = s2[0:S, 384:388]
        b.recip(rr_, den)
        b.tt(t2.rearrange("p (h d) -> p h d", h=4), v_.rearrange("p (h d) -> p h d", h=4), sv.unsqueeze(2).broadcast_to([S, 4, 256]), ALU.mult)
        for h in range(4):
            qcp = pb[4 + h % 2][0:S, (h // 2) * 256:(h // 2 + 1) * 256]
            hs_ = slice(h * 256, (h + 1) * 256)
            b.stt(hh[:, hs_], qcp, a2[:, h:h + 1], t2[:, hs_], ALU.mult, ALU.add)
            b.ts(hh[:, hs_], hh[:, hs_], rr_[:, h:h + 1], None, op0=ALU.mult)
            b.act(t2[:, hs_], hh[:, hs_], AF.Square, accum=s2[0:S, 388 + h:389 + h])
        rstd_from_ss(s2[0:S, 392:396], s2[0:S, 388:392], 256, S)
        b.act(t2, o_, AF.Sigmoid)
        gm2 = load_g(g_mlstm[0:1, :])
        b.tt(t2, t2, gm2[0:S, :], ALU.mult)
        for h in range(4):
            hs_ = slice(h * 256, (h + 1) * 256)
            b.stt(ub[0:S, hs_], hh[:, hs_], s2[0:S, 392 + h:393 + h], t2[:, hs_], ALU.mult, ALU.mult)
        tok_to_T(ub, S, mixT, 0, 8, 512, bank=2)

        b.P.tag = 'sample_ssd'
        z_ = sp_[:, OZ:OZ + 1024]
        xb = sp_[:, OXBC:OXBC + 1536]
        dtr = sp_[:, ODT:ODT + 16]
        b.dma(conv_s[:, 0:2, :], st_conv[:, 1:3, :], q='act')
        b.dma(conv_s[:, 2, :], xb, q='act')
        scv = xf[:, 0:4608].rearrange("p (j n) -> p j n", j=3)
        b.dma(scv, st_conv[:, :, :])
        cwb = uT[0:S, :, :].rearrange("p c n -> p (c n)").bitcast(F32)[:, 0:1536]
        b.dma(cwb, conv_w[3:4, :].broadcast_to([S, 1536]))
        b.tt(xb, xb, cwb, ALU.mult)
        for j in range(3):
            b.dma(cwb, conv_w[j:j + 1, :].broadcast_to([S, 1536]))
            b.tt(scv[:, j, :], scv[:, j, :], cwb, ALU.mult)
            b.tt(xb, xb, scv[:, j, :], ALU.add)
        b.dma(cwb, conv_b[0:1, :].broadcast_to([S, 1536]))
        b.tt(xb, xb, cwb, ALU.add)
        b.act(xb, xb, AF.Silu)
        xs_ = xb[:, 0:1024]
        Bm = xb[:, 1024:1280]
        Cm = xb[:, 1280:1536]
        b.dma(s2[0:S, 400:416], dt_bias.rearrange("h o -> o h").broadcast_to([S, 16]))
        b.dma(s2[0:S, 416:432], a_log.rearrange("h o -> o h").broadcast_to([S, 16]))
        dts = s2[0:S, 432:448]
        b.tt(dts, dtr, s2[0:S, 400:416], ALU.add)
        b.act(dts, dts, AF.Exp)
        b.act(dts, dts, AF.Ln, bias=1.0)
        eas = s2[0:S, 448:464]
        b.act(eas, s2[0:S, 416:432], AF.Exp)
        b.tt(eas, eas, dts, ALU.mult)
        b.act(eas, eas, AF.Exp, scale=-1.0)
        b.tt(t1[:, 0:256], Bm, Cm, ALU.mult)
        cbs = s2[0:S, 464:466]
        b.P.add('dve', lambda e: e.tensor_reduce(cbs, t1[:, 0:256].rearrange("p (g n) -> p g n", g=2), axis=AX.X, op=ALU.add), outs=[cbs], ins=[t1[:, 0:256]])
        cf = s2[0:S, 466:482]
        b.tt(cf.rearrange("p (g e) -> p g e", g=2), dts.rearrange("p (g e) -> p g e", g=2), cbs.unsqueeze(2).broadcast_to([S, 2, 8]), ALU.mult)
        b.tt(cf, cf, dskb[0:S, :], ALU.add)
        yac = xf[:, 4096:5120]
        b.tt(yac.rearrange("p (h q) -> p h q", h=16), xs_.rearrange("p (h q) -> p h q", h=16), cf.unsqueeze(2).broadcast_to([S, 16, 64]), ALU.mult)
        b.tt(t2.rearrange("p (h q) -> p h q", h=16), xs_.rearrange("p (h q) -> p h q", h=16), dts.unsqueeze(2).broadcast_to([S, 16, 64]), ALU.mult)
        for g in range(2):
            b.tr(pb[7][:, 192 + g * 16:192 + (g + 1) * 16], Cm[:, g * 128:(g + 1) * 128], ident32[0:S, 0:S])
        b.cp(CTs, pb[7][:, 192:224].rearrange("p (a c) -> p a c", a=2))
        for h2 in range(2):
            Re = s2[0:S, 128:256].rearrange("p (a c) -> p a c", a=16)
            b.tt(Re, eas.rearrange("p (j t) -> p j t", t=2)[:, :, h2].unsqueeze(1).broadcast_to([S, 16, 8]),
                 ident32[0:S, 0:16].unsqueeze(2).broadcast_to([S, 16, 8]), ALU.mult)
            b.mm(pb[7][h2 * 64:(h2 + 1) * 64, 256:384], ones32[0:S, 0:64], s2[0:S, 128:256])
        b.cp(EAc, pb[7][:, 256:384])
        yacc = hh
        b.memset(yacc, 0.0)
        b.P.tag = 's.ssdloop'
        h0b = [tmpa[:, :], xin[:, :], Fb[:, 0:2048].bitcast(F32)]
        hTb = [tmpb[:, :].bitcast(BF16), u1f[:, :].bitcast(BF16), Fb[:, 2048:4096]]
        xdmb = [t1.bitcast(BF16)[:, 0:1024], xf[:, 0:1024].bitcast(BF16)[:, 0:1024]]
        Bmb = kp[0:S, :]
        b.cp(Bmb, Bm)
        CTb = kp2[:, 0:32].rearrange("p (a c) -> p a c", a=2)
        b.cp(CTb, CTs)

        def s_load(bb):
            h0 = h0b[bb % 3].rearrange("p (j n) -> p j n", j=8)
            b.dma(h0, st_ssm[bb].rearrange("(j t) p n -> (t p) j n", t=2))

        def s_compute(bb):
            h0f = h0b[bb % 3]
            h0 = h0f.rearrange("p (j n) -> p j n", j=8)
            hb_ = hTb[bb % 3][:, 0:1024]
            hTs = hTb[bb % 3][:, 1024:2048]
            xdm = xdmb[bb % 2]
            b.cp(hb_, h0f, eng='act')
            ptq = pbf(0)
            for j in range(8):
                b.tr(ptq[:, j * 128:(j + 1) * 128], hb_[:, j * 128:(j + 1) * 128], identb[:, :])
            b.cp(hTs, ptq[:, 0:1024])
            for g in range(2):
                b.mm(pb[2 + g][0:S, :], CTb[:, g, :], hTs[:, g * 512:(g + 1) * 512])
                b.stt(yacc[:, g * 512:(g + 1) * 512], pb[2 + g][0:S, :], ident32[0:S, bb:bb + 1], yacc[:, g * 512:(g + 1) * 512], ALU.mult, ALU.add)
            b.ts(xdm, t2, ident32[0:S, bb:bb + 1], None, op0=ALU.mult)
            for j in range(8):
                g = j // 4
                b.mm(pb[4 + (bb % 2) * 2 + j // 4][:, (j % 4) * 128:(j % 4 + 1) * 128], xdm[:, j * 128:(j + 1) * 128], Bmb[:, g * 128:(g + 1) * 128])
            b.tt(h0, h0, EAc[:, bb * 8:(bb + 1) * 8].unsqueeze(2).broadcast_to([128, 8, 128]), ALU.mult, eng='pool')
            for jj in range(2):
                b.tt(h0f[:, jj * 512:(jj + 1) * 512], h0f[:, jj * 512:(jj + 1) * 512], pb[4 + (bb % 2) * 2 + jj][:, :], ALU.add)
            b.dma(ssm_s[bb].rearrange("(j t p) n -> (t p) j n", t=2, p=64), h0, q='pool')

        for n_ in range(S + 2):
            if n_ < S:
                s_load(n_)
            if n_ >= 2:
                s_compute(n_ - 2)
        b.P.tag = 's.ssdpost'
        b.tt(yacc.rearrange("p (h q) -> p h q", h=16), yacc.rearrange("p (h q) -> p h q", h=16), eas.unsqueeze(2).broadcast_to([S, 16, 64]), ALU.mult)
        b.tt(yac, yac, yacc, ALU.add)
        b.act(t2, z_, AF.Silu)
        b.tt(yac, yac, t2, ALU.mult)
        for g in range(2):
            gs = slice(g * 512, (g + 1) * 512)
            b.act(t2[:, gs], yac[:, gs], AF.Square, accum=s2[0:S, 484 + g:485 + g])
        rstd_from_ss(s2[0:S, 486:488], s2[0:S, 484:486], 512, S)
        gs2 = load_g(g_ssm[0:1, :])
        for g in range(2):
            gs = slice(g * 512, (g + 1) * 512)
            b.stt(ub[0:S, gs], yac[:, gs], s2[0:S, 486 + g:487 + g], gs2[0:S, gs], ALU.mult, ALU.mult)
        tok_to_T(ub, S, mixT, 8, 8, 512, bank=2)

    def main_loop():
        for blk in range(NBLK):
            last = blk == NBLK - 1
            tiles = [(i, 128) for i in range(TPB)] + ([(TPB, SB)] if last else [])
            NTOK = 512 + (SB if last else 0)
            t0 = blk * 512

            curblk[0] = blk
            b.P.tag = 'b%d.prenorm' % blk

            g_t = load_g(g_mix_pre[0:1, :])
            for (i, R) in tiles:
                src = xp[t0 + i * 128:t0 + (i + 1) * 128, :] if R == 128 else xs[:, :]
                b.dma(xres[0:R, i, :], src)
            ubs_0 = [Fb[:, n * 1024:(n + 1) * 1024] for n in range(4)] + [ub]
            prenorm_staged(tiles, g_t, uT, lambda i: i * 128, ubs_0)

            def formB(col0, ncols, evac):
                for g0 in range(0, ncols, 256):
                    n = min(256, ncols - g0)
                    wt = load_w(w_in[:, col0 + g0:col0 + g0 + n].rearrange("(kc p) n -> p kc n", p=128))
                    for m0 in range(0, n, 128):
                        mm_ = min(128, n - m0)
                        ps = pjbank()
                        for k in range(KC):
                            b.mm(ps[0:mm_, :], wt[:, k, m0:m0 + mm_], uT[:, k, 0:512], start=(k == 0), stop=(k == KC - 1))
                        evac(ps, (g0 + m0) // 128, mm_)

            def formA(col0, ncols, evac, tl):
                for g0 in range(0, ncols, 256):
                    n = min(256, ncols - g0)
                    wt = load_w(w_in[:, col0 + g0:col0 + g0 + n].rearrange("(kc p) n -> p kc n", p=128))
                    for (i, R) in tl:
                        ps = pjbank()
                        for k in range(KC):
                            b.mm(ps[0:R, 0:n], uT[:, k, i * 128:i * 128 + R], wt[:, k, 0:n], start=(k == 0), stop=(k == KC - 1))
                        evac(ps, i, R, g0, n)

            ptiles = tiles[:TPB]

            b.P.tag = 'b%d.inproj_m' % blk
            formB(OQ, 1024, lambda ps, c, m_: b.act(qT[:, c, :], ps[:, :], AF.Copy, scale=0.0625))
            formB(OK_, 1024, lambda ps, c, m_: b.cpa(kT[:, c, :], ps[:, :]))
            formA(OV, 1024, lambda ps, i, R, g0, n: b.cpa(vtok[:, i, g0:g0 + n], ps[:, 0:n]), ptiles)
            gm = load_g(g_mlstm[0:1, :])

            def ev_o(ps, i, R, g0, n):
                b.act(tmpa[:, 0:n], ps[:, 0:n], AF.Sigmoid)
                b.tt(gsig[:, i, g0:g0 + n], tmpa[:, 0:n], gm[:, g0:g0 + n], ALU.mult)
            formA(OO, 1024, ev_o, ptiles)
            wt = load_w(w_in[:, OIG:OIG + 8].rearrange("(kc p) n -> p kc n", p=128))
            for gi in range(2):
                ps = pjbank()
                for k in range(KC):
                    b.mm(ps[0:4, :], wt[:, k, gi * 4:gi * 4 + 4], uT[:, k, 0:512], start=(k == 0), stop=(k == KC - 1))
                b.cp(rows[gi][0:4, :], ps[0:4, :])

            chk(1 + 10 * blk)
            b.P.tag = 'b%d.gates' % blk
            ti, tf, lnv, Bn, Aa, Gf = rows[0], rows[1], rows[2], rows[3], rows[4], rows[5]
            b.act(ti[0:4, :], rows[0][0:4, :], AF.Tanh, scale=1.0 / 15.0, bias=smallp[0:4, 0:1])
            b.act(tf[0:4, :], rows[1][0:4, :], AF.Tanh, scale=1.0 / 15.0, bias=smallp[0:4, 1:2])
            b.act(lnv[0:4, :], tf[0:4, :], AF.Exp, scale=-15.0)
            b.act(lnv[0:4, :], lnv[0:4, :], AF.Ln, bias=1.0)
            b.scan(Bn[0:4, :], ones32[0:4, :], lnv[0:4, :], carry[0:4, 0:1], ALU.mult, ALU.add)
            b.stt(Aa[0:4, :], ti[0:4, :], 15.0, Bn[0:4, :], ALU.mult, ALU.add)
            Gx = rows[1]
            b.scan(Gx[0:4, :], ones32[0:4, :], Aa[0:4, :], carry[0:4, 1:2], ALU.mult, ALU.max)
            GP = rows[5]
            for (i, R) in ptiles:
                cs = slice(i * 128, (i + 1) * 128)
                gc = carry[0:4, 1:2] if i == 0 else Gx[0:4, i * 128 - 1:i * 128]
                b.ts(GP[0:4, cs], Aa[0:4, cs], gc, None, op0=ALU.subtract)
                b.ts(GP[32:36, cs], Gx[0:4, cs], -1.0, gc, op0=ALU.mult, op1=ALU.add)
                b.ts(GP[96:100, cs], Aa[0:4, cs], Gx[0:4, (i + 1) * 128 - 1:(i + 1) * 128], None, op0=ALU.subtract)
            b.tt(GP[64:68, :], Bn[0:4, :], Gx[0:4, :], ALU.subtract)
            for r0 in (0, 32, 64, 96):
                b.act(GP[r0:r0 + 4, :], GP[r0:r0 + 4, :], AF.Exp)
            if last:
                b.tt(sm[0:4, 8:9], Gx[0:4, 511:512], Bn[0:4, 511:512], ALU.subtract)
                b.dma_nc(m_p, sm[0:4, 8:9], q='act')
            b.cp(carry[0:4, 0:1], Bn[0:4, 511:512])
            b.cp(carry[0:4, 1:2], Gx[0:4, 511:512])

            b.P.tag = 'b%d.mlstm' % blk
            cub = [pb[2], pb[3], pb[6], pb[0]]
            for (i, R) in ptiles:
                cs = slice(i * 128, (i + 1) * 128)
                b.tr(pb[7][:, 0:128], GP[:, cs], ident32[:, :])
                b.cp(GT[:, :], pb[7][:, 0:128])
                b.mm(pb[7][:, 128:132], sel127[:, :], GT[:, 32:36])
                b.cp(decb[:, 0:4], pb[7][:, 128:132], eng='act')
                b.tt(eAm[:, :, :], mask32[:, :].unsqueeze(1).broadcast_to([128, 4, 128]), GT[:, 0:4].unsqueeze(2).broadcast_to([128, 4, 128]), ALU.mult)
                ptk = pbf(0)
                for h in range(4):
                    for c in range(2):
                        b.tr(ptk[:, h * 256 + c * 128:h * 256 + (c + 1) * 128], kT[:, h * 2 + c, cs], identb[:, :])
                for h in range(4):
                    for c in range(2):
                        b.mm(pb[1][:, h * 128:(h + 1) * 128], kT[:, h * 2 + c, cs], qT[:, h * 2 + c, cs], start=(c == 0), stop=(c == 1))
                b.tt(kq4[:, :, :], ptk[:, 0:1024].rearrange("p (h d) -> p h d", h=4), GT[:, 96:100].unsqueeze(2).broadcast_to([128, 4, 256]), ALU.mult)
                b.tt(spb[:, :, :], pb[1][:, :].rearrange("p (h t) -> p h t", h=4), eAm[:, :, :], ALU.mult)
                for h in range(4):
                    brp = pb[4 + h // 2][:, (h % 2) * 256:(h % 2 + 1) * 256]
                    b.mm(brp, spb[:, h, :], vtok[:, i, h * 256:(h + 1) * 256], start=True, stop=False)
                    b.mm(brp, qT[:, h * 2, cs], Cbf[:, h, 0, :], start=False, stop=False)
                    b.mm(brp, qT[:, h * 2 + 1, cs], Cbf[:, h, 1, :], start=False, stop=True)
                for h in range(4):
                    dn = pb[7][:, 136 + h:137 + h]
                    b.mm(dn, spb[:, h, :], onesb[:, :], start=True, stop=False)
                    b.mm(dn, qT[:, h * 2, cs], nbf[:, h * 2:h * 2 + 1], start=False, stop=False)
                    b.mm(dn, qT[:, h * 2 + 1, cs], nbf[:, h * 2 + 1:h * 2 + 2], start=False, stop=True)
                for h in range(4):
                    for c in range(2):
                        b.mm(cub[h][:, c * 256:(c + 1) * 256], kq4[:, h, c * 128:(c + 1) * 128], vtok[:, i, h * 256:(h + 1) * 256])
                for h in range(4):
                    for c in range(2):
                        b.mm(pb[7][:, 144 + h * 2 + c:145 + h * 2 + c], kq4[:, h, c * 128:(c + 1) * 128], onesb[:, :])
                for h in range(4):
                    brp = pb[4 + h // 2][:, (h % 2) * 256:(h % 2 + 1) * 256]
                    b.act(tmpa[:, h * 256:(h + 1) * 256], brp, AF.Square, accum=sm[:, 16 + h:17 + h])
                b.cp(sm[:, 12:16], pb[7][:, 136:140])
                for h in range(4):
                    b.stt(Cst[:, h, :, :].rearrange("p c v -> p (c v)"), Cst[:, h, :, :].rearrange("p c v -> p (c v)"), decb[:, h:h + 1], cub[h][:, :], ALU.mult, ALU.add)
                    b.cp(Cbf[:, h, :, :], Cst[:, h, :, :], eng='act')
                b.tt(nst[:, :].rearrange("p (h c) -> p h c", h=4), nst[:, :].rearrange("p (h c) -> p h c", h=4), decb[:, 0:4].unsqueeze(2).broadcast_to([128, 4, 2]), ALU.mult)
                b.tt(nst[:, :], nst[:, :], pb[7][:, 144:152], ALU.add)
                b.cp(nbf[:, :], nst[:, :])
                den = sm[:, 12:16]
                iw = GT[:, 32:36]
                emt = GT[:, 64:68]
                b.act(sm[:, 20:24], den, AF.Abs)
                b.tt(sm[:, 20:24], sm[:, 20:24], iw, ALU.mult)
                b.tt(sm[:, 20:24], sm[:, 20:24], emt, ALU.max)
                b.recip(sm[:, 24:28], sm[:, 20:24])
                b.tt(sm[:, 24:28], sm[:, 24:28], iw, ALU.mult)
                b.tt(sm[:, 28:32], sm[:, 16:20], sm[:, 24:28], ALU.mult)
                b.tt(sm[:, 28:32], sm[:, 28:32], sm[:, 24:28], ALU.mult)
                rstd_from_ss(sm[:, 32:36], sm[:, 28:32], 256, 128)
                b.tt(sm[:, 36:40], sm[:, 32:36], sm[:, 24:28], ALU.mult)
                for h in range(4):
                    brp = pb[4 + h // 2][:, (h % 2) * 256:(h % 2 + 1) * 256]
                    b.stt(hf[:, h * 256:(h + 1) * 256], brp, sm[:, 36 + h:37 + h], gsig[:, i, h * 256:(h + 1) * 256], ALU.mult, ALU.mult)
                tok_to_T(hf, 128, mixT, 0, 8, i * 128, bank=1)

            if last:
                for h in range(4):
                    b.dma(c_p[h].rearrange("(c p) v -> p c v", p=128), Cst[:, h, :, :], q='act')
                b.tr(pb[7][0:8, 160:288], nst[:, :], ident32[:, :])
                b.cp(tmpa[0:8, 0:128], pb[7][0:8, 160:288])
                b.dma(n_p, tmpa[0:8, 0:128], q='act')

            chk(2 + 10 * blk)
            b.P.tag = 'b%d.inproj_s' % blk
            formA(OZ, 1024, lambda ps, i, R, g0, n: b.act(zs[:, i, g0:g0 + n], ps[:, 0:n], AF.Silu), ptiles)
            def ev_xbc(ps, c, m_):
                xraw = tmpb if c % 2 == 0 else xin
                acc = tmpa[:, 0:512] if c % 2 == 0 else tmpa[:, 512:1024]
                b.cp(xraw[:, 0:3], convtail[:, c, :])
                b.cp(xraw[:, 3:515], ps[:, :], eng='act')
                b.cp(convtail[:, c, :], xraw[:, 512:515])
                b.ts(acc, xraw[:, 0:512], gpar[:, c:c + 1], gpar[:, 48 + c:49 + c], op0=ALU.mult, op1=ALU.add)
                for j in range(1, 4):
                    b.stt(acc, xraw[:, j:j + 512], gpar[:, j * 12 + c:j * 12 + c + 1], acc, ALU.mult, ALU.add)
                b.act(xbcT[:, c, :], acc, AF.Silu)
            formB(OXBC, 1536, ev_xbc)
            if last:
                for g3 in range(3):
                    for c4 in range(4):
                        c = g3 * 4 + c4
                        b.tr(pb[6][0:3, c4 * 128:(c4 + 1) * 128], convtail[:, c, :], ident32[:, :])
                    dstt = tmpa[0:3, g3 * 512:(g3 + 1) * 512] if g3 < 2 else tmpb[0:3, 0:512]
                    b.cp(dstt, pb[6][0:3, :])
                b.dma(conv_p[:, 0:1024], tmpa[0:3, 0:1024], q='act')
                b.dma(conv_p[:, 1024:1536], tmpb[0:3, 0:512], q='act')
            wt = load_w(w_in[:, ODT:ODT + 16].rearrange("(kc p) n -> p kc n", p=128))
            ps = pjbank()
            for k in range(KC):
                b.mm(ps[0:16, :], wt[:, k, 0:16], uT[:, k, 0:512], start=(k == 0), stop=(k == KC - 1))
            dtv, av, cumn = rows[0], rows[1], rows[2]
            SPk = rows[3]
            b.act(dtv[0:16, :], ps[0:16, :], AF.Exp, bias=smallp[0:16, 2:3])
            b.act(dtv[0:16, :], dtv[0:16, :], AF.Ln, bias=1.0)
            b.ts(av[0:16, :], dtv[0:16, :], smallp[0:16, 4:5], None, op0=ALU.mult)

            chk(3 + 10 * blk)
            b.P.tag = 'b%d.ssd' % blk
            gs_t = load_g(g_ssm[0:1, :])
            xwq = [xw, xw2[:, :]]
            btq = [btok, btok2[:, :, :]]

            def ssd_front(i):
                q_ = i % 2
                cs = slice(i * 128, (i + 1) * 128)
                GTq = GT2[:, q_, :]
                ops = []
                A = ops.append
                ptx = pbf(2)
                for c in range(8):
                    A(lambda c=c: b.tr(ptx[:, c * 128:(c + 1) * 128], xbcT[:, c, cs], identb[:, :]))
                ptb = pbf(3)
                for g in range(2):
                    A(lambda g=g: b.tr(ptb[:, 512 + g * 128:512 + (g + 1) * 128], xbcT[:, 8 + g, cs], identb[:, :]))
                for g in range(2):
                    A(lambda g=g: b.mm(pb[3][:, g * 128:(g + 1) * 128], xbcT[:, 8 + g, cs], xbcT[:, 10 + g, cs]))
                A(lambda: b.scan(cumn[0:16, cs], ones32[0:16, 0:128], av[0:16, cs], 0.0, ALU.mult, ALU.add))
                A(lambda: b.cp(SPk[0:16, cs], cumn[0:16, cs]))
                A(lambda: b.cp(SPk[32:48, cs], dtv[0:16, cs]))
                A(lambda: b.act(SPk[64:80, cs], cumn[0:16, cs], AF.Exp, scale=-1.0))
                wtmp = rows[4]
                A(lambda: b.ts(wtmp[0:16, cs], cumn[0:16, cs], cumn[0:16, (i + 1) * 128 - 1:(i + 1) * 128], None, op0=ALU.subtract))
                A(lambda: b.act(wtmp[0:16, cs], wtmp[0:16, cs], AF.Exp))
                A(lambda: b.tt(SPk[96:112, cs], wtmp[0:16, cs], dtv[0:16, cs], ALU.mult))
                A(lambda: b.cp(xtok, ptx[:, 0:1024], eng='act'))
                A(lambda: b.cp(tmpb[:, 0:256], pb[3][:, 0:256], eng='act'))
                A(lambda: b.cp(btq[q_], ptb[:, 512:768].rearrange("p (g n) -> p g n", g=2)))
                A(lambda: b.tr(pb[7][:, 0:128], SPk[:, cs], ident32[:, :]))
                A(lambda: b.cp(GTq, pb[7][:, 0:128]))
                A(lambda: b.mm(pb[7][:, 128:144], sel127[:, :], GTq[:, 64:80]))
                A(lambda: b.cp(decb2[:, q_, :], pb[7][:, 128:144], eng='act'))
                x3 = xtok.rearrange("p (h q) -> p h q", h=16)
                A(lambda: b.tt(xdt.rearrange("p (h q) -> p h q", h=16), x3, GTq[:, 32:48].unsqueeze(2).broadcast_to([128, 16, 64]), ALU.mult, eng='pool'))
                A(lambda: b.tt(xwq[q_].rearrange("p (h q) -> p h q", h=16), x3, GTq[:, 96:112].unsqueeze(2).broadcast_to([128, 16, 64]), ALU.mult, eng='pool'))
                A(lambda: b.tt(yap[q_][:, :].rearrange("p (h q) -> p h q", h=16), x3, dskb[:, :].unsqueeze(2).broadcast_to([128, 16, 64]), ALU.mult, eng='pool'))
                for j in range(4):
                    R4 = tmpb[0:16, 512:1024].rearrange("p (h n) -> p h n", h=4)
                    A(lambda j=j, R4=R4: b.tt(R4, cumn[0:16, cs].unsqueeze(1).broadcast_to([16, 4, 128]),
                                              ident32[0:16, 4 * j:4 * j + 4].unsqueeze(2).broadcast_to([16, 4, 128]), ALU.mult))
                    A(lambda: b.mm(pb[6][:, :], negones[:, :], tmpb[0:16, 512:1024], start=True, stop=False))
                    A(lambda: b.mm(pb[6][:, :], identb[:, :], negm4b[:, :, :], start=False, stop=True))
                    for hh in range(4):
                        A(lambda j=j, hh=hh: b.act(tmpa[:, hh * 128:(hh + 1) * 128], pb[6][:, hh * 128:(hh + 1) * 128], AF.Exp, bias=GTq[:, 4 * j + hh:4 * j + hh + 1]))
                    A(lambda j=j: b.tt(WT[:, 4 * j:4 * j + 4, :], tmpa[:, 0:512].rearrange("p (h n) -> p h n", h=4),
                                       tmpb[:, (j // 2) * 128:(j // 2 + 1) * 128].unsqueeze(1).broadcast_to([128, 4, 128]), ALU.mult))
                for h in range(16):
                    A(lambda h=h: b.mm(pb[4 + h // 8][:, (h % 8) * 64:(h % 8 + 1) * 64], WT[:, h, :], xdt[:, h * 64:(h + 1) * 64]))
                for g in range(2):
                    A(lambda g=g: b.tt(yap[q_][:, g * 512:(g + 1) * 512], yap[q_][:, g * 512:(g + 1) * 512], pb[4 + g][:, :], ALU.add))
                return ops

            def ssd_back(i):
                q_ = i % 2
                cs = slice(i * 128, (i + 1) * 128)
                GTq = GT2[:, q_, :]
                ya = yap[q_]
                ops = []
                A = ops.append
                for g in range(2):
                    A(lambda g=g: b.mm(pb[g][:, :], xbcT[:, 10 + g, cs], hTbf[:, 8 * g:8 * g + 8, :]))
                for g in range(2):
                    A(lambda g=g: b.tt(xin[:, g * 512:(g + 1) * 512].rearrange("p (h q) -> p h q", h=8), pb[g][:, :].rearrange("p (h q) -> p h q", h=8),
                                       GTq[:, 64 + 8 * g:72 + 8 * g].unsqueeze(2).broadcast_to([128, 8, 64]), ALU.mult))
                A(lambda: b.tt(ya[:, :], ya[:, :], xin[:, :], ALU.add))
                for g in range(2):
                    A(lambda g=g: b.mm(pb[g][:, :], btq[q_][:, g, :], xwq[q_][:, g * 512:(g + 1) * 512]))
                A(lambda: b.tt(ya[:, :], ya[:, :], zs[:, i, :], ALU.mult))
                for g in range(2):
                    hv = hT[:, 8 * g:8 * g + 8, :]
                    A(lambda g=g, hv=hv: b.tt(hv, hv, decb2[:, q_, 8 * g:8 * g + 8].unsqueeze(2).broadcast_to([128, 8, 64]), ALU.mult, eng='pool'))
                for g in range(2):
                    A(lambda g=g: b.act(xin[:, g * 512:(g + 1) * 512], ya[:, g * 512:(g + 1) * 512], AF.Square, accum=sm[:, 40 + g:41 + g]))
                for g in range(2):
                    hv = hT[:, 8 * g:8 * g + 8, :]
                    A(lambda g=g, hv=hv: b.tt(hv, hv, pb[g][:, :].rearrange("p (h q) -> p h q", h=8), ALU.add))
                A(lambda: b.cp(hTbf[:, :, :], hT[:, :, :], eng='act'))
                A(lambda: rstd_from_ss(sm[:, 42:44], sm[:, 40:42], 512, 128))
                for g in range(2):
                    A(lambda g=g: b.stt(hf[:, g * 512:(g + 1) * 512], ya[:, g * 512:(g + 1) * 512], sm[:, 42 + g:43 + g], gs_t[:, g * 512:(g + 1) * 512], ALU.mult, ALU.mult))
                A(lambda: tok_to_T(hf, 128, mixT, 8, 8, i * 128, bank=7))
                return ops

            def interleave(fl, bl):
                nf, nb = len(fl), len(bl)
                out = []
                fi = bi = 0
                while fi < nf or bi < nb:
                    if bi >= nb or (fi < nf and fi * nb <= bi * nf):
                        out.append(fl[fi]); fi += 1
                    else:
                        out.append(bl[bi]); bi += 1
                return out

            for op in ssd_front(0):
                op()
            for (i, R) in ptiles:
                fl = ssd_front(i + 1) if i + 1 < TPB else []
                for op in interleave(fl, ssd_back(i)):
                    op()

            if last:
                for j in range(8):
                    b.tr(pb[j % 2][:, 0:128], hT[:, 2 * j:2 * j + 2, :].rearrange("p a q -> p (a q)"), ident32[:, :])
                    b.cp(tmpb[:, (j % 4) * 128:(j % 4 + 1) * 128], pb[j % 2][:, 0:128])
                    b.dma(ssm_p[j * 128:(j + 1) * 128, :], tmpb[:, (j % 4) * 128:(j % 4 + 1) * 128], q='act')

            chk(4 + 10 * blk)
            b.P.tag = 'b%d.sample' % blk
            if last:
                if True:
                    sample_mixers()

            chk(5 + 10 * blk)
            b.P.tag = 'b%d.outproj' % blk
            g_t = load_g(g_mix_post[0:1, :])
            stream_proj(mixT, 16, w_out, tiles, lambda grp, banks: post_staged(
                grp, banks, g_t, lambda i, R: (xp[t0 + i * 128:t0 + (i + 1) * 128, :] if R == 128 else xs[:, :])))

            chk(6 + 10 * blk)
            for layer in range(2):
                if layer == 1:
                    pool_mixer(blk, tiles, last)
                    chk(8 + 10 * blk)
                ffn(layer, tiles, last)
                chk(7 + 2 * layer + 10 * blk)

            b.P.tag = 'b%d.out' % blk
            for (i, R) in tiles:
                if R == 128:
                    b.dma(y_p[t0 + i * 128:t0 + (i + 1) * 128, :], xres[:, i, :], q='act')
                else:
                    b.dma(y_s[:, :], xres[0:R, i, :], q='act')


    try:
        main_loop()
    except _Stop:
        pass
    b.P.emit()
    return nc, b.P


_CACHE = {}


def kernel(x_prompt, x_sample, state_mlstm_c, state_mlstm_n, state_mlstm_m, state_ssm, state_conv, state_pool,
           g_mix_pre, g_mix_post, g_ffn_pre, g_ffn_post, w_in_ab, b_igate, b_fgate, g_mlstm,
           conv_w, conv_b, dt_bias, a_log, d_skip, g_ssm, w_out_ab, w_pool, pool_scale,
           w_gate, w_up, w_down):
    f = lambda a: np.ascontiguousarray(np.asarray(a, dtype=np.float32))
    if 'nc' not in _CACHE:
        _CACHE['nc'] = build()[0]
    nc = _CACHE['nc']
    NCORE = 8
    shared = {
        "g_mix_pre": f(g_mix_pre), "g_mix_post": f(g_mix_post), "g_ffn_pre": f(g_ffn_pre), "g_ffn_post": f(g_ffn_post),
        "w_in_ab": f(w_in_ab[0]), "b_igate": f(b_igate[0]).reshape(4, 1), "b_fgate": f(b_fgate[0]).reshape(4, 1),
        "g_mlstm": f(g_mlstm[0]).reshape(1, D), "conv_w": f(conv_w[0]), "conv_b": f(conv_b[0]).reshape(1, 1536),
        "dt_bias": f(dt_bias[0]).reshape(16, 1), "a_log": f(a_log[0]).reshape(16, 1), "d_skip": f(d_skip[0]).reshape(1, 16),
        "g_ssm": f(g_ssm[0]).reshape(1, D), "w_out_ab": f(w_out_ab[0]), "w_pool": f(w_pool[0]),
        "pool_scale": f(pool_scale[0]).reshape(1, D), "w_gate": f(w_gate), "w_up": f(w_up), "w_down": f(w_down),
    }
    in_maps = []
    for c in range(NCORE):
        r = slice(c * SB, (c + 1) * SB)
        m = dict(shared)
        m["x_prompt"] = f(x_prompt[c])
        m["x_sample"] = f(x_sample[r, 0, :])
        m["state_mlstm_c"] = f(state_mlstm_c[0, r])
        m["state_mlstm_n"] = f(state_mlstm_n[0, r]).reshape(SB, 1024)
        m["state_mlstm_m"] = f(state_mlstm_m[0, r])
        m["state_ssm"] = f(state_ssm[0, r])
        m["state_conv"] = f(state_conv[0, r])
        m["state_pool"] = f(state_pool[0, r])
        in_maps.append(m)
    res = run_bass_kernel_spmd(nc, in_maps, core_ids=list(range(NCORE)))
    R = res.results
    cat = lambda k: np.concatenate([np.asarray(r_[k]) for r_ in R], axis=0)
    stk = lambda k: np.stack([np.asarray(r_[k]) for r_ in R], axis=0)
    y_prompt = stk("y_prompt")
    y_sample = cat("y_sample").reshape(128, 1, D)
    c_prompt = stk("c_prompt")[None]
    c_sample = cat("c_sample")[None]
    n_prompt = stk("n_prompt").reshape(1, 8, 4, 256)
    n_sample = cat("n_sample").reshape(1, 128, 4, 256)
    m_prompt = stk("m_prompt").reshape(1, 8, 4)
    m_sample = cat("m_sample").reshape(1, 128, 4)
    ssm_prompt = stk("ssm_prompt").reshape(1, 8, 16, 64, 128)
    ssm_sample = cat("ssm_sample").reshape(1, 128, 16, 64, 128)
    conv_prompt = stk("conv_prompt")[None]
    conv_sample = cat("conv_sample")[None]
    pool_prompt = stk("pool_prompt")[None]
    pool_sample = cat("pool_sample")[None]
    outs = (y_prompt, y_sample, c_prompt, c_sample, n_prompt, n_sample, m_prompt, m_sample,
            ssm_prompt, ssm_sample, conv_prompt, conv_sample, pool_prompt, pool_sample)
    return tuple(np.ascontiguousarray(o, dtype=np.float32) for o in outs)
```

```python
import numpy as np
import concourse.bass as bass
import concourse.mybir as mybir

F32 = mybir.dt.float32
BF16 = mybir.dt.bfloat16
ALU = mybir.AluOpType
AF = mybir.ActivationFunctionType
AX = mybir.AxisListType

_DTSIZE = {}


def dtsize(dt):
    s = str(dt)
    if '64' in s:
        return 8
    if '32' in s:
        return 4
    if '16' in s:
        return 2
    return 1


class Op:
    __slots__ = ('eng', 'fn', 'deps', 'idx', 'sig', 'dma', 'semslot', 'semval', 'waits', 'eidx', 'tag')


class Prog:
    ENGS = ('pe', 'act', 'dve', 'pool', 'sp')
    NDMA = 12

    def __init__(self, nc):
        self.nc = nc
        self.ops = []
        self.w = {}
        self.r = {}
        self.dma_count = {e: 0 for e in self.ENGS}
        self.dma_last = {}
        self.bank_last = {}
        self.tag = ''

    @staticmethod
    def box(ap):
        t = ap.tensor
        name = t.name
        esz = dtsize(ap.dtype)
        dims = list(ap.ap)
        sp = str(ap.space)
        if 'DRAM' in sp.upper():
            lo = ap.offset
            hi = lo
            for st, cnt in dims:
                hi += abs(st) * (cnt - 1)
            return name, (0, 1, lo * esz, (hi + 1) * esz)
        pstep, pcnt = dims[0]
        p0 = ap.start_partition
        if callable(p0):
            p0 = p0()
        off = ap.offset - p0 * pstep
        lo = off
        hi = off
        for st, cnt in dims[1:]:
            hi += abs(st) * (cnt - 1)
        return name, (p0, p0 + pcnt, lo * esz, (hi + 1) * esz)

    @staticmethod
    def _ov(a, b):
        return a[0] < b[1] and b[0] < a[1] and a[2] < b[3] and b[2] < a[3]

    @staticmethod
    def _cov(a, b):
        return a[0] <= b[0] and a[1] >= b[1] and a[2] <= b[2] and a[3] >= b[3]

    def add(self, eng, fn, outs=(), ins=(), dma=False):
        op = Op()
        op.eng = eng
        op.fn = fn
        op.idx = len(self.ops)
        op.sig = False
        op.dma = dma
        op.deps = set()
        op.waits = None
        op.tag = self.tag
        for ap in ins:
            name, bx = self.box(ap)
            for (b2, o2) in self.w.get(name, ()):
                if self._ov(bx, b2):
                    op.deps.add(o2)
        outboxes = []
        for ap in outs:
            name, bx = self.box(ap)
            outboxes.append((name, bx))
            for (b2, o2) in self.w.get(name, ()):
                if self._ov(bx, b2):
                    op.deps.add(o2)
            for (b2, o2) in self.r.get(name, ()):
                if self._ov(bx, b2):
                    op.deps.add(o2)
        for ap in ins:
            name, bx = self.box(ap)
            self.r.setdefault(name, []).append((bx, op))
        for name, bx in outboxes:
            self.w[name] = [(b2, o2) for (b2, o2) in self.w.get(name, []) if not self._cov(bx, b2)]
            self.w[name].append((bx, op))
            self.r[name] = [(b2, o2) for (b2, o2) in self.r.get(name, []) if not self._cov(bx, b2)]
        for ap in list(ins) + list(outs):
            if 'PSUM' in str(ap.space).upper():
                bl = self.bank_last.setdefault(ap.tensor.name, {})
                for e2, o2 in bl.items():
                    if e2 != eng:
                        op.deps.add(o2)
                bl[eng] = op
        op.deps.discard(op)
        if dma:
            n = self.dma_count[eng]
            self.dma_count[eng] = n + 1
            op.semslot = n % self.NDMA
            op.semval = 16 * (n // self.NDMA + 1)
            prev = self.dma_last.get((eng, op.semslot))
            if prev is not None:
                op.deps.add(prev)
            self.dma_last[(eng, op.semslot)] = op
        self.ops.append(op)
        return op

    def emit(self):
        nc = self.nc
        for op in self.ops:
            nd = set()
            for d in op.deps:
                if d.eng == 'pe' and op.eng == 'pe' and not d.dma and not op.dma:
                    continue
                nd.add(d)
            best = {}
            keep = set()
            for d in nd:
                if d.dma:
                    keep.add(d)
                else:
                    if d.eng not in best or d.idx > best[d.eng].idx:
                        best[d.eng] = d
            keep.update(best.values())
            op.deps = keep
            for d in keep:
                d.sig = True
        cnt = {e: 0 for e in self.ENGS}
        for op in self.ops:
            if op.dma:
                continue
            if op.sig:
                cnt[op.eng] += 1
                op.semval = cnt[op.eng]
        SEG = 1500
        esem = {e: [nc.alloc_semaphore('s_%s_%d' % (e, k)) for k in range(cnt[e] // SEG + 1)] for e in self.ENGS}
        dsem = {}
        for e in self.ENGS:
            if self.dma_count[e]:
                for i in range(min(self.NDMA, self.dma_count[e])):
                    dsem[(e, i)] = nc.alloc_semaphore('d_%s_%d' % (e, i))
        seen = {e: {} for e in self.ENGS}
        nwaits = 0
        for op in self.ops:
            need = {}
            for d in op.deps:
                key = ('d', d.eng, d.semslot) if d.dma else ('e', d.eng)
                if d.semval > need.get(key, 0):
                    need[key] = d.semval
            ws = []
            for key, v in need.items():
                if seen[op.eng].get(key, 0) >= v:
                    continue
                seen[op.eng][key] = v
                if key[0] == 'd':
                    ws.append((dsem[(key[1], key[2])], v))
                else:
                    ws.append((esem[key[1]][(v - 1) // SEG], (v - 1) % SEG + 1))
            op.waits = ws
            nwaits += len(ws)
        per = {e: [o for o in self.ops if o.eng == e] for e in self.ENGS}
        self.stats = {e: len(per[e]) for e in self.ENGS}
        self.stats['waits'] = nwaits
        self.stats['sig'] = dict(cnt)

        def run(eng, name):
            for op in per[name]:
                for sem, v in op.waits:
                    eng.wait_ge(sem, v)
                ins = op.fn(eng)
                if op.dma:
                    ins.then_inc(dsem[(name, op.semslot)], 16)
                elif op.sig:
                    ins.then_inc(esem[name][(op.semval - 1) // SEG], 1)
            for i in range(min(self.NDMA, self.dma_count[name])):
                last = self.dma_last[(name, i)]
                eng.wait_ge(dsem[(name, i)], last.semval)

        with nc.Block() as block:
            @block.tensor
            def _(e):
                run(e, 'pe')

            @block.scalar
            def _(e):
                run(e, 'act')

            @block.vector
            def _(e):
                run(e, 'dve')

            @block.gpsimd
            def _(e):
                run(e, 'pool')

            @block.sync
            def _(e):
                run(e, 'sp')

from concourse.bass_utils import run_bass_kernel_spmd

D = 1024
KC = 8
T = 2048
NBLK = 4
TPB = 4
SB = 16
FF = 2816
FC = 22
IN_DIM = 6680
EPS = 1e-6
OQ, OK_, OV, OO, OIG, OFG, OZ, OXBC, ODT = 0, 1024, 2048, 3072, 4096, 4100, 4104, 5128, 6664


KSTAGE = 99


class _Stop(Exception):
    pass


def chk(n):
    if n >= KSTAGE:
        raise _Stop()


class Bld:
    def __init__(s, nc):
        s.nc = nc
        s.P = Prog(nc)
        s.rr = 0

    def mm(s, out, lhsT, rhs, start=True, stop=True):
        s.P.add('pe', lambda e: e.matmul(out, lhsT, rhs, start=start, stop=stop), outs=[out], ins=[lhsT, rhs])

    def tr(s, out, in_, ident):
        s.P.add('pe', lambda e: e.transpose(out, in_, ident), outs=[out], ins=[in_, ident])

    def act(s, out, in_, func, scale=1.0, bias=None, accum=None):
        ins = [in_]
        outs = [out]
        kw = {}
        if bias is not None:
            kw['bias'] = bias
            if not isinstance(bias, (int, float)):
                ins.append(bias)
        if not isinstance(scale, (int, float)):
            ins.append(scale)
        if accum is not None:
            kw['accum_out'] = accum
            outs.append(accum)
        s.P.add('act', lambda e: e.activation(out, in_, func, scale=scale, **kw), outs=outs, ins=ins)

    def ts(s, out, in0, s1, s2=None, op0=ALU.mult, op1=None, eng='dve', accum=None):
        ins = [in0]
        outs = [out]
        for x in (s1, s2):
            if x is not None and not isinstance(x, (int, float)):
                ins.append(x)
        kw = {}
        if op1 is not None:
            kw['op1'] = op1
        if accum is not None:
            kw['accum_out'] = accum
            outs.append(accum)
        s.P.add(eng, lambda e: e.tensor_scalar(out, in0, s1, s2, op0=op0, **kw), outs=outs, ins=ins)

    def stt(s, out, in0, sc, in1, op0, op1):
        ins = [in0, in1]
        if not isinstance(sc, (int, float)):
            ins.append(sc)
        s.P.add('dve', lambda e: e.scalar_tensor_tensor(out, in0, sc, in1, op0=op0, op1=op1), outs=[out], ins=ins)

    def tt(s, out, in0, in1, op, eng='dve'):
        s.P.add(eng, lambda e: e.tensor_tensor(out, in0, in1, op=op), outs=[out], ins=[in0, in1])

    def cp(s, out, in_, eng='dve'):
        if eng == 'act':
            s.P.add('act', lambda e: e.copy(out, in_), outs=[out], ins=[in_])
        else:
            s.P.add(eng, lambda e: e.tensor_copy(out, in_), outs=[out], ins=[in_])

    def cpa(s, out, in_):
        s.rr += 1
        s.cp(out, in_, eng='act' if s.rr % 2 else 'dve')

    def memset(s, ap, v, eng='dve'):
        s.P.add(eng, lambda e: e.memset(ap, v), outs=[ap])

    def recip(s, out, in_):
        s.P.add('dve', lambda e: e.reciprocal(out, in_), outs=[out], ins=[in_])

    def scan(s, out, d0, d1, init, op0, op1):
        ins = [d0, d1]
        if not isinstance(init, (int, float)):
            ins.append(init)
        s.P.add('dve', lambda e: e.tensor_tensor_scan(out, d0, d1, init, op0=op0, op1=op1), outs=[out], ins=ins)

    def asel(s, ap, pattern, cm, base, cmp, fill=0.0):
        s.P.add('pool', lambda e: e.affine_select(ap, ap, pattern=pattern, compare_op=cmp, fill=fill, base=base, channel_multiplier=cm), outs=[ap], ins=[ap])

    def dma(s, out, in_, q='sp'):
        s.P.add(q, lambda e: e.dma_start(out=out, in_=in_), outs=[out], ins=[in_], dma=True)

    def dma_nc(s, out, in_, q='sp'):
        s.P.add(q, lambda e: e.dma_start(out=out, in_=in_, allow_slow_non_contiguous=True), outs=[out], ins=[in_], dma=True)

    def sb(s, name, shape, dt=F32):
        return s.nc.alloc_sbuf_tensor(name, list(shape), dt)


def build():
    nc = bass.Bass("TRN2", target_bir_lowering=False)
    b = Bld(nc)

    def din(name, shape):
        return nc.dram_tensor(name, list(shape), F32, kind="ExternalInput").ap()

    def dout(name, shape):
        return nc.dram_tensor(name, list(shape), F32, kind="ExternalOutput").ap()

    xp = din("x_prompt", [T, D])
    xs = din("x_sample", [SB, D])
    st_c = din("state_mlstm_c", [SB, 4, 256, 256])
    st_n = din("state_mlstm_n", [SB, 1024])
    st_m = din("state_mlstm_m", [SB, 4])
    st_ssm = din("state_ssm", [SB, 16, 64, 128])
    st_conv = din("state_conv", [SB, 3, 1536])
    st_pool = din("state_pool", [SB, 15, D])
    g_mix_pre = din("g_mix_pre", [2, D])
    g_mix_post = din("g_mix_post", [2, D])
    g_ffn_pre = din("g_ffn_pre", [2, D])
    g_ffn_post = din("g_ffn_post", [2, D])
    w_in = din("w_in_ab", [D, IN_DIM])
    b_ig = din("b_igate", [4, 1])
    b_fg = din("b_fgate", [4, 1])
    g_mlstm = din("g_mlstm", [1, D])
    conv_w = din("conv_w", [4, 1536])
    conv_b = din("conv_b", [1, 1536])
    dt_bias = din("dt_bias", [16, 1])
    a_log = din("a_log", [16, 1])
    d_skip = din("d_skip", [1, 16])
    g_ssm = din("g_ssm", [1, D])
    w_out = din("w_out_ab", [2048, D])
    w_pool = din("w_pool", [4, 256, 256])
    pool_scale = din("pool_scale", [1, D])
    w_gate = din("w_gate", [2, D, FF])
    w_up = din("w_up", [2, D, FF])
    w_down = din("w_down", [2, FF, D])

    y_p = dout("y_prompt", [T, D])
    y_s = dout("y_sample", [SB, D])
    c_p = dout("c_prompt", [4, 256, 256])
    c_s = dout("c_sample", [SB, 4, 256, 256])
    n_p = dout("n_prompt", [8, 128])
    n_s = dout("n_sample", [SB, 1024])
    m_p = dout("m_prompt", [4, 1])
    m_s = dout("m_sample", [SB, 4])
    ssm_p = dout("ssm_prompt", [16 * 64, 128])
    ssm_s = dout("ssm_sample", [SB, 16 * 64, 128])
    conv_p = dout("conv_prompt", [3, 1536])
    conv_s = dout("conv_sample", [SB, 3, 1536])
    pool_p = dout("pool_prompt", [15, D])
    pool_s = dout("pool_sample", [SB, 15, D])

    uT = b.sb("uT", [128, KC, 528], BF16)
    xres = b.sb("xres", [128, 5, D])
    mixT = b.sb("mixT", [128, 16, 528], BF16)
    Fb = b.sb("Fb", [128, 16384], BF16)
    NSLOT = 10
    wst = [b.sb("wst%d" % i, [128, KC, 256], BF16) for i in range(NSLOT)]
    kq4 = b.sb("kq4", [128, 4, 256], BF16)
    GT2 = b.sb("GT2", [128, 2, 128])
    decb2 = b.sb("decb2", [128, 2, 16])
    xw2 = b.sb("xw2", [128, 1024], BF16)
    btok2 = b.sb("btok2", [128, 2, 128], BF16)
    yap = [b.sb("yap0", [128, D]), b.sb("yap1", [128, D])]
    eAm = b.sb("eAm", [128, 4, 128])
    negm4b = b.sb("negm4b", [128, 4, 128], BF16)
    gbc = [b.sb("gbc%d" % i, [128, D]) for i in range(2)]
    xin = b.sb("xin", [128, D])
    ub = b.sb("ub", [128, D], BF16)
    tmpa = b.sb("tmpa", [128, D])
    tmpb = b.sb("tmpb", [128, D])
    Cst = b.sb("Cst", [128, 4, 2, 256])
    Cbf = b.sb("Cbf", [128, 4, 2, 256], BF16)
    nst = b.sb("nst", [128, 8])
    nbf = b.sb("nbf", [128, 8], BF16)
    hT = b.sb("hT", [128, 16, 64])
    hTbf = b.sb("hTbf", [128, 16, 64], BF16)
    ident32 = b.sb("ident32", [128, 128])
    identb = b.sb("identb", [128, 128], BF16)
    mask32 = b.sb("mask32", [128, 128])
    sel127 = b.sb("sel127", [128, 128])
    ones32 = b.sb("ones32", [128, 512])
    negm4 = b.sb("negm4", [128, 4, 128])
    negones = b.sb("negones", [16, 128])
    onesb = b.sb("onesb", [128, 1], BF16)
    rows = [b.sb("rows%d" % i, [128, 512]) for i in range(6)]
    sm = b.sb("sm", [128, 96])
    GT = b.sb("GT", [128, 128])
    kp = b.sb("kp", [128, 256], BF16)
    kp2 = b.sb("kp2", [128, 256], BF16)
    spb = b.sb("spb", [128, 4, 128], BF16)
    hf = b.sb("hf", [128, D], BF16)
    decb = b.sb("decb", [128, 16])
    carry = b.sb("carry", [128, 8])
    convtail = b.sb("convtail", [128, 12, 3])
    pooltail = b.sb("pooltail", [128, KC, 15], BF16)
    invcnt = b.sb("invcnt", [128, 4, 16])
    gpar = b.sb("gpar", [128, 64])
    gload = b.sb("gload", [64, 128])
    smallp = b.sb("smallp", [16, 8])
    dskb = b.sb("dskb", [128, 16])
    wpl = b.sb("wpl", [128, 4, 2, 256], BF16)
    u1f = b.sb("u1f", [128, D])

    pb = [nc.alloc_psum_tensor("pb%d" % i, [128, 512], F32) for i in range(8)]

    def pbf(i):
        return pb[i][:, :].bitcast(BF16)

    qT = Fb[:, 0:4096].rearrange("p (c n) -> p c n", c=8)
    kT = Fb[:, 4096:8192].rearrange("p (c n) -> p c n", c=8)
    vtok = Fb[:, 8192:12288].rearrange("p (i n) -> p i n", i=4)
    gsig = Fb[:, 12288:16384].rearrange("p (i n) -> p i n", i=4)
    zs = Fb[:, 0:4096].rearrange("p (i n) -> p i n", i=4)
    xbcT = Fb[:, 4096:10240].rearrange("p (c n) -> p c n", c=12)
    WT = Fb[:, 10240:12288].rearrange("p (h n) -> p h n", h=16)
    xtok = Fb[:, 12288:13312]
    xdt = Fb[:, 13312:14336]
    xw = Fb[:, 14336:15360]
    btok = Fb[:, 15360:15616].rearrange("p (g n) -> p g n", g=2)
    E4 = Fb[:, 15616:16128]
    hid = Fb[:, 0:11616].rearrange("p (c n) -> p c n", c=FC)
    sproj = Fb[:, 0:13360].bitcast(F32)

    b.memset(ident32[:, :], 1.0, 'pool')
    b.asel(ident32[:, :], [[-1, 128]], 1, 0, ALU.is_equal)
    b.cp(identb[:, :], ident32[:, :])
    b.memset(mask32[:, :], 1.0, 'pool')
    b.asel(mask32[:, :], [[1, 128]], -1, 0, ALU.is_ge)
    b.memset(sel127[:, :], 1.0, 'pool')
    b.asel(sel127[:, :], [[0, 128]], 1, -127, ALU.is_equal)
    b.memset(ones32[:, :], 1.0)
    b.memset(negm4[:, :, :], 0.0, 'pool')
    b.asel(negm4[:, :, :], [[0, 4], [1, 128]], -1, 0, ALU.is_ge, fill=-30000.0)
    b.memset(negones[:, :], -1.0)
    b.cp(negm4b[:, :, :], negm4[:, :, :])
    b.memset(onesb[:, :], 1.0)
    b.memset(Cst[:, :, :, :], 0.0)
    b.memset(Cbf[:, :, :, :], 0.0)
    b.memset(nst[:, :], 0.0)
    b.memset(nbf[:, :], 0.0)
    b.memset(hT[:, :, :], 0.0)
    b.memset(hTbf[:, :, :], 0.0)
    b.memset(carry[:, :], 0.0)
    b.memset(convtail[:, :, :], 0.0)
    b.memset(pooltail[:, :, :], 0.0)
    b.memset(GT[:, :], 0.0)
    for r in rows:
        b.memset(r[:, :], 0.0)
    for wi, w in enumerate((2, 4, 8, 16)):
        b.memset(invcnt[:, wi, :], 1.0 / w)
        for t_ in range(w - 1):
            b.memset(invcnt[:, wi, t_:t_ + 1], 1.0 / (t_ + 1))
    b.memset(gload[:, :], 0.0)
    b.dma(gload[0:48, :], conv_w.rearrange("j (c p) -> (j c) p", p=128))
    b.dma(gload[48:60, :], conv_b.rearrange("o (c p) -> (o c) p", p=128))
    b.tr(pb[7][:, 0:64], gload[0:64, :], ident32[0:64, 0:64])
    b.cp(gpar[:, :], pb[7][:, 0:64])
    b.dma_nc(smallp[0:4, 0:1], b_ig)
    b.dma_nc(smallp[0:4, 1:2], b_fg)
    b.dma_nc(smallp[0:16, 2:3], dt_bias)
    b.dma_nc(smallp[0:16, 3:4], a_log)
    b.ts(smallp[0:4, 0:2], smallp[0:4, 0:2], 1.0 / 15.0)
    b.act(smallp[0:16, 4:5], smallp[0:16, 3:4], AF.Exp)
    b.dma(dskb[:, :], d_skip.broadcast_to([128, 16]))
    for g in range(4):
        b.dma(wpl[:, g, :, :], w_pool[g].rearrange("(kc p) e -> p kc e", p=128), q='pool')
    b.dma(gbc[0][:, :], pool_scale[0:1, :].broadcast_to([128, D]))
    for g in range(4):
        b.tt(wpl[:, g, :, :], wpl[:, g, :, :], gbc[0][:, g * 256:(g + 1) * 256].unsqueeze(1).broadcast_to([128, 2, 256]), ALU.mult)

    gslot = [0]

    def load_g(src_row):
        t_ = gbc[gslot[0] % 2]
        gslot[0] += 1
        b.dma(t_[:, :], src_row.broadcast_to([128, D]))
        return t_

    wslot = [0]

    NPC = 104
    wscr = nc.dram_tensor("wscr", [NPC, 128, 2048], BF16, kind="Internal").ap()
    wcache = {}

    def fetch_piece(key, slot_view, src, scr_view_fn):
        if key in wcache:
            b.dma(slot_view, scr_view_fn(wcache[key]), q='sp')
        else:
            idx = len(wcache)
            assert idx < NPC
            wcache[key] = idx
            b.dma(slot_view, src, q='pool')
            b.dma(scr_view_fn(idx), slot_view, q='sp')

    def load_w(src):
        t_ = wst[wslot[0] % NSLOT]
        wslot[0] += 1
        n = src.shape[2]
        key = (src.tensor.name, src.offset, n)
        fetch_piece(key, t_[:, :, 0:n], src, lambda idx: wscr[idx].rearrange("p (k c) -> p k c", k=KC)[:, :, 0:n])
        return t_

    def rstd_from_ss(dst, ss, n, rows_):
        b.act(dst, ss, AF.Ln, scale=1.0 / n, bias=EPS)
        b.act(dst, dst, AF.Exp, scale=-0.5)

    def norm_to_T(src, R, g_t, dstT, col0, keep32=None):
        b.act(tmpa[0:R, :], src, AF.Square, accum=sm[0:R, 0:1])
        rstd_from_ss(sm[0:R, 1:2], sm[0:R, 0:1], D, R)
        if keep32 is not None:
            b.stt(keep32[0:R, :], src, sm[0:R, 1:2], g_t[0:R, :], ALU.mult, ALU.mult)
            b.cp(ub[0:R, :], keep32[0:R, :])
        else:
            b.stt(ub[0:R, :], src, sm[0:R, 1:2], g_t[0:R, :], ALU.mult, ALU.mult)
        pt = pbf(2)
        for c in range(KC):
            b.tr(pt[:, c * 128:c * 128 + R], ub[0:R, c * 128:(c + 1) * 128], identb[0:R, 0:R])
        if R == 128:
            b.cpa(dstT[:, :, col0:col0 + R], pt[:, 0:1024].rearrange("p (c n) -> p c n", c=KC))
        else:
            b.cpa(dstT[:, :, col0:col0 + R], pt[:, 0:1024].rearrange("p (c n) -> p c n", c=KC)[:, :, 0:R])

    def tok_to_T(src_bf, R, dstT, c0, nch, col0, bank=2):
        pt = pbf(bank)
        for c in range(nch):
            b.tr(pt[:, c * 128:c * 128 + R], src_bf[0:R, c * 128:(c + 1) * 128], identb[0:R, 0:R])
        b.cpa(dstT[:, c0:c0 + nch, col0:col0 + R], pt[:, 0:nch * 128].rearrange("p (c n) -> p c n", c=nch)[:, :, 0:R])

    pj = [0]

    def pjbank():
        pj[0] += 1
        return pb[pj[0] % 4]

    def post_norm_residual(ps2, R, g_t, xsrc, xdst):
        b.act(tmpa[0:R, 0:512], ps2[0][0:R, :], AF.Square, accum=sm[0:R, 2:3])
        b.act(tmpa[0:R, 512:1024], ps2[1][0:R, :], AF.Square, accum=sm[0:R, 3:4])
        b.tt(sm[0:R, 4:5], sm[0:R, 2:3], sm[0:R, 3:4], ALU.add)
        rstd_from_ss(sm[0:R, 5:6], sm[0:R, 4:5], D, R)
        for j in range(2):
            b.stt(tmpb[0:R, j * 512:(j + 1) * 512], ps2[j][0:R, :], sm[0:R, 5:6], g_t[0:R, j * 512:(j + 1) * 512], ALU.mult, ALU.mult)
        b.tt(xdst, tmpb[0:R, :], xsrc, ALU.add)


    smn = b.sb("smn", [128, 48])

    def post_staged(grp, banks, g_t, xadd):
        for n, (i, R) in enumerate(grp):
            b.act(tmpa[0:R, 0:512], banks[n][0][0:R, :], AF.Square, accum=smn[0:R, 2 * n:2 * n + 1])
            b.act(tmpa[0:R, 512:1024], banks[n][1][0:R, :], AF.Square, accum=smn[0:R, 2 * n + 1:2 * n + 2])
        ng = len(grp)
        b.P.add('dve', lambda e: e.tensor_reduce(smn[:, 16:16 + ng], smn[:, 0:2 * ng].rearrange("p (n t) -> p n t", t=2), axis=AX.X, op=ALU.add),
                outs=[smn[:, 16:16 + ng]], ins=[smn[:, 0:2 * ng]])
        rstd_from_ss(smn[:, 24:24 + ng], smn[:, 16:16 + ng], D, 128)
        for n, (i, R) in enumerate(grp):
            src = xadd(i, R)
            if src is not None:
                for j in range(2):
                    b.stt(xres[0:R, i, j * 512:(j + 1) * 512], banks[n][j][0:R, :], smn[0:R, 24 + n:25 + n], g_t[0:R, j * 512:(j + 1) * 512], ALU.mult, ALU.mult)
                xo = xres[0:R, i, :]
                b.P.add('pool', lambda e, xo=xo, src=src: e.dma_start(out=xo, in_=src, accum_op=ALU.add), outs=[xo], ins=[xo, src], dma=True)
            else:
                tmp = tmpb if n % 2 == 0 else tmpa
                for j in range(2):
                    b.stt(tmp[0:R, j * 512:(j + 1) * 512], banks[n][j][0:R, :], smn[0:R, 24 + n:25 + n], g_t[0:R, j * 512:(j + 1) * 512], ALU.mult, ALU.mult)
                b.tt(xres[0:R, i, :], tmp[0:R, :], xres[0:R, i, :], ALU.add)

    def prenorm_staged(tl, g_t, dstT, col0_fn, ubs, keep32_fn=None):
        nt = len(tl)
        for n, (i, R) in enumerate(tl):
            b.act(tmpa[0:R, :], xres[0:R, i, :], AF.Square, accum=smn[0:R, 32 + n:33 + n])
        rstd_from_ss(smn[:, 40:40 + nt], smn[:, 32:32 + nt], D, 128)
        for n, (i, R) in enumerate(tl):
            k32 = keep32_fn(i, R) if keep32_fn else None
            if k32 is not None:
                b.stt(k32[0:R, :], xres[0:R, i, :], smn[0:R, 40 + n:41 + n], g_t[0:R, :], ALU.mult, ALU.mult)
                b.cp(ubs[n][0:R, :], k32[0:R, :], eng='act')
            else:
                b.stt(ubs[n][0:R, :], xres[0:R, i, :], smn[0:R, 40 + n:41 + n], g_t[0:R, :], ALU.mult, ALU.mult)
        for n, (i, R) in enumerate(tl):
            pt = pbf(2 + n % 2)
            for c in range(KC):
                b.tr(pt[:, c * 128:c * 128 + R], ubs[n][0:R, c * 128:(c + 1) * 128], identb[0:R, 0:R])
            b.cpa(dstT[:, :, col0_fn(i):col0_fn(i) + R], pt[:, 0:1024].rearrange("p (c n) -> p c n", c=KC)[:, :, 0:R])

    curblk = [0]

    def wview(w2d, c0, n):
        return w2d[:, c0:c0 + n].rearrange("(kc p) n -> p kc n", p=128)

    def load_w2(w2d, k2):
        t_ = wst[wslot[0] % NSLOT]
        wslot[0] += 1
        v = t_[:, :, :].rearrange("p a n -> p (a n)").rearrange("p (kc n) -> p kc n", kc=2)
        src = w2d[k2 * 128:(k2 + 2) * 128, :].rearrange("(kc p) n -> p kc n", p=128)
        key = (src.tensor.name, src.offset, 'w2')
        fetch_piece(key, v, src, lambda idx: wscr[idx].rearrange("p (kc n) -> p kc n", kc=2))
        return v

    def stream_proj(actT, nk, w2d, tiles, post):
        groups = [tiles[0:4]] + ([tiles[4:]] if len(tiles) > 4 else [])
        for grp in groups:
            for k2 in range(0, nk, 2):
                wv = load_w2(w2d, k2)
                for kk in range(2):
                    k = k2 + kk
                    for gi, (i, R) in enumerate(grp):
                        for j in range(2):
                            b.mm(pb[2 * gi + j][0:R, :], actT[:, k, i * 128:i * 128 + R], wv[:, kk, j * 512:(j + 1) * 512], start=(k == 0), stop=(k == nk - 1))
            post(grp, [(pb[2 * gi], pb[2 * gi + 1]) for gi in range(len(grp))])

    def ffn(layer, tiles, last):
        b.P.tag = 'b%d.ffn%d.gateup' % (curblk[0], layer)
        g_t = load_g(g_ffn_pre[layer:layer + 1, :])
        ubs_f = [Fb[:, 11616 + n * 1024:11616 + (n + 1) * 1024] for n in range(4)] + [ub]
        prenorm_staged(tiles, g_t, uT, lambda i: i * 128, ubs_f)
        for j in range(11):
            wg = load_w(wview(w_gate[layer], j * 256, 256))
            wu = load_w(wview(w_up[layer], j * 256, 256))
            for m in range(2):
                nset = 3 if last else 4
                st_ = (2 * j + m) % nset
                pg, pu = ((pb[0], pb[1]), (pb[2], pb[3]), (pb[4], pb[5]), (pb[6], pb[7]))[st_]
                tmp = (tmpa[:, 0:512], tmpa[:, 512:1024], tmpb[:, 0:512], tmpb[:, 512:1024])[st_]
                for k in range(KC):
                    b.mm(pg[:, :], wg[:, k, m * 128:(m + 1) * 128], uT[:, k, 0:512], start=(k == 0), stop=(k == KC - 1))
                for k in range(KC):
                    b.mm(pu[:, :], wu[:, k, m * 128:(m + 1) * 128], uT[:, k, 0:512], start=(k == 0), stop=(k == KC - 1))
                b.act(tmp, pg[:, :], AF.Silu)
                b.tt(hid[:, 2 * j + m, 0:512], tmp, pu[:, :], ALU.mult)
                if last:
                    ps_ = pb[6] if (2 * j + m) % 2 == 0 else pb[7]
                    tmps = ub[:, 0:16] if (2 * j + m) % 2 == 0 else ub[:, 16:32]
                    for k in range(KC):
                        b.mm(ps_[:, 0:16], wg[:, k, m * 128:(m + 1) * 128], uT[:, k, 512:528], start=(k == 0), stop=(k == KC - 1))
                    for k in range(KC):
                        b.mm(ps_[:, 16:32], wu[:, k, m * 128:(m + 1) * 128], uT[:, k, 512:528], start=(k == 0), stop=(k == KC - 1))
                    b.act(tmps, ps_[:, 0:16], AF.Silu)
                    b.tt(hid[:, 2 * j + m, 512:528], tmps, ps_[:, 16:32], ALU.mult)
        b.P.tag = 'b%d.ffn%d.down' % (curblk[0], layer)
        g_p = load_g(g_ffn_post[layer:layer + 1, :])
        stream_proj(hid, FC, w_down[layer], tiles, lambda grp, banks: post_staged(grp, banks, g_p, lambda i, R: None))

    pext = Fb[:, 0:4352].rearrange("p (c n) -> p c n", c=8)
    dT = Fb[:, 4352:8576].rearrange("p (c n) -> p c n", c=8)
    hist = Fb[:, 8576:16256].bitcast(F32)

    def pool_mixer(blk, tiles, last):
        b.P.tag = 'b%d.pool' % curblk[0]
        g_t = load_g(g_mix_pre[1:2, :])
        b.cp(pext[:, :, 0:15], pooltail[:, :, :])
        ubs_p = [Fb[:, 8576 + n * 1024:8576 + (n + 1) * 1024] for n in range(4)] + [ub]

        def k32f(i, R):
            if R != 128:
                return xin
            if last and i == TPB - 1:
                return u1f
            return None
        prenorm_staged(tiles, g_t, pext, lambda i: 15 + i * 128, ubs_p, keep32_fn=k32f)
        b.cp(pooltail[:, :, :], pext[:, :, 512:527])
        if last:
            b.dma(pool_p[:, :], u1f[113:128, :], q='act')
            b.dma(pool_s[:, 0:14, :], st_pool[:, 1:15, :], q='act')
            b.dma(pool_s[:, 14, :], xin[0:SB, :], q='act')
        for c in range(8):
            wi = c // 2
            w = 2 << wi
            alt = (c % 2 == 1) and not last
            src, dst = (xin, u1f) if alt else (tmpa, tmpb)
            b.cp(src[:, 0:527], pext[:, c, 0:527], eng='act')
            for k in range(wi + 1):
                sh = 1 << k
                b.tt(dst[:, sh:527], src[:, sh:527], src[:, 0:527 - sh], ALU.add, eng=('pool' if alt else 'dve'))
                src, dst = dst, src
            b.stt(dT[:, c, 0:512], src[:, 15:527], 1.0 / w, pext[:, c, 15:527], ALU.mult, ALU.subtract)
            if blk == 0:
                b.tt(dst[:, 0:16], src[:, 15:31], invcnt[:, wi, :], ALU.mult)
                b.tt(dT[:, c, 0:16], dst[:, 0:16], pext[:, c, 15:31], ALU.subtract)
        if last:
            dsm = tmpa
            for wi in range(4):
                w = 2 << wi
                gc = slice(wi * 256, (wi + 1) * 256)
                hv = hist[0:SB, 0:(w - 1) * 256].rearrange("p (j n) -> p j n", j=w - 1)
                b.dma(hv, st_pool[:, 15 - (w - 1):15, gc])
                b.P.add('dve', lambda e, hv=hv, gc=gc: e.tensor_reduce(dsm[0:SB, gc], hv.rearrange("p j n -> p n j"), axis=AX.X, op=ALU.add),
                        outs=[dsm[0:SB, gc]], ins=[hv])
                b.tt(dsm[0:SB, gc], dsm[0:SB, gc], xin[0:SB, gc], ALU.add)
                b.stt(dsm[0:SB, gc], dsm[0:SB, gc], 1.0 / w, xin[0:SB, gc], ALU.mult, ALU.subtract)
            b.cp(ub[0:SB, :], dsm[0:SB, :])
            tok_to_T(ub, SB, dT, 0, 8, 512, bank=2)
        gp_t = load_g(g_mix_post[1:2, :])
        banks = []
        for n, (i, R) in enumerate(tiles[0:4]):
            pp = (pb[2 * n], pb[2 * n + 1])
            banks.append(pp)
            for g in range(4):
                for kc in range(2):
                    b.mm(pp[g // 2][0:R, (g % 2) * 256:(g % 2 + 1) * 256], dT[:, g * 2 + kc, i * 128:i * 128 + R], wpl[:, g, kc, :], start=(kc == 0), stop=(kc == 1))
        post_staged(tiles[0:4], banks, gp_t, lambda i, R: None)
        if len(tiles) > 4:
            (i, R) = tiles[4]
            pp = (pb[0], pb[1])
            for g in range(4):
                for kc in range(2):
                    b.mm(pp[g // 2][0:R, (g % 2) * 256:(g % 2 + 1) * 256], dT[:, g * 2 + kc, i * 128:i * 128 + R], wpl[:, g, kc, :], start=(kc == 0), stop=(kc == 1))
            post_staged(tiles[4:], [pp], gp_t, lambda i, R: None)

    def sample_mixers():
        S = SB
        xf = xres[0:S, :, :].rearrange("p i n -> p (i n)")
        sp_ = sproj[0:S, :]
        onehot3 = rows[4][:, 0:256].rearrange("p (a c) -> p a c", a=16)
        qTs = rows[4][:, 256:384].rearrange("p (a c) -> p a c", a=8)
        qTm = rows[4][:, 384:512].rearrange("p (a c) -> p a c", a=8)
        decS = rows[5][:, 0:64]
        EAc = rows[5][:, 64:192]
        CTs = rows[5][:, 192:224].rearrange("p (a c) -> p a c", a=2)
        s2 = rows[3]
        cbuf = [rows[0], rows[1], rows[2]]
        for (pc0, pcn) in ((OQ, 1024), (OK_, 1024), (OV, 1024), (OO, 1024), (OIG, 8), (OZ, 1024), (OXBC, 1536), (ODT, 16)):
            for gg in range(0, pcn, 256):
                g0 = pc0 + gg
                n = min(256, pcn - gg)
                wt = load_w(w_in[:, g0:g0 + n].rearrange("(kc p) n -> p kc n", p=128))
                ps = pjbank()
                for k in range(KC):
                    b.mm(ps[0:S, 0:n], uT[:, k, 512:528], wt[:, k, 0:n], start=(k == 0), stop=(k == KC - 1))
                b.cpa(sp_[:, g0:g0 + n], ps[0:S, 0:n])
        b.memset(onehot3, 0.0)
        for j in range(16):
            b.memset(onehot3[:, j, j:j + 1], 1.0)
        q_ = sp_[:, OQ:OQ + 1024]
        k_ = sp_[:, OK_:OK_ + 1024]
        v_ = sp_[:, OV:OV + 1024]
        o_ = sp_[:, OO:OO + 1024]
        b.P.tag = 's.gates'
        b.dma(s2[0:S, 0:4], b_ig.rearrange("h o -> o h").broadcast_to([S, 4]))
        b.dma(s2[0:S, 4:8], b_fg.rearrange("h o -> o h").broadcast_to([S, 4]))
        b.dma(s2[0:S, 8:12], st_m[:, :])
        b.tt(s2[0:S, 12:20], sp_[:, OIG:OIG + 8], s2[0:S, 0:8], ALU.add)
        b.act(s2[0:S, 12:20], s2[0:S, 12:20], AF.Tanh, scale=1.0 / 15.0)
        li = s2[0:S, 20:24]
        b.ts(li, s2[0:S, 12:16], 15.0)
        b.act(s2[0:S, 24:28], s2[0:S, 16:20], AF.Exp, scale=-15.0)
        b.act(s2[0:S, 24:28], s2[0:S, 24:28], AF.Ln, bias=1.0)
        lfm = s2[0:S, 28:32]
        b.tt(lfm, s2[0:S, 8:12], s2[0:S, 24:28], ALU.subtract)
        mt = s2[0:S, 32:36]
        b.tt(mt, lfm, li, ALU.max)
        b.dma(m_s[:, :], mt, q='act')
        a1 = s2[0:S, 36:40]
        a2 = s2[0:S, 40:44]
        emt = s2[0:S, 44:48]
        b.tt(a1, li, mt, ALU.subtract)
        b.act(a1, a1, AF.Exp)
        b.tt(a2, lfm, mt, ALU.subtract)
        b.act(a2, a2, AF.Exp)
        b.act(emt, mt, AF.Exp, scale=-1.0)
        b.ts(q_, q_, 0.0625)
        nsv = xf[:, 0:1024]
        t1 = xf[:, 1024:2048]
        t2 = xf[:, 2048:3072]
        hh = xf[:, 3072:4096]
        b.dma(nsv, st_n[:, :])
        b.tt(t1, q_, k_, ALU.mult)
        qk = s2[0:S, 48:52]
        b.P.add('dve', lambda e: e.tensor_reduce(qk, t1.rearrange("p (h d) -> p h d", h=4), axis=AX.X, op=ALU.add), outs=[qk], ins=[t1])
        b.tt(t1, q_, nsv, ALU.mult)
        qn = s2[0:S, 52:56]
        b.P.add('dve', lambda e: e.tensor_reduce(qn, t1.rearrange("p (h d) -> p h d", h=4), axis=AX.X, op=ALU.add), outs=[qn], ins=[t1])
        sv = s2[0:S, 56:60]
        b.tt(sv, qk, a1, ALU.mult)
        den = s2[0:S, 60:64]
        b.tt(den, qn, a2, ALU.mult)
        b.tt(den, den, sv, ALU.add)
        b.tt(t1.rearrange("p (h d) -> p h d", h=4), k_.rearrange("p (h d) -> p h d", h=4), a1.unsqueeze(2).broadcast_to([S, 4, 256]), ALU.mult)
        b.tt(t2.rearrange("p (h d) -> p h d", h=4), nsv.rearrange("p (h d) -> p h d", h=4), a2.unsqueeze(2).broadcast_to([S, 4, 256]), ALU.mult)
        b.tt(t2, t2, t1, ALU.add)
        b.dma(n_s[:, :], t2, q='act')
        for c in range(8):
            b.tr(pb[7][:, c * 16:(c + 1) * 16], q_[:, c * 128:(c + 1) * 128], ident32[0:S, 0:S])
        b.cp(qTs, pb[7][:, 0:128].rearrange("p (a c) -> p a c", a=8))
        Rd = s2[0:S, 64:128].rearrange("p (a c) -> p a c", a=16)
        b.tt(Rd, a2.unsqueeze(1).broadcast_to([S, 16, 4]), ident32[0:S, 0:16].unsqueeze(2).broadcast_to([S, 16, 4]), ALU.mult)
        b.mm(pb[7][:, 128:192], ones32[0:S, 0:128], s2[0:S, 64:128])
        b.cp(decS, pb[7][:, 128:192])
        b.P.tag = 's.cloop'
        qTmAll = Cst[:, :, :, :].rearrange("p h c v -> p (h c v)")[:, 0:1024].bitcast(BF16).rearrange("p (a bq j) -> p a bq j", a=8, bq=16)
        cbb = [hf, Fb[:, 13360:14384]]
        for a_ in range(8):
            b.tt(qTmAll[:, a_, :, :], qTs[:, a_:a_ + 1, :].broadcast_to([128, 16, 16]), onehot3, ALU.mult)
        kwb = [s2[0:S, 128:256].bitcast(BF16), s2[0:S, 256:384].bitcast(BF16)]
        vbf = ub[0:S, :]
        b.cp(vbf, v_)
        cb4 = [tmpa, tmpb, xin, u1f]
        its = [(bb, hp) for hp in range(2) for bb in range(S)]
        DEPTH = 3

        def c_load(n_):
            bb, hp = its[n_]
            cbv = cb4[n_ % 4][:, :].rearrange("p (hh c v) -> p hh c v", hh=2, c=2)
            b.dma(cbv, st_c[bb, 2 * hp:2 * hp + 2].rearrange("hh (c p) v -> p hh c v", p=128))

        def c_compute(n_):
            bb, hp = its[n_]
            cb = cb4[n_ % 4]
            cbv = cb[:, :].rearrange("p (hh c v) -> p hh c v", hh=2, c=2)
            pso = (pb[6], pb[7]) if n_ % 2 == 0 else (pb[2], pb[3])
            cq = cbb[n_ % 2]
            b.cp(cq[:, :], cb[:, :], eng='act')
            cqv = cq[:, :].rearrange("p (hh c v) -> p hh c v", hh=2, c=2)
            for hh in range(2):
                h = 2 * hp + hh
                qcp = pb[4 + hh][0:S, hp * 256:(hp + 1) * 256]
                for c in range(2):
                    b.mm(qcp, qTmAll[:, 2 * h + c, bb, :], cqv[:, hh, c, :], start=(bb == 0 and c == 0), stop=(bb == S - 1 and c == 1))
                kw_ = kwb[hh]
                b.ts(kw_, t1[:, h * 256:(h + 1) * 256], ident32[0:S, bb:bb + 1], None, op0=ALU.mult)
                for c in range(2):
                    b.mm(pso[hh][:, c * 256:(c + 1) * 256], kw_[:, c * 128:(c + 1) * 128], vbf[:, h * 256:(h + 1) * 256])
                b.stt(cb[:, hh * 512:(hh + 1) * 512], cb[:, hh * 512:(hh + 1) * 512], decS[:, bb * 4 + h:bb * 4 + h + 1], pso[hh][:, :], ALU.mult, ALU.add)
            b.dma(c_s[bb, 2 * hp:2 * hp + 2].rearrange("hh (c p) v -> p hh c v", p=128), cbv, q='pool')

        for n_ in range(len(its) + DEPTH):
            if n_ < len(its):
                c_load(n_)
            if n_ >= DEPTH:
                c_compute(n_ - DEPTH)
        b.P.tag = 's.hfin'
        b.act(den, den, AF.Abs)
        b.tt(den, den, emt, ALU.max)
        rr_ = s2[0:S, 384:388]
        b.recip(rr_, den)
        b.tt(t2.rearrange("p (h d) -> p h d", h=4), v_.rearrange("p (h d) -> p h d", h=4), sv.unsqueeze(2).broadcast_to([S, 4, 256]), ALU.mult)
        for h in range(4):
            qcp = pb[4 + h % 2][0:S, (h // 2) * 256:(h // 2 + 1) * 256]
            hs_ = slice(h * 256, (h + 1) * 256)
            b.stt(hh[:, hs_], qcp, a2[:, h:h + 1], t2[:, hs_], ALU.mult, ALU.add)
            b.ts(hh[:, hs_], hh[:, hs_], rr_[:, h:h + 1], None, op0=ALU.mult)
            b.act(t2[:, hs_], hh[:, hs_], AF.Square, accum=s2[0:S, 388 + h:389 + h])
        rstd_from_ss(s2[0:S, 392:396], s2[0:S, 388:392], 256, S)
        b.act(t2, o_, AF.Sigmoid)
        gm2 = load_g(g_mlstm[0:1, :])
        b.tt(t2, t2, gm2[0:S, :], ALU.mult)
        for h in range(4):
            hs_ = slice(h * 256, (h + 1) * 256)
            b.stt(ub[0:S, hs_], hh[:, hs_], s2[0:S, 392 + h:393 + h], t2[:, hs_], ALU.mult, ALU.mult)
        tok_to_T(ub, S, mixT, 0, 8, 512, bank=2)

        b.P.tag = 'sample_ssd'
        z_ = sp_[:, OZ:OZ + 1024]
        xb = sp_[:, OXBC:OXBC + 1536]
        dtr = sp_[:, ODT:ODT + 16]
        b.dma(conv_s[:, 0:2, :], st_conv[:, 1:3, :], q='act')
        b.dma(conv_s[:, 2, :], xb, q='act')
        scv = xf[:, 0:4608].rearrange("p (j n) -> p j n", j=3)
        b.dma(scv, st_conv[:, :, :])
        cwb = uT[0:S, :, :].rearrange("p c n -> p (c n)").bitcast(F32)[:, 0:1536]
        b.dma(cwb, conv_w[3:4, :].broadcast_to([S, 1536]))
        b.tt(xb, xb, cwb, ALU.mult)
        for j in range(3):
            b.dma(cwb, conv_w[j:j + 1, :].broadcast_to([S, 1536]))
            b.tt(scv[:, j, :], scv[:, j, :], cwb, ALU.mult)
            b.tt(xb, xb, scv[:, j, :], ALU.add)
        b.dma(cwb, conv_b[0:1, :].broadcast_to([S, 1536]))
        b.tt(xb, xb, cwb, ALU.add)
        b.act(xb, xb, AF.Silu)
        xs_ = xb[:, 0:1024]
        Bm = xb[:, 1024:1280]
        Cm = xb[:, 1280:1536]
        b.dma(s2[0:S, 400:416], dt_bias.rearrange("h o -> o h").broadcast_to([S, 16]))
        b.dma(s2[0:S, 416:432], a_log.rearrange("h o -> o h").broadcast_to([S, 16]))
        dts = s2[0:S, 432:448]
        b.tt(dts, dtr, s2[0:S, 400:416], ALU.add)
        b.act(dts, dts, AF.Exp)
        b.act(dts, dts, AF.Ln, bias=1.0)
        eas = s2[0:S, 448:464]
        b.act(eas, s2[0:S, 416:432], AF.Exp)
        b.tt(eas, eas, dts, ALU.mult)
        b.act(eas, eas, AF.Exp, scale=-1.0)
        b.tt(t1[:, 0:256], Bm, Cm, ALU.mult)
        cbs = s2[0:S, 464:466]
        b.P.add('dve', lambda e: e.tensor_reduce(cbs, t1[:, 0:256].rearrange("p (g n) -> p g n", g=2), axis=AX.X, op=ALU.add), outs=[cbs], ins=[t1[:, 0:256]])
        cf = s2[0:S, 466:482]
        b.tt(cf.rearrange("p (g e) -> p g e", g=2), dts.rearrange("p (g e) -> p g e", g=2), cbs.unsqueeze(2).broadcast_to([S, 2, 8]), ALU.mult)
        b.tt(cf, cf, dskb[0:S, :], ALU.add)
        yac = xf[:, 4096:5120]
        b.tt(yac.rearrange("p (h q) -> p h q", h=16), xs_.rearrange("p (h q) -> p h q", h=16), cf.unsqueeze(2).broadcast_to([S, 16, 64]), ALU.mult)
        b.tt(t2.rearrange("p (h q) -> p h q", h=16), xs_.rearrange("p (h q) -> p h q", h=16), dts.unsqueeze(2).broadcast_to([S, 16, 64]), ALU.mult)
        for g in range(2):
            b.tr(pb[7][:, 192 + g * 16:192 + (g + 1) * 16], Cm[:, g * 128:(g + 1) * 128], ident32[0:S, 0:S])
        b.cp(CTs, pb[7][:, 192:224].rearrange("p (a c) -> p a c", a=2))
        for h2 in range(2):
            Re = s2[0:S, 128:256].rearrange("p (a c) -> p a c", a=16)
            b.tt(Re, eas.rearrange("p (j t) -> p j t", t=2)[:, :, h2].unsqueeze(1).broadcast_to([S, 16, 8]),
                 ident32[0:S, 0:16].unsqueeze(2).broadcast_to([S, 16, 8]), ALU.mult)
            b.mm(pb[7][h2 * 64:(h2 + 1) * 64, 256:384], ones32[0:S, 0:64], s2[0:S, 128:256])
        b.cp(EAc, pb[7][:, 256:384])
        yacc = hh
        b.memset(yacc, 0.0)
        b.P.tag = 's.ssdloop'
        h0b = [tmpa[:, :], xin[:, :], Fb[:, 0:2048].bitcast(F32)]
        hTb = [tmpb[:, :].bitcast(BF16), u1f[:, :].bitcast(BF16), Fb[:, 2048:4096]]
        xdmb = [t1.bitcast(BF16)[:, 0:1024], xf[:, 0:1024].bitcast(BF16)[:, 0:1024]]
        Bmb = kp[0:S, :]
        b.cp(Bmb, Bm)
        CTb = kp2[:, 0:32].rearrange("p (a c) -> p a c", a=2)
        b.cp(CTb, CTs)

        def s_load(bb):
            h0 = h0b[bb % 3].rearrange("p (j n) -> p j n", j=8)
            b.dma(h0, st_ssm[bb].rearrange("(j t) p n -> (t p) j n", t=2))

        def s_compute(bb):
            h0f = h0b[bb % 3]
            h0 = h0f.rearrange("p (j n) -> p j n", j=8)
            hb_ = hTb[bb % 3][:, 0:1024]
            hTs = hTb[bb % 3][:, 1024:2048]
            xdm = xdmb[bb % 2]
            b.cp(hb_, h0f, eng='act')
            ptq = pbf(0)
            for j in range(8):
                b.tr(ptq[:, j * 128:(j + 1) * 128], hb_[:, j * 128:(j + 1) * 128], identb[:, :])
            b.cp(hTs, ptq[:, 0:1024])
            for g in range(2):
                b.mm(pb[2 + g][0:S, :], CTb[:, g, :], hTs[:, g * 512:(g + 1) * 512])
                b.stt(yacc[:, g * 512:(g + 1) * 512], pb[2 + g][0:S, :], ident32[0:S, bb:bb + 1], yacc[:, g * 512:(g + 1) * 512], ALU.mult, ALU.add)
            b.ts(xdm, t2, ident32[0:S, bb:bb + 1], None, op0=ALU.mult)
            for j in range(8):
                g = j // 4
                b.mm(pb[4 + (bb % 2) * 2 + j // 4][:, (j % 4) * 128:(j % 4 + 1) * 128], xdm[:, j * 128:(j + 1) * 128], Bmb[:, g * 128:(g + 1) * 128])
            b.tt(h0, h0, EAc[:, bb * 8:(bb + 1) * 8].unsqueeze(2).broadcast_to([128, 8, 128]), ALU.mult, eng='pool')
            for jj in range(2):
                b.tt(h0f[:, jj * 512:(jj + 1) * 512], h0f[:, jj * 512:(jj + 1) * 512], pb[4 + (bb % 2) * 2 + jj][:, :], ALU.add)
            b.dma(ssm_s[bb].rearrange("(j t p) n -> (t p) j n", t=2, p=64), h0, q='pool')

        for n_ in range(S + 2):
            if n_ < S:
                s_load(n_)
            if n_ >= 2:
                s_compute(n_ - 2)
        b.P.tag = 's.ssdpost'
        b.tt(yacc.rearrange("p (h q) -> p h q", h=16), yacc.rearrange("p (h q) -> p h q", h=16), eas.unsqueeze(2).broadcast_to([S, 16, 64]), ALU.mult)
        b.tt(yac, yac, yacc, ALU.add)
        b.act(t2, z_, AF.Silu)
        b.tt(yac, yac, t2, ALU.mult)
        for g in range(2):
            gs = slice(g * 512, (g + 1) * 512)
            b.act(t2[:, gs], yac[:, gs], AF.Square, accum=s2[0:S, 484 + g:485 + g])
        rstd_from_ss(s2[0:S, 486:488], s2[0:S, 484:486], 512, S)
        gs2 = load_g(g_ssm[0:1, :])
        for g in range(2):
            gs = slice(g * 512, (g + 1) * 512)
            b.stt(ub[0:S, gs], yac[:, gs], s2[0:S, 486 + g:487 + g], gs2[0:S, gs], ALU.mult, ALU.mult)
        tok_to_T(ub, S, mixT, 8, 8, 512, bank=2)

    def main_loop():
        for blk in range(NBLK):
            last = blk == NBLK - 1
            tiles = [(i, 128) for i in range(TPB)] + ([(TPB, SB)] if last else [])
            NTOK = 512 + (SB if last else 0)
            t0 = blk * 512

            curblk[0] = blk
            b.P.tag = 'b%d.prenorm' % blk

            g_t = load_g(g_mix_pre[0:1, :])
            for (i, R) in tiles:
                src = xp[t0 + i * 128:t0 + (i + 1) * 128, :] if R == 128 else xs[:, :]
                b.dma(xres[0:R, i, :], src)
            ubs_0 = [Fb[:, n * 1024:(n + 1) * 1024] for n in range(4)] + [ub]
            prenorm_staged(tiles, g_t, uT, lambda i: i * 128, ubs_0)

            def formB(col0, ncols, evac):
                for g0 in range(0, ncols, 256):
                    n = min(256, ncols - g0)
                    wt = load_w(w_in[:, col0 + g0:col0 + g0 + n].rearrange("(kc p) n -> p kc n", p=128))
                    for m0 in range(0, n, 128):
                        mm_ = min(128, n - m0)
                        ps = pjbank()
                        for k in range(KC):
                            b.mm(ps[0:mm_, :], wt[:, k, m0:m0 + mm_], uT[:, k, 0:512], start=(k == 0), stop=(k == KC - 1))
                        evac(ps, (g0 + m0) // 128, mm_)

            def formA(col0, ncols, evac, tl):
                for g0 in range(0, ncols, 256):
                    n = min(256, ncols - g0)
                    wt = load_w(w_in[:, col0 + g0:col0 + g0 + n].rearrange("(kc p) n -> p kc n", p=128))
                    for (i, R) in tl:
                        ps = pjbank()
                        for k in range(KC):
                            b.mm(ps[0:R, 0:n], uT[:, k, i * 128:i * 128 + R], wt[:, k, 0:n], start=(k == 0), stop=(k == KC - 1))
                        evac(ps, i, R, g0, n)

            ptiles = tiles[:TPB]

            b.P.tag = 'b%d.inproj_m' % blk
            formB(OQ, 1024, lambda ps, c, m_: b.act(qT[:, c, :], ps[:, :], AF.Copy, scale=0.0625))
            formB(OK_, 1024, lambda ps, c, m_: b.cpa(kT[:, c, :], ps[:, :]))
            formA(OV, 1024, lambda ps, i, R, g0, n: b.cpa(vtok[:, i, g0:g0 + n], ps[:, 0:n]), ptiles)
            gm = load_g(g_mlstm[0:1, :])

            def ev_o(ps, i, R, g0, n):
                b.act(tmpa[:, 0:n], ps[:, 0:n], AF.Sigmoid)
                b.tt(gsig[:, i, g0:g0 + n], tmpa[:, 0:n], gm[:, g0:g0 + n], ALU.mult)
            formA(OO, 1024, ev_o, ptiles)
            wt = load_w(w_in[:, OIG:OIG + 8].rearrange("(kc p) n -> p kc n", p=128))
            for gi in range(2):
                ps = pjbank()
                for k in range(KC):
                    b.mm(ps[0:4, :], wt[:, k, gi * 4:gi * 4 + 4], uT[:, k, 0:512], start=(k == 0), stop=(k == KC - 1))
                b.cp(rows[gi][0:4, :], ps[0:4, :])

            chk(1 + 10 * blk)
            b.P.tag = 'b%d.gates' % blk
            ti, tf, lnv, Bn, Aa, Gf = rows[0], rows[1], rows[2], rows[3], rows[4], rows[5]
            b.act(ti[0:4, :], rows[0][0:4, :], AF.Tanh, scale=1.0 / 15.0, bias=smallp[0:4, 0:1])
            b.act(tf[0:4, :], rows[1][0:4, :], AF.Tanh, scale=1.0 / 15.0, bias=smallp[0:4, 1:2])
            b.act(lnv[0:4, :], tf[0:4, :], AF.Exp, scale=-15.0)
            b.act(lnv[0:4, :], lnv[0:4, :], AF.Ln, bias=1.0)
            b.scan(Bn[0:4, :], ones32[0:4, :], lnv[0:4, :], carry[0:4, 0:1], ALU.mult, ALU.add)
            b.stt(Aa[0:4, :], ti[0:4, :], 15.0, Bn[0:4, :], ALU.mult, ALU.add)
            Gx = rows[1]
            b.scan(Gx[0:4, :], ones32[0:4, :], Aa[0:4, :], carry[0:4, 1:2], ALU.mult, ALU.max)
            GP = rows[5]
            for (i, R) in ptiles:
                cs = slice(i * 128, (i + 1) * 128)
                gc = carry[0:4, 1:2] if i == 0 else Gx[0:4, i * 128 - 1:i * 128]
                b.ts(GP[0:4, cs], Aa[0:4, cs], gc, None, op0=ALU.subtract)
                b.ts(GP[32:36, cs], Gx[0:4, cs], -1.0, gc, op0=ALU.mult, op1=ALU.add)
                b.ts(GP[96:100, cs], Aa[0:4, cs], Gx[0:4, (i + 1) * 128 - 1:(i + 1) * 128], None, op0=ALU.subtract)
            b.tt(GP[64:68, :], Bn[0:4, :], Gx[0:4, :], ALU.subtract)
            for r0 in (0, 32, 64, 96):
                b.act(GP[r0:r0 + 4, :], GP[r0:r0 + 4, :], AF.Exp)
            if last:
                b.tt(sm[0:4, 8:9], Gx[0:4, 511:512], Bn[0:4, 511:512], ALU.subtract)
                b.dma_nc(m_p, sm[0:4, 8:9], q='act')
            b.cp(carry[0:4, 0:1], Bn[0:4, 511:512])
            b.cp(carry[0:4, 1:2], Gx[0:4, 511:512])

            b.P.tag = 'b%d.mlstm' % blk
            cub = [pb[2], pb[3], pb[6], pb[0]]
            for (i, R) in ptiles:
                cs = slice(i * 128, (i + 1) * 128)
                b.tr(pb[7][:, 0:128], GP[:, cs], ident32[:, :])
                b.cp(GT[:, :], pb[7][:, 0:128])
                b.mm(pb[7][:, 128:132], sel127[:, :], GT[:, 32:36])
                b.cp(decb[:, 0:4], pb[7][:, 128:132], eng='act')
                b.tt(eAm[:, :, :], mask32[:, :].unsqueeze(1).broadcast_to([128, 4, 128]), GT[:, 0:4].unsqueeze(2).broadcast_to([128, 4, 128]), ALU.mult)
                ptk = pbf(0)
                for h in range(4):
                    for c in range(2):
                        b.tr(ptk[:, h * 256 + c * 128:h * 256 + (c + 1) * 128], kT[:, h * 2 + c, cs], identb[:, :])
                for h in range(4):
                    for c in range(2):
                        b.mm(pb[1][:, h * 128:(h + 1) * 128], kT[:, h * 2 + c, cs], qT[:, h * 2 + c, cs], start=(c == 0), stop=(c == 1))
                b.tt(kq4[:, :, :], ptk[:, 0:1024].rearrange("p (h d) -> p h d", h=4), GT[:, 96:100].unsqueeze(2).broadcast_to([128, 4, 256]), ALU.mult)
                b.tt(spb[:, :, :], pb[1][:, :].rearrange("p (h t) -> p h t", h=4), eAm[:, :, :], ALU.mult)
                for h in range(4):
                    brp = pb[4 + h // 2][:, (h % 2) * 256:(h % 2 + 1) * 256]
                    b.mm(brp, spb[:, h, :], vtok[:, i, h * 256:(h + 1) * 256], start=True, stop=False)
                    b.mm(brp, qT[:, h * 2, cs], Cbf[:, h, 0, :], start=False, stop=False)
                    b.mm(brp, qT[:, h * 2 + 1, cs], Cbf[:, h, 1, :], start=False, stop=True)
                for h in range(4):
                    dn = pb[7][:, 136 + h:137 + h]
                    b.mm(dn, spb[:, h, :], onesb[:, :], start=True, stop=False)
                    b.mm(dn, qT[:, h * 2, cs], nbf[:, h * 2:h * 2 + 1], start=False, stop=False)
                    b.mm(dn, qT[:, h * 2 + 1, cs], nbf[:, h * 2 + 1:h * 2 + 2], start=False, stop=True)
                for h in range(4):
                    for c in range(2):
                        b.mm(cub[h][:, c * 256:(c + 1) * 256], kq4[:, h, c * 128:(c + 1) * 128], vtok[:, i, h * 256:(h + 1) * 256])
                for h in range(4):
                    for c in range(2):
                        b.mm(pb[7][:, 144 + h * 2 + c:145 + h * 2 + c], kq4[:, h, c * 128:(c + 1) * 128], onesb[:, :])
                for h in range(4):
                    brp = pb[4 + h // 2][:, (h % 2) * 256:(h % 2 + 1) * 256]
                    b.act(tmpa[:, h * 256:(h + 1) * 256], brp, AF.Square, accum=sm[:, 16 + h:17 + h])
                b.cp(sm[:, 12:16], pb[7][:, 136:140])
                for h in range(4):
                    b.stt(Cst[:, h, :, :].rearrange("p c v -> p (c v)"), Cst[:, h, :, :].rearrange("p c v -> p (c v)"), decb[:, h:h + 1], cub[h][:, :], ALU.mult, ALU.add)
                    b.cp(Cbf[:, h, :, :], Cst[:, h, :, :], eng='act')
                b.tt(nst[:, :].rearrange("p (h c) -> p h c", h=4), nst[:, :].rearrange("p (h c) -> p h c", h=4), decb[:, 0:4].unsqueeze(2).broadcast_to([128, 4, 2]), ALU.mult)
                b.tt(nst[:, :], nst[:, :], pb[7][:, 144:152], ALU.add)
                b.cp(nbf[:, :], nst[:, :])
                den = sm[:, 12:16]
                iw = GT[:, 32:36]
                emt = GT[:, 64:68]
                b.act(sm[:, 20:24], den, AF.Abs)
                b.tt(sm[:, 20:24], sm[:, 20:24], iw, ALU.mult)
                b.tt(sm[:, 20:24], sm[:, 20:24], emt, ALU.max)
                b.recip(sm[:, 24:28], sm[:, 20:24])
                b.tt(sm[:, 24:28], sm[:, 24:28], iw, ALU.mult)
                b.tt(sm[:, 28:32], sm[:, 16:20], sm[:, 24:28], ALU.mult)
                b.tt(sm[:, 28:32], sm[:, 28:32], sm[:, 24:28], ALU.mult)
                rstd_from_ss(sm[:, 32:36], sm[:, 28:32], 256, 128)
                b.tt(sm[:, 36:40], sm[:, 32:36], sm[:, 24:28], ALU.mult)
                for h in range(4):
                    brp = pb[4 + h // 2][:, (h % 2) * 256:(h % 2 + 1) * 256]
                    b.stt(hf[:, h * 256:(h + 1) * 256], brp, sm[:, 36 + h:37 + h], gsig[:, i, h * 256:(h + 1) * 256], ALU.mult, ALU.mult)
                tok_to_T(hf, 128, mixT, 0, 8, i * 128, bank=1)

            if last:
                for h in range(4):
                    b.dma(c_p[h].rearrange("(c p) v -> p c v", p=128), Cst[:, h, :, :], q='act')
                b.tr(pb[7][0:8, 160:288], nst[:, :], ident32[:, :])
                b.cp(tmpa[0:8, 0:128], pb[7][0:8, 160:288])
                b.dma(n_p, tmpa[0:8, 0:128], q='act')

            chk(2 + 10 * blk)
            b.P.tag = 'b%d.inproj_s' % blk
            formA(OZ, 1024, lambda ps, i, R, g0, n: b.act(zs[:, i, g0:g0 + n], ps[:, 0:n], AF.Silu), ptiles)
            def ev_xbc(ps, c, m_):
                xraw = tmpb if c % 2 == 0 else xin
                acc = tmpa[:, 0:512] if c % 2 == 0 else tmpa[:, 512:1024]
                b.cp(xraw[:, 0:3], convtail[:, c, :])
                b.cp(xraw[:, 3:515], ps[:, :], eng='act')
                b.cp(convtail[:, c, :], xraw[:, 512:515])
                b.ts(acc, xraw[:, 0:512], gpar[:, c:c + 1], gpar[:, 48 + c:49 + c], op0=ALU.mult, op1=ALU.add)
                for j in range(1, 4):
                    b.stt(acc, xraw[:, j:j + 512], gpar[:, j * 12 + c:j * 12 + c + 1], acc, ALU.mult, ALU.add)
                b.act(xbcT[:, c, :], acc, AF.Silu)
            formB(OXBC, 1536, ev_xbc)
            if last:
                for g3 in range(3):
                    for c4 in range(4):
                        c = g3 * 4 + c4
                        b.tr(pb[6][0:3, c4 * 128:(c4 + 1) * 128], convtail[:, c, :], ident32[:, :])
                    dstt = tmpa[0:3, g3 * 512:(g3 + 1) * 512] if g3 < 2 else tmpb[0:3, 0:512]
                    b.cp(dstt, pb[6][0:3, :])
                b.dma(conv_p[:, 0:1024], tmpa[0:3, 0:1024], q='act')
                b.dma(conv_p[:, 1024:1536], tmpb[0:3, 0:512], q='act')
            wt = load_w(w_in[:, ODT:ODT + 16].rearrange("(kc p) n -> p kc n", p=128))
            ps = pjbank()
            for k in range(KC):
                b.mm(ps[0:16, :], wt[:, k, 0:16], uT[:, k, 0:512], start=(k == 0), stop=(k == KC - 1))
            dtv, av, cumn = rows[0], rows[1], rows[2]
            SPk = rows[3]
            b.act(dtv[0:16, :], ps[0:16, :], AF.Exp, bias=smallp[0:16, 2:3])
            b.act(dtv[0:16, :], dtv[0:16, :], AF.Ln, bias=1.0)
            b.ts(av[0:16, :], dtv[0:16, :], smallp[0:16, 4:5], None, op0=ALU.mult)

            chk(3 + 10 * blk)
            b.P.tag = 'b%d.ssd' % blk
            gs_t = load_g(g_ssm[0:1, :])
            xwq = [xw, xw2[:, :]]
            btq = [btok, btok2[:, :, :]]

            def ssd_front(i):
                q_ = i % 2
                cs = slice(i * 128, (i + 1) * 128)
                GTq = GT2[:, q_, :]
                ops = []
                A = ops.append
                ptx = pbf(2)
                for c in range(8):
                    A(lambda c=c: b.tr(ptx[:, c * 128:(c + 1) * 128], xbcT[:, c, cs], identb[:, :]))
                ptb = pbf(3)
                for g in range(2):
                    A(lambda g=g: b.tr(ptb[:, 512 + g * 128:512 + (g + 1) * 128], xbcT[:, 8 + g, cs], identb[:, :]))
                for g in range(2):
                    A(lambda g=g: b.mm(pb[3][:, g * 128:(g + 1) * 128], xbcT[:, 8 + g, cs], xbcT[:, 10 + g, cs]))
                A(lambda: b.scan(cumn[0:16, cs], ones32[0:16, 0:128], av[0:16, cs], 0.0, ALU.mult, ALU.add))
                A(lambda: b.cp(SPk[0:16, cs], cumn[0:16, cs]))
                A(lambda: b.cp(SPk[32:48, cs], dtv[0:16, cs]))
                A(lambda: b.act(SPk[64:80, cs], cumn[0:16, cs], AF.Exp, scale=-1.0))
                wtmp = rows[4]
                A(lambda: b.ts(wtmp[0:16, cs], cumn[0:16, cs], cumn[0:16, (i + 1) * 128 - 1:(i + 1) * 128], None, op0=ALU.subtract))
                A(lambda: b.act(wtmp[0:16, cs], wtmp[0:16, cs], AF.Exp))
                A(lambda: b.tt(SPk[96:112, cs], wtmp[0:16, cs], dtv[0:16, cs], ALU.mult))
                A(lambda: b.cp(xtok, ptx[:, 0:1024], eng='act'))
                A(lambda: b.cp(tmpb[:, 0:256], pb[3][:, 0:256], eng='act'))
                A(lambda: b.cp(btq[q_], ptb[:, 512:768].rearrange("p (g n) -> p g n", g=2)))
                A(lambda: b.tr(pb[7][:, 0:128], SPk[:, cs], ident32[:, :]))
                A(lambda: b.cp(GTq, pb[7][:, 0:128]))
                A(lambda: b.mm(pb[7][:, 128:144], sel127[:, :], GTq[:, 64:80]))
                A(lambda: b.cp(decb2[:, q_, :], pb[7][:, 128:144], eng='act'))
                x3 = xtok.rearrange("p (h q) -> p h q", h=16)
                A(lambda: b.tt(xdt.rearrange("p (h q) -> p h q", h=16), x3, GTq[:, 32:48].unsqueeze(2).broadcast_to([128, 16, 64]), ALU.mult, eng='pool'))
                A(lambda: b.tt(xwq[q_].rearrange("p (h q) -> p h q", h=16), x3, GTq[:, 96:112].unsqueeze(2).broadcast_to([128, 16, 64]), ALU.mult, eng='pool'))
                A(lambda: b.tt(yap[q_][:, :].rearrange("p (h q) -> p h q", h=16), x3, dskb[:, :].unsqueeze(2).broadcast_to([128, 16, 64]), ALU.mult, eng='pool'))
                for j in range(4):
                    R4 = tmpb[0:16, 512:1024].rearrange("p (h n) -> p h n", h=4)
                    A(lambda j=j, R4=R4: b.tt(R4, cumn[0:16, cs].unsqueeze(1).broadcast_to([16, 4, 128]),
                                              ident32[0:16, 4 * j:4 * j + 4].unsqueeze(2).broadcast_to([16, 4, 128]), ALU.mult))
                    A(lambda: b.mm(pb[6][:, :], negones[:, :], tmpb[0:16, 512:1024], start=True, stop=False))
                    A(lambda: b.mm(pb[6][:, :], identb[:, :], negm4b[:, :, :], start=False, stop=True))
                    for hh in range(4):
                        A(lambda j=j, hh=hh: b.act(tmpa[:, hh * 128:(hh + 1) * 128], pb[6][:, hh * 128:(hh + 1) * 128], AF.Exp, bias=GTq[:, 4 * j + hh:4 * j + hh + 1]))
                    A(lambda j=j: b.tt(WT[:, 4 * j:4 * j + 4, :], tmpa[:, 0:512].rearrange("p (h n) -> p h n", h=4),
                                       tmpb[:, (j // 2) * 128:(j // 2 + 1) * 128].unsqueeze(1).broadcast_to([128, 4, 128]), ALU.mult))
                for h in range(16):
                    A(lambda h=h: b.mm(pb[4 + h // 8][:, (h % 8) * 64:(h % 8 + 1) * 64], WT[:, h, :], xdt[:, h * 64:(h + 1) * 64]))
                for g in range(2):
                    A(lambda g=g: b.tt(yap[q_][:, g * 512:(g + 1) * 512], yap[q_][:, g * 512:(g + 1) * 512], pb[4 + g][:, :], ALU.add))
                return ops

            def ssd_back(i):
                q_ = i % 2
                cs = slice(i * 128, (i + 1) * 128)
                GTq = GT2[:, q_, :]
                ya = yap[q_]
                ops = []
                A = ops.append
                for g in range(2):
                    A(lambda g=g: b.mm(pb[g][:, :], xbcT[:, 10 + g, cs], hTbf[:, 8 * g:8 * g + 8, :]))
                for g in range(2):
                    A(lambda g=g: b.tt(xin[:, g * 512:(g + 1) * 512].rearrange("p (h q) -> p h q", h=8), pb[g][:, :].rearrange("p (h q) -> p h q", h=8),
                                       GTq[:, 64 + 8 * g:72 + 8 * g].unsqueeze(2).broadcast_to([128, 8, 64]), ALU.mult))
                A(lambda: b.tt(ya[:, :], ya[:, :], xin[:, :], ALU.add))
                for g in range(2):
                    A(lambda g=g: b.mm(pb[g][:, :], btq[q_][:, g, :], xwq[q_][:, g * 512:(g + 1) * 512]))
                A(lambda: b.tt(ya[:, :], ya[:, :], zs[:, i, :], ALU.mult))
                for g in range(2):
                    hv = hT[:, 8 * g:8 * g + 8, :]
                    A(lambda g=g, hv=hv: b.tt(hv, hv, decb2[:, q_, 8 * g:8 * g + 8].unsqueeze(2).broadcast_to([128, 8, 64]), ALU.mult, eng='pool'))
                for g in range(2):
                    A(lambda g=g: b.act(xin[:, g * 512:(g + 1) * 512], ya[:, g * 512:(g + 1) * 512], AF.Square, accum=sm[:, 40 + g:41 + g]))
                for g in range(2):
                    hv = hT[:, 8 * g:8 * g + 8, :]
                    A(lambda g=g, hv=hv: b.tt(hv, hv, pb[g][:, :].rearrange("p (h q) -> p h q", h=8), ALU.add))
                A(lambda: b.cp(hTbf[:, :, :], hT[:, :, :], eng='act'))
                A(lambda: rstd_from_ss(sm[:, 42:44], sm[:, 40:42], 512, 128))
                for g in range(2):
                    A(lambda g=g: b.stt(hf[:, g * 512:(g + 1) * 512], ya[:, g * 512:(g + 1) * 512], sm[:, 42 + g:43 + g], gs_t[:, g * 512:(g + 1) * 512], ALU.mult, ALU.mult))
                A(lambda: tok_to_T(hf, 128, mixT, 8, 8, i * 128, bank=7))
                return ops

            def interleave(fl, bl):
                nf, nb = len(fl), len(bl)
                out = []
                fi = bi = 0
                while fi < nf or bi < nb:
                    if bi >= nb or (fi < nf and fi * nb <= bi * nf):
                        out.append(fl[fi]); fi += 1
                    else:
                        out.append(bl[bi]); bi += 1
                return out

            for op in ssd_front(0):
                op()
            for (i, R) in ptiles:
                fl = ssd_front(i + 1) if i + 1 < TPB else []
                for op in interleave(fl, ssd_back(i)):
                    op()

            if last:
                for j in range(8):
                    b.tr(pb[j % 2][:, 0:128], hT[:, 2 * j:2 * j + 2, :].rearrange("p a q -> p (a q)"), ident32[:, :])
                    b.cp(tmpb[:, (j % 4) * 128:(j % 4 + 1) * 128], pb[j % 2][:, 0:128])
                    b.dma(ssm_p[j * 128:(j + 1) * 128, :], tmpb[:, (j % 4) * 128:(j % 4 + 1) * 128], q='act')

            chk(4 + 10 * blk)
            b.P.tag = 'b%d.sample' % blk
            if last:
                if True:
                    sample_mixers()

            chk(5 + 10 * blk)
            b.P.tag = 'b%d.outproj' % blk
            g_t = load_g(g_mix_post[0:1, :])
            stream_proj(mixT, 16, w_out, tiles, lambda grp, banks: post_staged(
                grp, banks, g_t, lambda i, R: (xp[t0 + i * 128:t0 + (i + 1) * 128, :] if R == 128 else xs[:, :])))

            chk(6 + 10 * blk)
            for layer in range(2):
                if layer == 1:
                    pool_mixer(blk, tiles, last)
                    chk(8 + 10 * blk)
                ffn(layer, tiles, last)
                chk(7 + 2 * layer + 10 * blk)

            b.P.tag = 'b%d.out' % blk
            for (i, R) in tiles:
                if R == 128:
                    b.dma(y_p[t0 + i * 128:t0 + (i + 1) * 128, :], xres[:, i, :], q='act')
                else:
                    b.dma(y_s[:, :], xres[0:R, i, :], q='act')


    try:
        main_loop()
    except _Stop:
        pass
    b.P.emit()
    return nc, b.P


_CACHE = {}


def kernel(x_prompt, x_sample, state_mlstm_c, state_mlstm_n, state_mlstm_m, state_ssm, state_conv, state_pool,
           g_mix_pre, g_mix_post, g_ffn_pre, g_ffn_post, w_in_ab, b_igate, b_fgate, g_mlstm,
           conv_w, conv_b, dt_bias, a_log, d_skip, g_ssm, w_out_ab, w_pool, pool_scale,
           w_gate, w_up, w_down):
    f = lambda a: np.ascontiguousarray(np.asarray(a, dtype=np.float32))
    if 'nc' not in _CACHE:
        _CACHE['nc'] = build()[0]
    nc = _CACHE['nc']
    NCORE = 8
    shared = {
        "g_mix_pre": f(g_mix_pre), "g_mix_post": f(g_mix_post), "g_ffn_pre": f(g_ffn_pre), "g_ffn_post": f(g_ffn_post),
        "w_in_ab": f(w_in_ab[0]), "b_igate": f(b_igate[0]).reshape(4, 1), "b_fgate": f(b_fgate[0]).reshape(4, 1),
        "g_mlstm": f(g_mlstm[0]).reshape(1, D), "conv_w": f(conv_w[0]), "conv_b": f(conv_b[0]).reshape(1, 1536),
        "dt_bias": f(dt_bias[0]).reshape(16, 1), "a_log": f(a_log[0]).reshape(16, 1), "d_skip": f(d_skip[0]).reshape(1, 16),
        "g_ssm": f(g_ssm[0]).reshape(1, D), "w_out_ab": f(w_out_ab[0]), "w_pool": f(w_pool[0]),
        "pool_scale": f(pool_scale[0]).reshape(1, D), "w_gate": f(w_gate), "w_up": f(w_up), "w_down": f(w_down),
    }
    in_maps = []
    for c in range(NCORE):
        r = slice(c * SB, (c + 1) * SB)
        m = dict(shared)
        m["x_prompt"] = f(x_prompt[c])
        m["x_sample"] = f(x_sample[r, 0, :])
        m["state_mlstm_c"] = f(state_mlstm_c[0, r])
        m["state_mlstm_n"] = f(state_mlstm_n[0, r]).reshape(SB, 1024)
        m["state_mlstm_m"] = f(state_mlstm_m[0, r])
        m["state_ssm"] = f(state_ssm[0, r])
        m["state_conv"] = f(state_conv[0, r])
        m["state_pool"] = f(state_pool[0, r])
        in_maps.append(m)
    res = run_bass_kernel_spmd(nc, in_maps, core_ids=list(range(NCORE)))
    R = res.results
    cat = lambda k: np.concatenate([np.asarray(r_[k]) for r_ in R], axis=0)
    stk = lambda k: np.stack([np.asarray(r_[k]) for r_ in R], axis=0)
    y_prompt = stk("y_prompt")
    y_sample = cat("y_sample").reshape(128, 1, D)
    c_prompt = stk("c_prompt")[None]
    c_sample = cat("c_sample")[None]
    n_prompt = stk("n_prompt").reshape(1, 8, 4, 256)
    n_sample = cat("n_sample").reshape(1, 128, 4, 256)
    m_prompt = stk("m_prompt").reshape(1, 8, 4)
    m_sample = cat("m_sample").reshape(1, 128, 4)
    ssm_prompt = stk("ssm_prompt").reshape(1, 8, 16, 64, 128)
    ssm_sample = cat("ssm_sample").reshape(1, 128, 16, 64, 128)
    conv_prompt = stk("conv_prompt")[None]
    conv_sample = cat("conv_sample")[None]
    pool_prompt = stk("pool_prompt")[None]
    pool_sample = cat("pool_sample")[None]
    outs = (y_prompt, y_sample, c_prompt, c_sample, n_prompt, n_sample, m_prompt, m_sample,
            ssm_prompt, ssm_sample, conv_prompt, conv_sample, pool_prompt, pool_sample)
    return tuple(np.ascontiguousarray(o, dtype=np.float32) for o in outs)
```
